# Optimizing a Trainium2 kernel written in Bass

```python
import jax, jax.numpy as jnp
from jax import lax
import numpy as np

D_MODEL = 2048
BATCH = 2
SEQ = 16384
DEPTH = 2

CTX_LEN = 256
GRID_W = 64
EPS = 1e-6
N_MOD = 9
D_FF = 5632
HEAD_DIM = 128
ATTN_HEADS = 8
ATTN_KV_HEADS = 2
Q_BLOCK = 128
ROPE_THETA = 10000.0
GLA_HEADS = 4
GLA_DK = 64
GLA_DV = 128
GLA_GATE_RANK = 16
GLA_TAU = 16.0
CHUNK = 128
GMLP_GROUPS = 4
GMLP_GROUP_DIM = 128

ATTN_Q_W = ATTN_HEADS * HEAD_DIM
ATTN_KV_W = ATTN_KV_HEADS * HEAD_DIM
GLA_K_W = GLA_HEADS * GLA_DK
GLA_V_W = GLA_HEADS * GLA_DV
GMLP_W = GMLP_GROUPS * GMLP_GROUP_DIM
MIX_W = ATTN_Q_W + GLA_V_W + GMLP_W
IN_SIZES = (ATTN_Q_W, ATTN_KV_W, ATTN_KV_W, GLA_K_W, GLA_K_W, GLA_V_W, GLA_V_W, 2 * GLA_GATE_RANK, GMLP_W, GMLP_W)
IN_W = 4128

kernel_name = "hybrid_prefix_dit_attn_gla_gmlp"


def _rmsnorm(x, g):
    xf = x.astype(jnp.float32)
    y = xf * lax.rsqrt(jnp.mean(xf * xf, axis=-1, keepdims=True) + EPS)
    return (y * g.astype(jnp.float32)).astype(x.dtype)


def _mod_norm(x, mod, i, g):
    shift = mod[:, 3 * i][:, None, :]
    scale = mod[:, 3 * i + 1][:, None, :]
    return _rmsnorm(x, g) * (1.0 + scale) + shift


def _swiglu(h, w_gu, w_down):
    gate, up = jnp.split(h @ w_gu, 2, axis=-1)
    return (jax.nn.silu(gate) * up) @ w_down


def _ffn_sublayer(x, mod, i, g, w_gu, w_down):
    h = _mod_norm(x, mod, i, g)
    return x + mod[:, 3 * i + 2][:, None, :] * (0.5 * _swiglu(h, w_gu, w_down))


def _heads(z, n):
    b, l, w = z.shape
    return z.reshape(b, l, n, w // n).transpose(0, 2, 1, 3)


def _unheads(z):
    b, n, l, d = z.shape
    return z.transpose(0, 2, 1, 3).reshape(b, l, n * d)


def _rope_tables(length):
    rows = length // GRID_W
    row = jnp.repeat(jnp.arange(rows, dtype=jnp.float32), GRID_W)
    col = jnp.broadcast_to(jnp.arange(GRID_W, dtype=jnp.float32), (rows, GRID_W)).reshape(-1)
    nf = HEAD_DIM // 4
    inv = ROPE_THETA ** (-jnp.arange(nf, dtype=jnp.float32) / nf)
    ang = jnp.concatenate([row[:, None] * inv, col[:, None] * inv], axis=-1)
    return jnp.cos(ang), jnp.sin(ang)


def _apply_rope_2d(x, cos, sin):
    nf = HEAD_DIM // 4
    xf = x.astype(jnp.float32)
    x_row, x_col = jnp.split(xf, 2, axis=-1)

    def rot(z, cs, sn):
        z1, z2 = jnp.split(z, 2, axis=-1)
        return jnp.concatenate([z1 * cs - z2 * sn, z1 * sn + z2 * cs], axis=-1)

    out = jnp.concatenate([rot(x_row, cos[:, :nf], sin[:, :nf]), rot(x_col, cos[:, nf:], sin[:, nf:])], axis=-1)
    return out.astype(x.dtype)


def _attend(q, k, v):
    b, hq, lq, dh = q.shape
    hkv = k.shape[1]
    grp = hq // hkv
    nb = lq // Q_BLOCK
    qb = q.reshape(b, hkv, grp, nb, Q_BLOCK, dh).transpose(3, 0, 1, 2, 4, 5)
    scale = dh ** -0.5

    def block(qblk):
        s = jnp.einsum('bkgqd,bksd->bkgqs', qblk, k, preferred_element_type=jnp.float32) * scale
        p = jax.nn.softmax(s, axis=-1)
        return jnp.einsum('bkgqs,bksd->bkgqd', p.astype(v.dtype), v)

    o = lax.map(block, qb)
    return o.transpose(1, 2, 3, 0, 4, 5).reshape(b, hq, lq, dh)


def _gla_scan(q, k, v, g, s0):
    b, h, l, dk = q.shape
    dv = v.shape[-1]
    n = l // CHUNK

    def to_chunks(z):
        return z.astype(jnp.float32).reshape(b, h, n, CHUNK, z.shape[-1]).transpose(2, 0, 1, 3, 4)

    mask = jnp.tril(jnp.ones((CHUNK, CHUNK), dtype=bool))[:, :, None]

    def step(s, inp):
        qc, kc, vc, gc = inp
        bcum = jnp.cumsum(gc, axis=2)
        diff = bcum[:, :, :, None, :] - bcum[:, :, None, :, :]
        decay = jnp.exp(jnp.where(mask, diff, -jnp.inf))
        a = jnp.einsum('bhid,bhjd,bhijd->bhij', qc, kc, decay)
        o = jnp.einsum('bhid,bhde->bhie', qc * jnp.exp(bcum), s) + jnp.einsum('bhij,bhje->bhie', a, vc)
        b_last = bcum[:, :, -1:, :]
        s_new = jnp.exp(b_last[:, :, 0, :])[..., None] * s + jnp.einsum('bhjd,bhje->bhde', kc * jnp.exp(b_last - bcum), vc)
        return s_new, o

    s_fin, o = lax.scan(step, s0, (to_chunks(q), to_chunks(k), to_chunks(v), to_chunks(g)))
    o = o.transpose(1, 2, 0, 3, 4).reshape(b, h, l, dv)
    return o.astype(v.dtype), s_fin


def _gla_bidir(q, k, v, g_fwd, g_bwd, s0_fwd, s0_bwd):
    o_f, s_f = _gla_scan(q, k, v, g_fwd, s0_fwd)
    flip = lambda z: jnp.flip(z, axis=2)
    o_b, s_b = _gla_scan(flip(q), flip(k), flip(v), flip(g_bwd), s0_bwd)
    return o_f + flip(o_b), s_f, s_b


def _stream_features(h, w_in, qk_g, gate_w, gate_b, rope):
    offs = np.cumsum(IN_SIZES)[:-1].tolist()
    aq, ak, av, gq, gk, gv, gr, glr, mu, mv = jnp.split(h @ w_in, offs, axis=-1)
    q = _rmsnorm(_heads(aq, ATTN_HEADS), qk_g[0])
    k = _rmsnorm(_heads(ak, ATTN_KV_HEADS), qk_g[1])
    if rope is not None:
        q = _apply_rope_2d(q, rope[0], rope[1])
        k = _apply_rope_2d(k, rope[0], rope[1])
    lr_f, lr_b = jnp.split(glr, 2, axis=-1)

    def log_decay(lr, i):
        logits = (lr @ gate_w[i] + gate_b[i]).astype(jnp.float32)
        return _heads(jax.nn.log_sigmoid(logits) / GLA_TAU, GLA_HEADS)

    return {
        "q": q, "k": k, "v": _heads(av, ATTN_KV_HEADS),
        "gq": _heads(gq, GLA_HEADS) * (GLA_DK ** -0.5), "gk": _heads(gk, GLA_HEADS), "gv": _heads(gv, GLA_HEADS),
        "gr": gr, "gf": log_decay(lr_f, 0), "gb": log_decay(lr_b, 1),
        "mu": mu, "mv": mv,
    }


def _gla_out(o, r, g):
    ot = o.transpose(0, 2, 1, 3)
    on = _rmsnorm(ot, g.reshape(GLA_HEADS, GLA_DV))
    b, l = on.shape[:2]
    return on.reshape(b, l, GLA_V_W) * jax.nn.silu(r)


def _chunk_gmlp(u, v, w_s, b_s, g):
    b, l, _ = u.shape
    n = l // CHUNK
    vn = _rmsnorm(v, g).reshape(b, n, CHUNK, GMLP_GROUPS, GMLP_GROUP_DIM)
    z = jnp.einsum('gij,bnjgc->bnigc', w_s, vn) + b_s.T[:, :, None]
    return u * z.reshape(b, l, GMLP_W)


def _token_mixing(h_lat, h_ctx, w_in, w_out, qk_g, gate_w, gate_b, gla_g, w_s, b_s, gm_g, rope, want_ctx):
    fl = _stream_features(h_lat, w_in, qk_g, gate_w, gate_b, rope)
    fc = _stream_features(h_ctx, w_in, qk_g, gate_w, gate_b, None)
    att_l = _attend(fl["q"], jnp.concatenate([fc["k"], fl["k"]], axis=2), jnp.concatenate([fc["v"], fl["v"]], axis=2))
    bsz = h_ctx.shape[0]
    s0 = jnp.zeros((bsz, GLA_HEADS, GLA_DK, GLA_DV), jnp.float32)
    o_c, s_f, s_b = _gla_bidir(fc["gq"], fc["gk"], fc["gv"], fc["gf"], fc["gb"], s0, s0)
    o_l, _, _ = _gla_bidir(fl["gq"], fl["gk"], fl["gv"], fl["gf"], fl["gb"], s_f, s_b)
    y_l = jnp.concatenate([_unheads(att_l), _gla_out(o_l, fl["gr"], gla_g),
                           _chunk_gmlp(fl["mu"], fl["mv"], w_s, b_s, gm_g)], axis=-1) @ w_out
    y_c = None
    if want_ctx:
        att_c = _attend(fc["q"], fc["k"], fc["v"])
        y_c = jnp.concatenate([_unheads(att_c), _gla_out(o_c, fc["gr"], gla_g),
                               _chunk_gmlp(fc["mu"], fc["mv"], w_s, b_s, gm_g)], axis=-1) @ w_out
    return y_l, y_c


def setup_inputs(seed: int = 0) -> dict:
    key = jax.random.key(seed)
    ks = jax.random.split(key, 24)
    D = D_MODEL

    def nrm(k, shape, scale):
        return jax.random.normal(k, shape, jnp.float32) * scale

    return {
        "x": nrm(ks[0], (BATCH, SEQ, D), 1.0),
        "c": nrm(ks[1], (BATCH, D), 1.0),
        "ctx": nrm(ks[2], (BATCH, CTX_LEN, D), 1.0),
        "c_ctx": nrm(ks[3], (D,), 1.0),
        "mod_w": nrm(ks[4], (DEPTH, D, N_MOD * D), 0.5 * D ** -0.5),
        "mod_b": nrm(ks[5], (DEPTH, N_MOD * D), 0.02),
        "norm_g": 1.0 + nrm(ks[6], (DEPTH, 3, D), 0.02),
        "ffn1_w_gu": nrm(ks[7], (DEPTH, D, 2 * D_FF), D ** -0.5),
        "ffn1_w_down": nrm(ks[8], (DEPTH, D_FF, D), D_FF ** -0.5),
        "ffn2_w_gu": nrm(ks[9], (DEPTH, D, 2 * D_FF), D ** -0.5),
        "ffn2_w_down": nrm(ks[10], (DEPTH, D_FF, D), D_FF ** -0.5),
        "w_in": nrm(ks[11], (DEPTH, D, IN_W), D ** -0.5),
        "w_out": nrm(ks[12], (DEPTH, MIX_W, D), MIX_W ** -0.5),
        "qk_norm_g": 1.0 + nrm(ks[13], (DEPTH, 2, HEAD_DIM), 0.02),
        "gla_gate_w": nrm(ks[14], (DEPTH, 2, GLA_GATE_RANK, GLA_K_W), GLA_GATE_RANK ** -0.5),
        "gla_gate_b": 1.0 + nrm(ks[15], (DEPTH, 2, GLA_K_W), 0.1),
        "gla_norm_g": 1.0 + nrm(ks[16], (DEPTH, GLA_V_W), 0.02),
        "gmlp_w_s": nrm(ks[17], (DEPTH, GMLP_GROUPS, CHUNK, CHUNK), 0.5 * CHUNK ** -0.5),
        "gmlp_b_s": 1.0 + nrm(ks[18], (DEPTH, GMLP_GROUPS, CHUNK), 0.02),
        "gmlp_norm_g": 1.0 + nrm(ks[19], (DEPTH, GMLP_W), 0.02),
        "final_norm_g": 1.0 + nrm(ks[20], (D,), 0.02),
    }


def reference(x, c, ctx, c_ctx, mod_w, mod_b, norm_g, ffn1_w_gu, ffn1_w_down, ffn2_w_gu, ffn2_w_down,
              w_in, w_out, qk_norm_g, gla_gate_w, gla_gate_b, gla_norm_g, gmlp_w_s, gmlp_b_s, gmlp_norm_g,
              final_norm_g):
    length = x.shape[1]
    rope = _rope_tables(length)
    sc = jax.nn.silu(c)
    scc = jax.nn.silu(c_ctx)
    xl, xc = x, ctx
    for l in range(DEPTH):
        last = l == DEPTH - 1
        mod_l = (sc @ mod_w[l] + mod_b[l]).reshape(-1, N_MOD, D_MODEL)
        mod_c = (scc @ mod_w[l] + mod_b[l]).reshape(1, N_MOD, D_MODEL)
        xl = _ffn_sublayer(xl, mod_l, 0, norm_g[l, 0], ffn1_w_gu[l], ffn1_w_down[l])
        xc = _ffn_sublayer(xc, mod_c, 0, norm_g[l, 0], ffn1_w_gu[l], ffn1_w_down[l])
        hl = _mod_norm(xl, mod_l, 1, norm_g[l, 1])
        hc = _mod_norm(xc, mod_c, 1, norm_g[l, 1])
        yl, yc = _token_mixing(hl, hc, w_in[l], w_out[l], qk_norm_g[l], gla_gate_w[l], gla_gate_b[l],
                               gla_norm_g[l], gmlp_w_s[l], gmlp_b_s[l], gmlp_norm_g[l], rope, not last)
        xl = xl + mod_l[:, 5][:, None, :] * yl
        xl = _ffn_sublayer(xl, mod_l, 2, norm_g[l, 2], ffn2_w_gu[l], ffn2_w_down[l])
        if not last:
            xc = xc + mod_c[:, 5][:, None, :] * yc
            xc = _ffn_sublayer(xc, mod_c, 2, norm_g[l, 2], ffn2_w_gu[l], ffn2_w_down[l])
    return _rmsnorm(xl, final_norm_g)
```

```python
import numpy as np
import ml_dtypes
from contextlib import ExitStack
import concourse.bass as bass
import concourse.mybir as mybir
from concourse.bass_utils import run_bass_kernel_spmd

F32 = mybir.dt.float32
BF16 = mybir.dt.bfloat16
AF = mybir.ActivationFunctionType
ALU = mybir.AluOpType
NPBF = ml_dtypes.bfloat16

D = 2048
KC = 16
DFF = 5632
NJ = 44
CTX = 256
EPS = 1e-6
INW = 4128
NFM = 23
NTM = 3
DBG_SKIP = set()


class Prog:
    def __init__(self, nc, es, nch=10):
        self.nc = nc
        self.es = es
        self.eng = {'pe': nc.tensor, 'act': nc.scalar, 'dve': nc.vector, 'pool': nc.gpsimd, 'sp': nc.sync}
        self.semh = {}
        for e in self.eng:
            self.semh['e_' + e] = es.enter_context(nc.semaphore('e_' + e))
        self.cnt = {e: 0 for e in self.eng}
        self.known = {e: {} for e in self.eng}
        self.lastw = {}
        self.readers = {}
        self.chans = {}
        for q in ('sp', 'pool'):
            lst = []
            for i in range(nch):
                key = 'd_%s%d' % (q, i)
                self.semh[key] = es.enter_context(nc.semaphore(key))
                lst.append([key, 0])
            self.chans[q] = [lst, 0]
        self.uid = 0

    def _wait(self, eng, sk, val):
        if self.known[eng].get(sk, 0) >= val:
            return
        self.known[eng][sk] = val
        self.eng[eng].wait_ge(self.semh[sk], val)

    def _deps(self, eng, reads, writes, extra=()):
        need = {}

        def add(ev):
            sk, val, e = ev
            if e == eng and eng == 'pe':
                return
            if need.get(sk, 0) < val:
                need[sk] = val
        for r in reads:
            w = self.lastw.get(r)
            if w is not None:
                add(w)
        for w_ in writes:
            w = self.lastw.get(w_)
            if w is not None:
                add(w)
            rd = self.readers.get(w_)
            if rd:
                for sk, (val, e) in rd.items():
                    add((sk, val, e))
        for ev in extra:
            if ev is not None:
                add(ev)
        for sk, val in need.items():
            self._wait(eng, sk, val)

    def _record(self, ev, reads, writes):
        sk, val, e = ev
        for r in reads:
            self.readers.setdefault(r, {})[sk] = (val, e)
        for w in writes:
            self.lastw[w] = ev
            self.readers[w] = {}

    def op(self, eng, fns, reads=(), writes=()):
        if callable(fns):
            fns = [fns]
        self._deps(eng, reads, writes)
        ins = None
        for f in fns:
            ins = f()
        self.cnt[eng] += 1
        ins.then_inc(self.semh['e_' + eng], 1)
        ev = ('e_' + eng, self.cnt[eng], eng)
        self._record(ev, reads, writes)
        return ev

    def dma(self, q, out, in_, reads=(), writes=(), **kw):
        lst, idx = self.chans[q]
        ch = lst[idx % len(lst)]
        self.chans[q][1] = idx + 1
        prev = (ch[0], ch[1], 'dma') if ch[1] else None
        self._deps(q, reads, writes, extra=(prev,))
        ch[1] += 16
        self.eng[q].dma_start(out=out, in_=in_, **kw).then_inc(self.semh[ch[0]], 16)
        ev = (ch[0], ch[1], 'dma')
        self._record(ev, reads, writes)
        return ev

    def barrier(self, full=False):
        targets = [('e_' + e, self.cnt[e]) for e in self.eng if self.cnt[e] > 0]
        for q in self.chans:
            if q == 'pool' and not full:
                continue
            for ch in self.chans[q][0]:
                if ch[1] > 0:
                    targets.append((ch[0], ch[1]))
        for e in self.eng:
            for sk, val in targets:
                if sk == 'e_pe' and e == 'pe':
                    continue
                self._wait(e, sk, val)
        if full:
            self.lastw.clear()
        else:
            self.lastw = {k: ev for k, ev in self.lastw.items() if ev[0].startswith('d_pool')}
        self.readers.clear()

    def key(self, name):
        self.uid += 1
        return (name, self.uid)


class Scope:
    def __init__(self, P):
        self.P = P
        self.es = ExitStack()

    def __enter__(self):
        self.es.__enter__()
        return self

    def tile(self, name, shape, dt):
        self.P.uid += 1
        return self.es.enter_context(self.P.nc.sbuf_tensor('%s_%d' % (name, self.P.uid), list(shape), dt))

    def __exit__(self, *a):
        self.P.barrier()
        return self.es.__exit__(*a)


def _fm_cols():
    groups = []
    for h in range(8):
        groups.append(np.arange(h * 128, (h + 1) * 128))
    for g in range(2):
        groups.append(1024 + np.arange(g * 128, (g + 1) * 128))
    for g in range(2):
        groups.append(1536 + np.arange(g * 128, (g + 1) * 128))
    for g in range(2):
        groups.append(1792 + np.arange(g * 128, (g + 1) * 128))
    for g in range(4):
        groups.append(2560 + np.arange(g * 128, (g + 1) * 128))
    glr = np.full(128, -1)
    glr[:32] = 3072 + np.arange(32)
    groups.append(glr)
    for g in range(4):
        groups.append(3104 + np.arange(g * 128, (g + 1) * 128))
    return groups


def _tm_cols():
    return [np.concatenate([1280 + np.arange(256), 1792 + np.arange(256)]),
            2048 + np.arange(512),
            3616 + np.arange(512)]


def lay_wgu(w):
    g = w[:, :DFF].reshape(KC, 128, NJ, 128)
    u = w[:, DFF:].reshape(KC, 128, NJ, 128)
    s = np.stack([g, u], axis=3)
    return np.ascontiguousarray(s.transpose(2, 1, 0, 3, 4)).reshape(NJ, 128, KC * 256)


def lay_wd(w):
    s = w.reshape(NJ, 128, KC, 128)
    return np.ascontiguousarray(s.transpose(2, 1, 0, 3)).reshape(KC, 128, NJ * 128)


def lay_cols(w, groups):
    outs = []
    for cols in groups:
        sel = np.where(cols >= 0, cols, 0)
        blk = w[:, sel]
        if (cols < 0).any():
            blk = blk.copy()
            blk[:, cols < 0] = 0.0
        blk = blk.reshape(KC, 128, len(cols)).transpose(1, 0, 2)
        outs.append(np.ascontiguousarray(blk).reshape(128, KC * len(cols)))
    return np.stack(outs, 0)


def lay_wout(w):
    s = w.reshape(KC, 128, KC, 128)
    return np.ascontiguousarray(s.transpose(2, 1, 0, 3)).reshape(KC, 128, KC * 128)


def make_consts(nlat, seg):
    c = {}
    c['ones'] = np.ones((128, 128), np.float32)
    j = np.arange(128)[:, None]
    i = np.arange(128)[None, :]
    tri = np.stack([(j <= i), (j >= i), (j > i), (j < i)], 0).astype(np.float32)
    c['tri'] = np.ascontiguousarray(tri.transpose(1, 0, 2)).reshape(128, 4 * 128)
    pm = np.zeros((128, 128), np.float32)
    for m in range(128):
        blk = (m // 32) % 2
        k = m + 32 if blk == 0 else m - 32
        pm[k, m] = 1.0
    c['pm'] = pm
    t = seg * nlat + np.arange(nlat)
    row = (t // 64).astype(np.float32)
    col = (t % 64).astype(np.float32)
    nf = 32
    inv = (10000.0 ** (-np.arange(nf, dtype=np.float32) / nf)).astype(np.float32)
    ar = row[None, :] * inv[:, None]
    ac = col[None, :] * inv[:, None]
    cos = np.concatenate([np.cos(ar), np.cos(ar), np.cos(ac), np.cos(ac)], 0).astype(np.float32)
    sin = np.concatenate([-np.sin(ar), np.sin(ar), -np.sin(ac), np.sin(ac)], 0).astype(np.float32)
    c['cos'] = np.ascontiguousarray(cos)
    c['sin'] = np.ascontiguousarray(sin)
    return c


class Builder:
    def __init__(self, nlat, nkeys, do_B, do_A, final, ctx_B=True):
        self.nlat = nlat
        self.NT = CTX + nlat
        self.NCH = self.NT // 128
        self.nkeys = nkeys
        self.do_B, self.do_A, self.final, self.ctx_B = do_B, do_A, final, ctx_B
        self.nc = bass.Bass("TRN2", target_bir_lowering=False)
        self.inputs = []
        self.outputs = []
        self.in_specs = {}
        self.fused = False
        self.sfx = ''

    def din(self, name, shape, dt=F32):
        self.inputs.append(name)
        self.in_specs[name] = (list(shape), dt)
        return self.nc.dram_tensor(name, list(shape), dt, kind="ExternalInput").ap()

    def dout(self, name, shape, dt=F32):
        if self.fused and name != 'yT':
            return self.dscr(name, shape, dt)
        self.outputs.append(name)
        return self.nc.dram_tensor(name, list(shape), dt, kind="ExternalOutput").ap()

    def dscr(self, name, shape, dt=BF16):
        return self.nc.dram_tensor(name, list(shape), dt, kind="Internal").ap()

    def supertiles(self, include_ctx=True):
        st = []
        if include_ctx:
            st.append((0, CTX, True))
        for i in range(self.nlat // 512):
            st.append((CTX + i * 512, 512, False))
        return st

    def prologue(self):
        nc, P, G = self.nc, self.P, self.G
        ones_f = G.tile('ones_f', [128, 128], F32)
        self.ones_bf = G.tile('ones_bf', [128, 128], BF16)
        self.tri = G.tile('tri', [128, 4, 128], F32)
        self.pm = G.tile('pm', [128, 128], F32)
        c_ones = self.din('c_ones', [128, 128])
        c_tri = self.din('c_tri', [128, 512])
        c_pm = self.din('c_pm', [128, 128])
        P.dma('sp', ones_f[:], c_ones[:, :], writes=['ones_f'])
        P.dma('sp', self.tri[:].rearrange('p a b -> p (a b)'), c_tri[:, :], writes=['tri'])
        P.dma('sp', self.pm[:], c_pm[:, :], writes=['pm'])
        P.op('dve', lambda: nc.vector.tensor_copy(out=self.ones_bf[:], in_=ones_f[:]), reads=['ones_f'], writes=['ones_bf'])
        self.ones_f = ones_f
        self.eps_t = G.tile('eps_t', [128, 1], F32)
        self.one_t = G.tile('one_t', [128, 1], F32)
        P.op('dve', lambda: nc.vector.memset(self.eps_t[:], EPS), writes=['eps_t'])
        P.op('dve', lambda: nc.vector.memset(self.one_t[:], 1.0), writes=['one_t'])
        self.x_t = G.tile('x_t', [128, KC, 512], F32)

    def stage_setup(self):
        P, G = self.P, self.G
        if self.do_B:
            self.setup_layer('b')
            self.setup_B()
        if self.do_A:
            self.setup_layer('a')
            self.setup_A()
        if self.final:
            self.fin_g = G.tile('fin_g', [128, KC], F32)
            fg = self.din('final_g', [128, KC])
            P.dma('sp', self.fin_g[:], fg[:, :], writes=['fin_g'])
            self.y_out = self.dout('yT', [D, self.nlat])

    def run_stage(self):
        P = self.P
        if self.do_B:
            self.attention()
            self.gla_output()
        for (t0, T, is_ctx) in self.supertiles():
            if is_ctx and self.do_B and not self.do_A and not self.ctx_B:
                continue
            r = 1 if is_ctx else 0
            self.load_x(t0, T)
            if self.do_B and (self.ctx_B or not is_ctx):
                self.outproj(t0, T, r)
                self.ffn('b', 2, T, r)
            if self.do_A:
                self.ffn('a', 0, T, r)
                self.inproj(t0, T, r, is_ctx)
            if self.final:
                if not is_ctx:
                    self.final_norm(t0, T)
            else:
                self.store_x(t0, T)
            P.barrier()
        if self.do_A:
            self.gla_scan()

    def build(self):
        nc = self.nc
        NT = self.NT
        with ExitStack() as es:
            P = Prog(nc, es)
            self.P = P
            self.ps = [es.enter_context(nc.psum_tensor('ps%d' % i, [128, 512], F32)) for i in range(8)]
            G = Scope(P)
            G.__enter__()
            self.G = G
            self.prologue()
            self.xT_in = self.din('xT_in', [D, NT])
            if not self.final:
                self.xT_out = self.dout('xT_out', [D, NT])
            self.stage_setup()
            P.barrier()
            self.run_stage()
            P.barrier(full=True)
            G.__exit__(None, None, None)
        return nc

    def cast_weights(self, name, shape3, chunk_reads=None):
        src = self.din(name + self.sfx, shape3)
        dst = self.dscr(name + self.sfx + '_bf', shape3)
        for g in range(shape3[0]):
            self.P.dma('pool', dst[g], src[g], writes=[(name + self.sfx, g)], max_dma_last_dim=8192)
        return dst

    def setup_layer(self, tag):
        nc, P, G = self.nc, self.P, self.G
        L = {}
        ng = G.tile('ng' + tag, [128, 3, KC], F32)
        d_ng = self.din('normg_' + tag + self.sfx, [128, 3 * KC])
        if self.fused:
            lyr = self.lb if tag == 'b' else self.la
            modT = self.modall[:].rearrange('p l (a b) c -> p l a b c', a=9)[:, lyr]
        else:
            modT = G.tile('modT' + tag, [128, 9, KC, 2], F32)
            d_mod = self.din('modT_' + tag, [128, 9 * KC * 2])
            P.dma('sp', modT[:].rearrange('p a b c -> p (a b c)'), d_mod[:, :], writes=['modT' + tag])
        P.dma('sp', ng[:].rearrange('p a b -> p (a b)'), d_ng[:, :], writes=['ng' + tag])
        L['gs'] = G.tile('gs' + tag, [128, 3, 2, KC], F32)
        L['sh'] = G.tile('sh' + tag, [128, 3, 2, KC], F32)
        L['hg'] = G.tile('hg' + tag, [128, 3, 2, KC], F32)
        for i3 in range(3):
            for r in range(2):
                P.op('dve', lambda i3=i3, r=r: nc.vector.scalar_tensor_tensor(
                    out=L['gs'][:, i3, r, :], in0=modT[:, 3 * i3 + 1, :, r], scalar=1.0, in1=ng[:, i3, :],
                    op0=ALU.add, op1=ALU.mult), reads=['modT' + tag, 'ng' + tag], writes=[('gs', tag, i3, r)])
                P.op('dve', lambda i3=i3, r=r: nc.vector.tensor_copy(out=L['sh'][:, i3, r, :], in_=modT[:, 3 * i3, :, r]),
                     reads=['modT' + tag], writes=[('sh', tag, i3, r)])
                P.op('dve', lambda i3=i3, r=r: nc.vector.tensor_scalar(
                    out=L['hg'][:, i3, r, :], in0=modT[:, 3 * i3 + 2, :, r], scalar1=(1.0 if i3 == 1 else 0.5), scalar2=None,
                    op0=ALU.mult), reads=['modT' + tag], writes=[('hg', tag, i3, r)])
        setattr(self, 'L' + tag, L)

    def setup_A(self):
        nc, P, G = self.nc, self.P, self.G
        NT = self.NT
        A = {'sfx': self.sfx}
        A['wgu'] = self.cast_weights('wgu_a', [NJ, 128, KC * 256])
        A['wd'] = self.cast_weights('wd_a', [KC, 128, NJ * 128])
        A['wfm'] = self.cast_weights('wfm_a', [NFM, 128, KC * 128])
        A['wtm'] = self.cast_weights('wtm_a', [NTM, 128, KC * 512])
        A['qkg'] = G.tile('qkg', [128, 2], F32)
        A['gatew'] = G.tile('gatew', [16, 2, 256], F32)
        A['gateb'] = G.tile('gateb', [1, 2, 256], F32)
        A['wsT_f'] = G.tile('wsT_f', [128, 4, 128], F32)
        A['wsT'] = G.tile('wsT', [128, 4, 128], BF16)
        A['bsb'] = G.tile('bsb', [128, 4, 128], F32)
        A['gmg'] = G.tile('gmg', [128, 4], F32)
        A['cosd'] = self.din('p_cos' + self.sfx, [128, self.nlat])
        A['sind'] = self.din('p_sin' + self.sfx, [128, self.nlat])
        for nm, shp in [('qkg', [128, 2]), ('gatew', [16, 512]), ('gateb', [1, 512]), ('wsT_f', [128, 512]),
                        ('bsb', [128, 512]), ('gmg', [128, 4])]:
            d = self.din('p_' + nm + self.sfx, shp)
            t = A[nm]
            ap = t[:] if len(t.shape) == 2 else t[:].rearrange('p a b -> p (a b)')
            P.dma('sp', ap, d[:, :], writes=[nm])
        P.op('dve', lambda: nc.vector.tensor_copy(out=A['wsT'][:], in_=A['wsT_f'][:]), reads=['wsT_f'], writes=['wsT'])
        A['qT'] = self.dout('o_qT' + self.sfx, [8, 128, NT], BF16)
        A['kT'] = self.dout('o_kT' + self.sfx, [2, 128, NT], BF16)
        A['v'] = self.dout('o_v' + self.sfx, [NT, 256], BF16)
        A['gq'] = self.dout('o_gq' + self.sfx, [2, 2, 128, NT], BF16)
        A['gk'] = self.dout('o_gk' + self.sfx, [2, 2, 128, NT], BF16)
        A['gv'] = self.dout('o_gv' + self.sfx, [NT, 512], BF16)
        A['kh'] = self.dscr('s_kh' + self.sfx, [2, NT, 256], BF16)
        A['S0'] = self.dout('o_S0' + self.sfx, [2, self.NCH, 128, 256], F32)
        A['grs'] = self.dout('o_grs' + self.sfx, [4, 128, NT], BF16)
        A['gm'] = self.dout('o_gm' + self.sfx, [4, 128, NT], BF16)
        A['glasum'] = self.dout('o_glasum' + self.sfx, [128, 2 * 2 * 2 * 129], F32)
        A['dcum'] = self.dout('o_dcum' + self.sfx, [128, 2 * self.NCH * 2], F32)
        A['eb'] = G.tile('eb', [128, 2, self.NCH, 2], F32)
        self.A = A

    def setup_B(self):
        nc, P, G = self.nc, self.P, self.G
        NT = self.NT
        B = {'sfx': self.sfx}
        B['wout'] = self.cast_weights('wout_b', [KC, 128, KC * 128])
        B['wgu'] = self.cast_weights('wgu_b', [NJ, 128, KC * 256])
        B['wd'] = self.cast_weights('wd_b', [KC, 128, NJ * 128])
        if self.fused:
            pa = self.prevA
            for k_ in ('qT', 'kT', 'v', 'gq', 'gk', 'gv', 'S0', 'grs', 'gm', 'dcum', 'glasum'):
                B[k_] = pa[k_]
            B['glag'] = G.tile('glag', [128, 4], F32)
            d = self.din('p_glag' + self.sfx, [128, 4])
            P.dma('sp', B['glag'][:], d[:, :], writes=['glag'])
            B['mix'] = self.dscr('s_mix' + self.sfx, [12, 128, NT], BF16)
            self.B = B
            return
        B['qT'] = self.din('i_qT', [8, 128, NT], BF16)
        B['kT'] = self.din('i_kT', [2, 128, self.nkeys], BF16)
        B['v'] = self.din('i_v', [self.nkeys, 256], BF16)
        B['gq'] = self.din('i_gq', [2, 2, 128, NT], BF16)
        B['gk'] = self.din('i_gk', [2, 2, 128, NT], BF16)
        B['gv'] = self.din('i_gv', [NT, 512], BF16)
        B['S0'] = self.din('i_S0', [2, self.NCH, 128, 256], F32)
        B['grs'] = self.din('i_grs', [4, 128, NT], BF16)
        B['gm'] = self.din('i_gm', [4, 128, NT], BF16)
        B['dcum'] = self.din('i_dcum', [128, 2 * self.NCH * 2], F32)
        B['ctxS'] = self.din('i_ctxS', [128, 2 * 2 * 128], F32)
        B['predD'] = self.din('i_predD', [128, 2 * 3 * 2], F32)
        B['predB'] = self.din('i_predB', [128, 2 * 3 * 2 * 128], F32)
        B['glag'] = G.tile('glag', [128, 4], F32)
        d = self.din('p_glag', [128, 4])
        P.dma('sp', B['glag'][:], d[:, :], writes=['glag'])
        B['mix'] = self.dscr('s_mix', [12, 128, NT], BF16)
        self.B = B

    def load_x(self, t0, T):
        P = self.P
        src = self.xT_in.rearrange('(kc p) t -> p kc t', p=128)
        for h in range(2):
            P.dma('sp', self.x_t[:, h * 8:(h + 1) * 8, :T], src[:, h * 8:(h + 1) * 8, t0:t0 + T],
                  writes=[('x', m) for m in range(h * 8, h * 8 + 8)])

    def store_x(self, t0, T):
        P = self.P
        dst = self.xT_out.rearrange('(kc p) t -> p kc t', p=128)
        for h in range(2):
            P.dma('sp', dst[:, h * 8:(h + 1) * 8, t0:t0 + T], self.x_t[:, h * 8:(h + 1) * 8, :T],
                  reads=[('x', m) for m in range(h * 8, h * 8 + 8)])

    def modnorm(self, S, hT, T, gs, sh):
        nc, P, ps = self.nc, self.P, self.ps
        x_t = self.x_t
        sq = [S.tile('sq', [128, 512], BF16) for _ in range(2)]
        tmp = [S.tile('mtmp', [128, 512], F32) for _ in range(2)]
        rs = S.tile('rs', [128, 512], F32)
        rstd = S.tile('rstd', [128, 512], F32)
        for kc in range(KC):
            b = kc % 2
            P.op('act', lambda kc=kc, b=b: nc.scalar.activation(out=sq[b][:, :T], in_=x_t[:, kc, :T], func=AF.Square),
                 reads=[('x', kc)], writes=[('sq', b)])
            P.op('pe', lambda kc=kc, b=b: nc.tensor.matmul(ps[0][:, :T], lhsT=self.ones_bf[:], rhs=sq[b][:, :T],
                                                           start=(kc == 0), stop=(kc == KC - 1)),
                 reads=[('sq', b)], writes=[('ps', 0)])
        P.op('act', lambda: nc.scalar.activation(out=rs[:, :T], in_=ps[0][:, :T], func=AF.Sqrt, scale=1.0 / D, bias=self.eps_t[:, 0:1]),
             reads=[('ps', 0)], writes=['rs'])
        P.op('dve', lambda: nc.vector.reciprocal(out=rstd[:, :T], in_=rs[:, :T]), reads=['rs'], writes=['rstd'])
        for kc in range(KC):
            b = kc % 2
            P.op('dve', lambda kc=kc, b=b: nc.vector.scalar_tensor_tensor(
                out=tmp[b][:, :T], in0=x_t[:, kc, :T], scalar=gs[:, kc:kc + 1], in1=rstd[:, :T], op0=ALU.mult, op1=ALU.mult),
                reads=[('x', kc), 'rstd'], writes=[('mtmp', b)])
            P.op('act', lambda kc=kc, b=b: nc.scalar.activation(out=hT[:, kc, :T], in_=tmp[b][:, :T], func=AF.Identity,
                                                                bias=sh[:, kc:kc + 1], scale=1.0),
                 reads=[('mtmp', b)], writes=[('hT', kc)])

    def ffn(self, tag, i3, T, r):
        nc, P, ps = self.nc, self.P, self.ps
        L = getattr(self, 'L' + tag)
        W = self.A if tag == 'a' else self.B
        wgu, wd = W['wgu'], W['wd']
        wn_gu = 'wgu_' + tag + W['sfx']
        wn_d = 'wd_' + tag + W['sfx']
        x_t = self.x_t
        with Scope(P) as S:
            hT = S.tile('hT', [128, KC, 512], BF16)
            act = S.tile('act', [128, NJ, 512], BF16)
            gu = [S.tile('gu', [128, KC, 256], BF16) for _ in range(3)]
            wdt = [S.tile('wdt', [128, NJ, 128], BF16) for _ in range(2)]
            sgt = [S.tile('sgt', [128, 512], F32) for _ in range(2)]

            def load_gu(j):
                P.dma('sp', gu[j % 3][:].rearrange('p a b -> p (a b)'), wgu[j], reads=[(wn_gu, j)], writes=[('gu', j % 3)])

            def load_wd(m):
                P.dma('sp', wdt[m % 2][:].rearrange('p a b -> p (a b)'), wd[m], reads=[(wn_d, m)], writes=[('wdt', m % 2)])
            load_gu(0)
            load_gu(1)
            self.modnorm(S, hT, T, L['gs'][:, i3, r, :], L['sh'][:, i3, r, :])
            hkeys = [('hT', kc) for kc in range(KC)]
            for j in range(NJ):
                if j + 2 < NJ:
                    load_gu(j + 2)
                if j == NJ - 4:
                    load_wd(0)
                if j == NJ - 2:
                    load_wd(1)
                w = gu[j % 3]
                pg = ps[1 + 2 * (j % 2)]
                pu = ps[2 + 2 * (j % 2)]
                P.op('pe', [lambda kc=kc, w=w, pg=pg: nc.tensor.matmul(pg[:, :T], lhsT=w[:, kc, 0:128], rhs=hT[:, kc, :T],
                                                                       start=(kc == 0), stop=(kc == KC - 1)) for kc in range(KC)],
                     reads=hkeys + [('gu', j % 3)], writes=[('ps', 1 + 2 * (j % 2))])
                P.op('pe', [lambda kc=kc, w=w, pu=pu: nc.tensor.matmul(pu[:, :T], lhsT=w[:, kc, 128:256], rhs=hT[:, kc, :T],
                                                                       start=(kc == 0), stop=(kc == KC - 1)) for kc in range(KC)],
                     reads=hkeys + [('gu', j % 3)], writes=[('ps', 2 + 2 * (j % 2))])
                P.op('act', lambda j=j, pg=pg: nc.scalar.activation(out=sgt[j % 2][:, :T], in_=pg[:, :T], func=AF.Silu),
                     reads=[('ps', 1 + 2 * (j % 2))], writes=[('sgt', j % 2)])
                P.op('dve', lambda j=j, pu=pu: nc.vector.tensor_tensor(out=act[:, j, :T], in0=sgt[j % 2][:, :T], in1=pu[:, :T], op=ALU.mult),
                     reads=[('sgt', j % 2), ('ps', 2 + 2 * (j % 2))], writes=[('act', j)])
            akeys = [('act', j) for j in range(NJ)]
            hg = L['hg']
            for m in range(KC):
                w = wdt[m % 2]
                po = ps[5 + (m % 2)]
                P.op('pe', [lambda j=j, w=w, po=po: nc.tensor.matmul(po[:, :T], lhsT=w[:, j, :], rhs=act[:, j, :T],
                                                                     start=(j == 0), stop=(j == NJ - 1)) for j in range(NJ)],
                     reads=akeys + [('wdt', m % 2)], writes=[('ps', 5 + (m % 2))])
                P.op('dve', lambda m=m, po=po: nc.vector.scalar_tensor_tensor(
                    out=x_t[:, m, :T], in0=po[:, :T], scalar=hg[:, i3, r, m:m + 1], in1=x_t[:, m, :T], op0=ALU.mult, op1=ALU.add),
                    reads=[('ps', 5 + (m % 2)), ('x', m)], writes=[('x', m)])
                if m + 2 < KC:
                    load_wd(m + 2)

    def outproj(self, t0, T, r):
        nc, P, ps = self.nc, self.P, self.ps
        B, L = self.B, self.Lb
        x_t = self.x_t
        with Scope(P) as S:
            mix = S.tile('mixt', [128, KC, 512], BF16)
            wt = [S.tile('wo', [128, KC, 128], BF16) for _ in range(3)]
            P.dma('sp', mix[:, 0:12, :T], B['mix'][:, :, t0:t0 + T].rearrange('g p t -> p g t'), reads=['mix_scr'], writes=['mixt'])
            P.dma('sp', mix[:, 12:16, :T], B['gm'][:, :, t0:t0 + T].rearrange('g p t -> p g t'), writes=['mixt2'])

            def load_w(m):
                P.dma('sp', wt[m % 3][:].rearrange('p a b -> p (a b)'), B['wout'][m], reads=[('wout_b' + B['sfx'], m)], writes=[('wo', m % 3)])
            load_w(0)
            load_w(1)
            for m in range(KC):
                if m + 2 < KC:
                    load_w(m + 2)
                w = wt[m % 3]
                po = ps[5 + (m % 2)]
                P.op('pe', [lambda kc=kc, w=w, po=po: nc.tensor.matmul(po[:, :T], lhsT=w[:, kc, :], rhs=mix[:, kc, :T],
                                                                       start=(kc == 0), stop=(kc == KC - 1)) for kc in range(KC)],
                     reads=['mixt', 'mixt2', ('wo', m % 3)], writes=[('ps', 5 + (m % 2))])
                P.op('dve', lambda m=m, po=po: nc.vector.scalar_tensor_tensor(
                    out=x_t[:, m, :T], in0=po[:, :T], scalar=L['hg'][:, 1, r, m:m + 1], in1=x_t[:, m, :T], op0=ALU.mult, op1=ALU.add),
                    reads=[('ps', 5 + (m % 2)), ('x', m)], writes=[('x', m)])

    def final_norm(self, t0, T):
        nc, P, ps = self.nc, self.P, self.ps
        x_t = self.x_t
        with Scope(P) as S:
            sq = [S.tile('sq', [128, 512], BF16) for _ in range(2)]
            rs = S.tile('rs', [128, 512], F32)
            rstd = S.tile('rstd', [128, 512], F32)
            for kc in range(KC):
                b = kc % 2
                P.op('act', lambda kc=kc, b=b: nc.scalar.activation(out=sq[b][:, :T], in_=x_t[:, kc, :T], func=AF.Square),
                     reads=[('x', kc)], writes=[('sq', b)])
                P.op('pe', lambda kc=kc, b=b: nc.tensor.matmul(ps[0][:, :T], lhsT=self.ones_bf[:], rhs=sq[b][:, :T],
                                                               start=(kc == 0), stop=(kc == KC - 1)),
                     reads=[('sq', b)], writes=[('ps', 0)])
            P.op('act', lambda: nc.scalar.activation(out=rs[:, :T], in_=ps[0][:, :T], func=AF.Sqrt, scale=1.0 / D, bias=self.eps_t[:, 0:1]),
                 reads=[('ps', 0)], writes=['rs'])
            P.op('dve', lambda: nc.vector.reciprocal(out=rstd[:, :T], in_=rs[:, :T]), reads=['rs'], writes=['rstd'])
            for kc in range(KC):
                P.op('dve', lambda kc=kc: nc.vector.scalar_tensor_tensor(
                    out=x_t[:, kc, :T], in0=x_t[:, kc, :T], scalar=self.fin_g[:, kc:kc + 1], in1=rstd[:, :T], op0=ALU.mult, op1=ALU.mult),
                    reads=[('x', kc), 'rstd', 'fin_g'], writes=[('x', kc)])
            dst = self.y_out.rearrange('(kc p) t -> p kc t', p=128)
            tl = t0 - CTX
            for h in range(2):
                P.dma('sp', dst[:, h * 8:(h + 1) * 8, tl:tl + T], x_t[:, h * 8:(h + 1) * 8, :T],
                      reads=[('x', m) for m in range(h * 8, h * 8 + 8)])

    def inproj(self, t0, T, r, is_ctx):
        nc, P, ps = self.nc, self.P, self.ps
        A, L = self.A, self.La
        nchk = T // 128
        tl = t0 - CTX
        SC = 128.0 ** -0.5
        with Scope(P) as S:
            hT = S.tile('hT', [128, KC, 512], BF16)
            wf = [S.tile('wf', [128, KC, 128], BF16) for _ in range(3)]
            wt = [S.tile('wt', [128, KC, 512], BF16) for _ in range(2)]
            zsq = S.tile('zsq', [128, 512], BF16)
            hrs = S.tile('hrs', [128, 512], F32)
            hrstd = S.tile('hrstd', [128, 512], F32)
            qn = S.tile('qn', [128, 512], F32)
            t1 = S.tile('t1', [128, 512], F32)
            t2 = S.tile('t2', [128, 512], F32)
            qf = [S.tile('qf', [128, 512], BF16) for _ in range(2)]
            gqT = [S.tile('gqT', [128, 512], F32) for _ in range(2)]
            gkT = [S.tile('gkT', [128, 512], F32) for _ in range(2)]
            lr = [S.tile('lr', [16, 512], F32) for _ in range(2)]
            uT = [S.tile('uT', [128, 512], F32) for _ in range(4)]
            grt = [S.tile('grt', [128, 512], BF16) for _ in range(2)]
            if not is_ctx:
                cs = S.tile('cs', [128, 512], F32)
                sn = S.tile('sn', [128, 512], F32)
                P.dma('sp', cs[:, :T], A['cosd'][:, tl:tl + T], writes=['cs'])
                P.dma('sp', sn[:, :T], A['sind'][:, tl:tl + T], writes=['sn'])

            def load_f(g):
                P.dma('sp', wf[g % 3][:].rearrange('p a b -> p (a b)'), A['wfm'][g], reads=[('wfm_a' + A['sfx'], g)], writes=[('wf', g % 3)])

            def load_t(g):
                P.dma('sp', wt[g % 2][:].rearrange('p a b -> p (a b)'), A['wtm'][g], reads=[('wtm_a' + A['sfx'], g)], writes=[('wt', g % 2)])
            load_f(0)
            load_f(1)
            load_t(0)
            load_t(1)
            self.modnorm(S, hT, T, L['gs'][:, 1, r, :], L['sh'][:, 1, r, :])
            hkeys = [('hT', kc) for kc in range(KC)]
            for g in range(NFM):
                if g + 2 < NFM:
                    load_f(g + 2)
                w = wf[g % 3]
                pb = 1 + (g % 2)
                pz = ps[pb]
                if g == 18 and 'glr' in DBG_SKIP:
                    continue
                if g == 18:
                    for d in range(2):
                        pzd = ps[1 + d]
                        P.op('pe', [lambda kc=kc, w=w, pzd=pzd, d=d: nc.tensor.matmul(
                            pzd[0:16, :T], lhsT=w[:, kc, d * 16:(d + 1) * 16], rhs=hT[:, kc, :T],
                            start=(kc == 0), stop=(kc == KC - 1)) for kc in range(KC)],
                            reads=hkeys + [('wf', g % 3)], writes=[('ps', 1 + d)])
                        P.op('act', lambda d=d, pzd=pzd: nc.scalar.copy(out=lr[d][:, :T], in_=pzd[0:16, :T]),
                             reads=[('ps', 1 + d)], writes=[('lr', d)])
                    continue
                P.op('pe', [lambda kc=kc, w=w, pz=pz: nc.tensor.matmul(pz[:, :T], lhsT=w[:, kc, :], rhs=hT[:, kc, :T],
                                                                       start=(kc == 0), stop=(kc == KC - 1)) for kc in range(KC)],
                     reads=hkeys + [('wf', g % 3)], writes=[('ps', pb)])
                if 'fmpost' in DBG_SKIP:
                    continue
                if g < 10:
                    which = 0 if g < 8 else 1
                    P.op('act', lambda pz=pz: nc.scalar.activation(out=zsq[:, :T], in_=pz[:, :T], func=AF.Square),
                         reads=[('ps', pb)], writes=['zsq'])
                    P.op('pe', lambda: nc.tensor.matmul(ps[3][:, :T], lhsT=self.ones_bf[:], rhs=zsq[:, :T], start=True, stop=True),
                         reads=['zsq'], writes=[('ps', 3)])
                    P.op('act', lambda: nc.scalar.activation(out=hrs[:, :T], in_=ps[3][:, :T], func=AF.Sqrt, scale=1.0 / 128,
                                                             bias=self.eps_t[:, 0:1]), reads=[('ps', 3)], writes=['hrs'])
                    P.op('dve', lambda: nc.vector.reciprocal(out=hrstd[:, :T], in_=hrs[:, :T]), reads=['hrs'], writes=['hrstd'])
                    P.op('dve', lambda pz=pz, which=which: nc.vector.scalar_tensor_tensor(
                        out=qn[:, :T], in0=pz[:, :T], scalar=A['qkg'][:, which:which + 1], in1=hrstd[:, :T], op0=ALU.mult, op1=ALU.mult),
                        reads=[('ps', pb), 'hrstd'], writes=['qn'])
                    q_o = qf[g % 2]
                    if is_ctx:
                        P.op('pool', lambda q_o=q_o: nc.gpsimd.tensor_copy(out=q_o[:, :T], in_=qn[:, :T]), reads=['qn'], writes=[('qf', g % 2)])
                    else:
                        P.op('pe', lambda: nc.tensor.matmul(ps[4][:, :T], lhsT=self.pm[:], rhs=qn[:, :T], start=True, stop=True),
                             reads=['qn'], writes=[('ps', 4)])
                        P.op('pool', lambda: nc.gpsimd.tensor_tensor(out=t1[:, :T], in0=qn[:, :T], in1=cs[:, :T], op=ALU.mult),
                             reads=['qn', 'cs'], writes=['t1'])
                        P.op('dve', lambda: nc.vector.tensor_tensor(out=t2[:, :T], in0=ps[4][:, :T], in1=sn[:, :T], op=ALU.mult),
                             reads=[('ps', 4), 'sn'], writes=['t2'])
                        P.op('pool', lambda q_o=q_o: nc.gpsimd.tensor_tensor(out=q_o[:, :T], in0=t1[:, :T], in1=t2[:, :T], op=ALU.add),
                             reads=['t1', 't2'], writes=[('qf', g % 2)])
                    dst = A['qT'][g, :, t0:t0 + T] if g < 8 else A['kT'][g - 8, :, t0:t0 + T]
                    P.dma('sp', dst, q_o[:, :T], reads=[('qf', g % 2)])
                elif g < 12:
                    P.op('act', lambda pz=pz, g=g: nc.scalar.copy(out=gqT[g - 10][:, :T], in_=pz[:, :T]), reads=[('ps', pb)], writes=[('gqT', g - 10)])
                elif g < 14:
                    P.op('act', lambda pz=pz, g=g: nc.scalar.copy(out=gkT[g - 12][:, :T], in_=pz[:, :T]), reads=[('ps', pb)], writes=[('gkT', g - 12)])
                elif g < 18:
                    go = grt[g % 2]
                    P.op('act', lambda pz=pz, go=go: nc.scalar.activation(out=go[:, :T], in_=pz[:, :T], func=AF.Silu),
                         reads=[('ps', pb)], writes=[('grt', g % 2)])
                    P.dma('sp', A['grs'][g - 14, :, t0:t0 + T], go[:, :T], reads=[('grt', g % 2)])
                else:
                    P.op('act', lambda pz=pz, g=g: nc.scalar.copy(out=uT[g - 19][:, :T], in_=pz[:, :T]), reads=[('ps', pb)], writes=[('uT', g - 19)])
            vt = [S.tile('vt', [128, 256], BF16) for _ in range(2)]
            gktm = [S.tile('gktm', [128, 256], F32) for _ in range(4)]
            gvt = [S.tile('gvt', [128, 512], BF16) for _ in range(2)]
            vn = S.tile('vn', [128, 512], BF16)
            junk = S.tile('junk', [128, 512], BF16)
            ssq = S.tile('ssq', [128, 1], F32)
            srs = S.tile('srs', [128, 1], F32)
            srstd = S.tile('srstd', [128, 1], F32)
            tz = S.tile('tz', [128, 128], F32)
            gmo = S.tile('gmo', [128, 4, 512], BF16)
            ee = S.tile('ee', [128, 256], F32)
            sp = S.tile('sp', [128, 256], F32)
            E1 = S.tile('E1', [128, 2, 128], F32)
            E2 = S.tile('E2', [128, 2, 128], F32)
            E3 = S.tile('E3', [128, 256], F32)
            gqo = [S.tile('gqo', [128, 2, 512], BF16) for _ in range(2)]
            gko = [S.tile('gko', [128, 2, 512], BF16) for _ in range(2)]
            kht = [S.tile('kht', [128, 256], BF16) for _ in range(2)]
            for gt in range(NTM):
                if 'tm' in DBG_SKIP:
                    break
                if gt == 2 and 'gmlp' in DBG_SKIP:
                    break
                w = wt[gt % 2]
                for c in range(nchk):
                    cc = slice(c * 128, (c + 1) * 128)
                    pb = 5 + (c % 2)
                    pz = ps[pb]
                    P.op('pe', [lambda kc=kc, w=w, pz=pz, cc=cc: nc.tensor.matmul(pz[:, :], lhsT=hT[:, kc, cc], rhs=w[:, kc, :],
                                                                                  start=(kc == 0), stop=(kc == KC - 1)) for kc in range(KC)],
                         reads=hkeys + [('wt', gt % 2)], writes=[('ps', pb)])
                    tok = slice(t0 + c * 128, t0 + (c + 1) * 128)
                    if 'tmpost' in DBG_SKIP or ('tm%dpost' % gt) in DBG_SKIP:
                        continue
                    if gt == 0:
                        v_o = vt[c % 2]
                        if 'tm0a' not in DBG_SKIP:
                            P.op('act', lambda pz=pz, v_o=v_o: nc.scalar.copy(out=v_o[:], in_=pz[:, 0:256]), reads=[('ps', pb)], writes=[('vt', c % 2)])
                            P.dma('sp', A['v'][tok, :], v_o[:], reads=[('vt', c % 2)])
                        if 'tm0b' not in DBG_SKIP:
                            P.op('act', lambda pz=pz, c=c: nc.scalar.copy(out=gktm[c][:], in_=pz[:, 256:512]), reads=[('ps', pb)], writes=[('gktm', c)])
                    elif gt == 1:
                        g_o = gvt[c % 2]
                        P.op('act', lambda pz=pz, g_o=g_o: nc.scalar.copy(out=g_o[:], in_=pz[:, :]), reads=[('ps', pb)], writes=[('gvt', c % 2)])
                        P.dma('sp', A['gv'][tok, :], g_o[:], reads=[('gvt', c % 2)])
                    else:
                        P.op('act', lambda pz=pz: nc.scalar.activation(out=junk[:], in_=pz[:, :], func=AF.Square, accum_out=ssq[:]),
                             reads=[('ps', pb)], writes=['junk', 'ssq'])
                        P.op('act', lambda: nc.scalar.activation(out=srs[:], in_=ssq[:], func=AF.Sqrt, scale=1.0 / 512, bias=self.eps_t[:, 0:1]),
                             reads=['ssq'], writes=['srs'])
                        P.op('dve', lambda: nc.vector.reciprocal(out=srstd[:], in_=srs[:]), reads=['srs'], writes=['srstd'])
                        P.op('dve', lambda pz=pz: nc.vector.tensor_scalar(out=vn[:], in0=pz[:, :], scalar1=srstd[:, 0:1], scalar2=None, op0=ALU.mult),
                             reads=[('ps', pb), 'srstd'], writes=['vn'])
                        P.op('pe', [lambda gi=gi: nc.tensor.matmul(ps[7][:, gi * 128:(gi + 1) * 128], lhsT=vn[:, gi * 128:(gi + 1) * 128],
                                                                   rhs=A['wsT'][:, gi, :], start=True, stop=True) for gi in range(4)],
                             reads=['vn'], writes=[('ps', 7)])
                        for gi in range(4):
                            P.op('dve', lambda gi=gi: nc.vector.scalar_tensor_tensor(
                                out=tz[:], in0=ps[7][:, gi * 128:(gi + 1) * 128], scalar=A['gmg'][:, gi:gi + 1], in1=A['bsb'][:, gi, :],
                                op0=ALU.mult, op1=ALU.add), reads=[('ps', 7)], writes=['tz'])
                            P.op('pool', lambda gi=gi, cc=cc: nc.gpsimd.tensor_tensor(out=gmo[:, gi, cc], in0=tz[:], in1=uT[gi][:, cc], op=ALU.mult),
                                 reads=['tz', ('uT', gi)], writes=['gmo'])
                if gt + 2 < NTM:
                    load_t(gt + 2)
            if 'gmlp' not in DBG_SKIP and 'tm' not in DBG_SKIP:
                P.dma('sp', A['gm'][:, :, t0:t0 + T].rearrange('g p t -> p g t'), gmo[:, :, :T], reads=['gmo'])
            for c in range(nchk):
                if 'gla' in DBG_SKIP:
                    break
                cc = slice(c * 128, (c + 1) * 128)
                cg = (t0 // 128) + c
                tok = slice(t0 + c * 128, t0 + (c + 1) * 128)
                for d in range(2):
                    P.op('pe', [lambda d=d, cc=cc: nc.tensor.matmul(ps[1][:, 0:256], lhsT=lr[d][0:16, cc], rhs=A['gatew'][0:16, d, :], start=True, stop=False),
                                lambda d=d: nc.tensor.matmul(ps[1][:, 0:256], lhsT=self.ones_f[0:1, 0:128], rhs=A['gateb'][0:1, d, :], start=False, stop=True)],
                         reads=[('lr', d)], writes=[('ps', 1)])
                    P.op('act', lambda: nc.scalar.activation(out=ee[:], in_=ps[1][:, 0:256], func=AF.Exp, scale=-1.0), reads=[('ps', 1)], writes=['ee'])
                    P.op('act', lambda: nc.scalar.activation(out=sp[:], in_=ee[:], func=AF.Ln, bias=self.one_t[:, 0:1], scale=1.0), reads=['ee'], writes=['sp'])
                    P.op('pe', [lambda half=half, d=d: nc.tensor.matmul(ps[2][:, half * 128:(half + 1) * 128], lhsT=sp[:, half * 128:(half + 1) * 128],
                                                                        rhs=self.tri[:, d, :], start=True, stop=True) for half in range(2)],
                         reads=['sp'], writes=[('ps', 2)])
                    P.op('pe', lambda d=d: nc.tensor.matmul(ps[3][:, 0:256], lhsT=self.tri[:, 2 + d, :], rhs=sp[:], start=True, stop=True),
                         reads=['sp'], writes=[('ps', 3)])
                    P.op('act', lambda: nc.scalar.activation(out=E1[:].rearrange('p a b -> p (a b)'), in_=ps[2][:, 0:256], func=AF.Exp, scale=-1.0 / 16),
                         reads=[('ps', 2)], writes=['E1'])
                    P.op('act', lambda: nc.scalar.activation(out=E2[:].rearrange('p a b -> p (a b)'), in_=ps[2][:, 0:256], func=AF.Exp, scale=1.0 / 16),
                         reads=[('ps', 2)], writes=['E2'])
                    P.op('act', lambda: nc.scalar.activation(out=E3[:], in_=ps[3][:, 0:256], func=AF.Exp, scale=-1.0 / 16),
                         reads=[('ps', 3)], writes=['E3'])
                    for half in range(2):
                        P.op('dve', lambda half=half, d=d, cc=cc: nc.vector.scalar_tensor_tensor(
                            out=gqo[d][:, half, cc], in0=gqT[half][:, cc], scalar=0.125, in1=E1[:, half, :], op0=ALU.mult, op1=ALU.mult),
                            reads=[('gqT', half), 'E1'], writes=[('gqo', d)])
                        P.op('pool', lambda half=half, d=d, cc=cc: nc.gpsimd.tensor_tensor(
                            out=gko[d][:, half, cc], in0=gkT[half][:, cc], in1=E2[:, half, :], op=ALU.mult),
                            reads=[('gkT', half), 'E2'], writes=[('gko', d)])
                    k_o = kht[d]
                    P.op('dve', lambda c=c, k_o=k_o: nc.vector.tensor_tensor(out=k_o[:], in0=gktm[c][:], in1=E3[:], op=ALU.mult),
                         reads=[('gktm', c), 'E3'], writes=[('kht', d)])
                    P.dma('sp', A['kh'][d, tok, :], k_o[:], reads=[('kht', d)])
                    col = 127 if d == 0 else 0
                    P.op('dve', lambda d=d, cg=cg, col=col: nc.vector.tensor_copy(out=A['eb'][:, d, cg, :], in_=E1[:, :, col]),
                         reads=['E1'], writes=['eb'])
            for d in range(2):
                if 'gla' in DBG_SKIP:
                    break
                P.dma('sp', A['gq'][d, :, :, t0:t0 + T].rearrange('h p t -> p h t'), gqo[d][:, :, :T], reads=[('gqo', d)])
                P.dma('sp', A['gk'][d, :, :, t0:t0 + T].rearrange('h p t -> p h t'), gko[d][:, :, :T], reads=[('gko', d)])

    def gla_scan(self):
        nc, P, ps = self.nc, self.P, self.ps
        A = self.A
        NCH = self.NCH
        with Scope(P) as S:
            Sst = [S.tile('Sst', [128, 2, 128], F32) for _ in range(2)]
            Drun = [S.tile('Drun', [128, 2], F32) for _ in range(2)]
            dcum = S.tile('dcum', [128, 2, NCH, 2], F32)
            gsum = S.tile('gsum', [128, 2, 2, 2, 129], F32)
            stage = [S.tile('stage', [128, 256], F32) for _ in range(4)]
            kht = [S.tile('skh', [128, 256], BF16) for _ in range(4)]
            gvt = [S.tile('sgv', [128, 512], BF16) for _ in range(4)]
            it = 0
            for kind in range(2):
                lo, hi = (0, 2) if kind == 0 else (2, NCH)
                orders = [list(range(lo, hi)), list(range(hi - 1, lo - 1, -1))]
                for d in range(2):
                    P.op('dve', lambda d=d: nc.vector.memset(Sst[d][:], 0.0), writes=[('Sst', d)])
                    P.op('dve', lambda d=d: nc.vector.memset(Drun[d][:], 1.0), writes=[('Drun', d)])
                for step in range(hi - lo):
                    for d in range(2):
                        c = orders[d][step]
                        tok = slice(c * 128, (c + 1) * 128)
                        b = it % 4
                        it += 1
                        P.dma('sp', kht[b][:], A['kh'][d, tok, :], writes=[('skh', b)])
                        P.dma('sp', gvt[b][:], A['gv'][tok, :], writes=[('sgv', b)])
                        P.op('act', lambda d=d, b=b: nc.scalar.copy(out=stage[b][:], in_=Sst[d][:].rearrange('p a b -> p (a b)')),
                             reads=[('Sst', d)], writes=[('stage', b)])
                        P.dma('sp', A['S0'][d, c], stage[b][:], reads=[('stage', b)])
                        P.op('dve', lambda d=d, c=c: nc.vector.tensor_copy(out=dcum[:, d, c, :], in_=Drun[d][:]), reads=[('Drun', d)], writes=['dcum'])
                        pb = 1 + d
                        P.op('pe', [lambda h=h, b=b, pb=pb: nc.tensor.matmul(
                            ps[pb][(h % 2) * 64:(h % 2) * 64 + 64, (h // 2) * 128:(h // 2) * 128 + 128],
                            lhsT=kht[b][:, h * 64:(h + 1) * 64], rhs=gvt[b][:, h * 128:(h + 1) * 128], start=True, stop=True) for h in range(4)],
                            reads=[('skh', b), ('sgv', b)], writes=[('ps', pb)])
                        for half in range(2):
                            P.op('dve', lambda d=d, c=c, half=half, pb=pb: nc.vector.scalar_tensor_tensor(
                                out=Sst[d][:, half, :], in0=Sst[d][:, half, :], scalar=A['eb'][:, d, c, half:half + 1],
                                in1=ps[pb][:, half * 128:(half + 1) * 128], op0=ALU.mult, op1=ALU.add),
                                reads=[('ps', pb), ('Sst', d)], writes=[('Sst', d)])
                        P.op('dve', lambda d=d, c=c: nc.vector.tensor_tensor(out=Drun[d][:], in0=Drun[d][:], in1=A['eb'][:, d, c, :], op=ALU.mult),
                             reads=[('Drun', d)], writes=[('Drun', d)])
                for d in range(2):
                    P.op('dve', lambda d=d, kind=kind: nc.vector.tensor_copy(out=gsum[:, kind, d, :, 0:128], in_=Sst[d][:]),
                         reads=[('Sst', d)], writes=['gsum'])
                    P.op('dve', lambda d=d, kind=kind: nc.vector.tensor_copy(out=gsum[:, kind, d, :, 128], in_=Drun[d][:]),
                         reads=[('Drun', d)], writes=['gsum'])
            P.dma('sp', A['glasum'][:, :], gsum[:].rearrange('p a b c e -> p (a b c e)'), reads=['gsum'])
            P.dma('sp', A['dcum'][:, :], dcum[:].rearrange('p a b c -> p (a b c)'), reads=['dcum'])

    def attention(self):
        nc, P, ps = self.nc, self.P, self.ps
        B = self.B
        nkeys = self.nkeys
        nkc = nkeys // 128
        SC = 128.0 ** -0.5
        with Scope(P) as S:
            KT = S.tile('KT', [128, 2, nkeys], BF16)
            V = S.tile('V', [128, nkc, 256], BF16)
            step = 2048
            for g in range(2):
                for a in range(0, nkeys, step):
                    b_ = min(nkeys, a + step)
                    P.dma('sp', KT[:, g, a:b_], B['kT'][g, :, a:b_], writes=['KT'])
            vsrc = B['v'].rearrange('(kc p) c -> p kc c', p=128)
            for a in range(0, nkc, 16):
                b_ = min(nkc, a + 16)
                P.dma('sp', V[:, a:b_, :], vsrc[:, a:b_, :], writes=['V'])
            q4 = [S.tile('q4', [128, 4, 128], BF16) for _ in range(2)]
            pT = [S.tile('pT', [128, 512], BF16) for _ in range(3)]
            rl = S.tile('rl', [128, 512], F32)
            oT = [S.tile('oT', [128, 512], BF16) for _ in range(2)]
            it = 0
            for qt in range(self.NCH):
                is_ctx = qt < 2
                if is_ctx and not self.ctx_B:
                    continue
                kcs = [0, 1] if is_ctx else list(range(nkc))
                tok = slice(qt * 128, (qt + 1) * 128)
                for g in range(2):
                    b = it % 2
                    it += 1
                    q = q4[b]
                    P.dma('sp', q[:], B['qT'][g * 4:(g + 1) * 4, :, tok].rearrange('h p t -> p h t'), writes=[('q4', b)])
                    qr = q[:].rearrange('p h t -> p (h t)')
                    po, pl = ps[2 + b], ps[4 + b]
                    n = len(kcs)

                    def emit_s(ki):
                        kc = kcs[ki]
                        P.op('pe', lambda kc=kc, ki=ki: nc.tensor.matmul(ps[ki % 2][:, :], lhsT=KT[:, g, kc * 128:(kc + 1) * 128], rhs=qr, start=True, stop=True),
                             reads=['KT', ('q4', b)], writes=[('ps', ki % 2)])
                    emit_s(0)
                    for ki in range(n):
                        kc = kcs[ki]
                        p_ = pT[ki % 3]
                        P.op('act', lambda ki=ki, p_=p_: nc.scalar.activation(out=p_[:], in_=ps[ki % 2][:, :], func=AF.Exp, scale=SC),
                             reads=[('ps', ki % 2)], writes=[('pT', ki % 3)])
                        if ki + 1 < n:
                            emit_s(ki + 1)
                        P.op('pe', [lambda kc=kc, ki=ki, p_=p_: nc.tensor.matmul(po[:, :], lhsT=V[:, kc, g * 128:(g + 1) * 128], rhs=p_[:], start=(ki == 0), stop=(ki == n - 1)),
                                    lambda ki=ki, p_=p_: nc.tensor.matmul(pl[:, :], lhsT=self.ones_bf[:], rhs=p_[:], start=(ki == 0), stop=(ki == n - 1))],
                             reads=['V', ('pT', ki % 3)], writes=[('ps', 2 + b), ('ps', 4 + b)])
                    P.op('dve', lambda pl=pl: nc.vector.reciprocal(out=rl[:], in_=pl[:, :]), reads=[('ps', 4 + b)], writes=['rl'])
                    o_ = oT[b]
                    P.op('dve', lambda po=po, o_=o_: nc.vector.tensor_tensor(out=o_[:], in0=po[:, :], in1=rl[:], op=ALU.mult),
                         reads=[('ps', 2 + b), 'rl'], writes=[('oT', b)])
                    P.dma('sp', B['mix'][g * 4:(g + 1) * 4, :, tok].rearrange('h p t -> p h t'), o_[:].rearrange('p (h t) -> p h t', h=4),
                          reads=[('oT', b)], writes=['mix_scr'])

    def gla_output(self):
        nc, P, ps = self.nc, self.P, self.ps
        B = self.B
        NCH = self.NCH
        with Scope(P) as S:
            Sst = S.tile('Sst', [128, 2, 2, 128], F32)
            pD = S.tile('pD', [128, 2, 3, 2], F32)
            pB = S.tile('pB', [128, 2, 3, 2, 128], F32)
            dcum = S.tile('dcum', [128, 2, NCH, 2], F32)
            if self.fused:
                gsv = B['glasum'].rearrange('p (k d h e) -> p k d h e', k=2, d=2, h=2)
                for d in range(2):
                    P.dma('sp', Sst[:, d, :, :], gsv[:, 0, d, :, 0:128], writes=['Sst'])
            else:
                P.dma('sp', Sst[:].rearrange('p a b c -> p (a b c)'), B['ctxS'][:, :], writes=['Sst'])
                P.dma('sp', pD[:].rearrange('p a b c -> p (a b c)'), B['predD'][:, :], writes=['pD'])
                P.dma('sp', pB[:].rearrange('p a b c e -> p (a b c e)'), B['predB'][:, :], writes=['pB'])
            P.dma('sp', dcum[:].rearrange('p a b c -> p (a b c)'), B['dcum'][:, :], writes=['dcum'])
            for d in range(2):
                if self.fused:
                    break
                for slot in range(3):
                    for half in range(2):
                        P.op('dve', lambda d=d, slot=slot, half=half: nc.vector.scalar_tensor_tensor(
                            out=Sst[:, d, half, :], in0=Sst[:, d, half, :], scalar=pD[:, d, slot, half:half + 1], in1=pB[:, d, slot, half, :],
                            op0=ALU.mult, op1=ALU.add), reads=['Sst', 'pD', 'pB'], writes=['Sst'])
            NB = 2
            gq = [S.tile('gq', [128, 2, 2, 128], BF16) for _ in range(NB)]
            gk = [S.tile('gk', [128, 2, 2, 128], BF16) for _ in range(NB)]
            gv = [S.tile('gv', [128, 512], BF16) for _ in range(NB)]
            S0 = [S.tile('S0', [128, 2, 256], F32) for _ in range(NB)]
            grs = [S.tile('grs', [128, 4, 128], BF16) for _ in range(NB)]
            Sc = [S.tile('Sc', [128, 2, 2, 128], BF16) for _ in range(NB)]
            Am = [S.tile('Am', [128, 128], BF16) for _ in range(4)]
            osq = S.tile('osq', [128, 128], BF16)
            ors = S.tile('ors', [128, 128], F32)
            orstd = S.tile('orstd', [128, 128], F32)
            on = S.tile('on', [128, 128], F32)
            og = [S.tile('og', [128, 4, 128], BF16) for _ in range(NB)]
            ai = 0
            for ci, c in enumerate(range(NCH)):
                is_ctx = c < 2
                if is_ctx and not self.ctx_B:
                    continue
                b = ci % NB
                tok = slice(c * 128, (c + 1) * 128)
                P.dma('sp', gq[b][:], B['gq'][:, :, :, tok].rearrange('d h p t -> p d h t'), writes=[('gq', b)])
                P.dma('sp', gk[b][:], B['gk'][:, :, :, tok].rearrange('d h p t -> p d h t'), writes=[('gk', b)])
                P.dma('sp', gv[b][:], B['gv'][tok, :], writes=[('gv', b)])
                P.dma('sp', S0[b][:], B['S0'][:, c].rearrange('d p f -> p d f'), writes=[('S0', b)])
                P.dma('sp', grs[b][:], B['grs'][:, :, tok].rearrange('g p t -> p g t'), writes=[('grs', b)])
                for d in range(2):
                    for half in range(2):
                        if is_ctx:
                            P.op('dve', lambda d=d, half=half, b=b: nc.vector.tensor_copy(out=Sc[b][:, d, half, :], in_=S0[b][:, d, half * 128:(half + 1) * 128]),
                                 reads=[('S0', b)], writes=[('Sc', b)])
                        else:
                            P.op('dve', lambda d=d, half=half, b=b, c=c: nc.vector.scalar_tensor_tensor(
                                out=Sc[b][:, d, half, :], in0=Sst[:, d, half, :], scalar=dcum[:, d, c, half:half + 1],
                                in1=S0[b][:, d, half * 128:(half + 1) * 128], op0=ALU.mult, op1=ALU.add),
                                reads=[('S0', b), 'Sst', 'dcum'], writes=[('Sc', b)])
                for h in range(4):
                    half = h // 2
                    hs = slice((h % 2) * 64, (h % 2) * 64 + 64)
                    ams = []
                    for d in range(2):
                        pa = ps[d]
                        P.op('pe', lambda d=d, b=b, half=half, hs=hs, pa=pa: nc.tensor.matmul(
                            pa[:, 0:128], lhsT=gk[b][hs, d, half, :], rhs=gq[b][hs, d, half, :], start=True, stop=True),
                            reads=[('gq', b), ('gk', b)], writes=[('ps', d)])
                        a_ = Am[ai % 4]
                        ams.append((a_, ai % 4))
                        P.op('dve', lambda d=d, pa=pa, a_=a_: nc.vector.tensor_tensor(out=a_[:], in0=pa[:, 0:128], in1=self.tri[:, d, :], op=ALU.mult),
                             reads=[('ps', d)], writes=[('Am', ai % 4)])
                        ai += 1
                    po = ps[2 + (h % 2)]
                    fl = []
                    for d in range(2):
                        a_, _ = ams[d]
                        fl.append(lambda d=d, a_=a_, b=b, h=h, po=po: nc.tensor.matmul(po[:, 0:128], lhsT=gv[b][:, h * 128:(h + 1) * 128], rhs=a_[:],
                                                                                       start=(d == 0), stop=False))
                        fl.append(lambda d=d, b=b, half=half, hs=hs, po=po: nc.tensor.matmul(po[:, 0:128], lhsT=Sc[b][hs, d, half, :], rhs=gq[b][hs, d, half, :],
                                                                                             start=False, stop=(d == 1)))
                    P.op('pe', fl, reads=[('gv', b), ('Sc', b), ('gq', b)] + [('Am', k) for _, k in ams], writes=[('ps', 2 + (h % 2))])
                    P.op('act', lambda po=po: nc.scalar.activation(out=osq[:], in_=po[:, 0:128], func=AF.Square), reads=[('ps', 2 + (h % 2))], writes=['osq'])
                    P.op('pe', lambda: nc.tensor.matmul(ps[4][:, 0:128], lhsT=self.ones_bf[:], rhs=osq[:], start=True, stop=True), reads=['osq'], writes=[('ps', 4)])
                    P.op('act', lambda: nc.scalar.activation(out=ors[:], in_=ps[4][:, 0:128], func=AF.Sqrt, scale=1.0 / 128, bias=self.eps_t[:, 0:1]),
                         reads=[('ps', 4)], writes=['ors'])
                    P.op('dve', lambda: nc.vector.reciprocal(out=orstd[:], in_=ors[:]), reads=['ors'], writes=['orstd'])
                    P.op('dve', lambda po=po, h=h: nc.vector.scalar_tensor_tensor(out=on[:], in0=po[:, 0:128], scalar=B['glag'][:, h:h + 1], in1=orstd[:],
                                                                                  op0=ALU.mult, op1=ALU.mult), reads=[('ps', 2 + (h % 2)), 'orstd'], writes=['on'])
                    P.op('pool', lambda h=h, b=b: nc.gpsimd.tensor_tensor(out=og[b][:, h, :], in0=on[:], in1=grs[b][:, h, :], op=ALU.mult),
                         reads=['on', ('grs', b)], writes=[('og', b)])
                P.dma('sp', B['mix'][8:12, :, tok].rearrange('g p t -> p g t'), og[b][:], reads=[('og', b)], writes=['mix_scr'])


def build_mod(nchunks, R):
    nc = bass.Bass("TRN2", target_bir_lowering=False)
    ngrp = nchunks // 6
    cs = nc.dram_tensor('cs', [128, KC * R], F32, kind="ExternalInput").ap()
    wmod = nc.dram_tensor('wmod', [2 * ngrp, 128, KC * 768], F32, kind="ExternalInput").ap()
    modb = nc.dram_tensor('modb', [128, 2 * nchunks], F32, kind="ExternalInput").ap()
    modo = nc.dram_tensor('modo', [128, 2 * nchunks * R], F32, kind="ExternalOutput").ap()
    with ExitStack() as es:
        P = Prog(nc, es)
        ps = [es.enter_context(nc.psum_tensor('ps%d' % i, [128, 512], F32)) for i in range(2)]
        with Scope(P) as S:
            cst = S.tile('cst', [128, KC, R], F32)
            scs = S.tile('scs', [128, KC, R], F32)
            mb = S.tile('mb', [128, 2, nchunks], F32)
            mo = S.tile('mo', [128, 2, nchunks, R], F32)
            wt = [S.tile('wt', [128, KC, 768], F32) for _ in range(2)]
            P.dma('sp', cst[:].rearrange('p a b -> p (a b)'), cs[:, :], writes=['cst'])
            P.dma('sp', mb[:].rearrange('p a b -> p (a b)'), modb[:, :], writes=['mb'])
            P.op('act', lambda: nc.scalar.activation(out=scs[:].rearrange('p a b -> p (a b)'), in_=cst[:].rearrange('p a b -> p (a b)'), func=AF.Silu),
                 reads=['cst'], writes=['scs'])
            k = 0
            for l in range(2):
                for grp in range(ngrp):
                    w = wt[k % 2]
                    P.dma('sp', w[:].rearrange('p a b -> p (a b)'), wmod[l * ngrp + grp], writes=[('wt', k % 2)])
                    for n in range(6):
                        nn = grp * 6 + n
                        pb = nn % 2
                        P.op('pe', [lambda kc=kc, w=w, n=n, pb=pb: nc.tensor.matmul(ps[pb][:, 0:R], lhsT=w[:, kc, n * 128:(n + 1) * 128], rhs=scs[:, kc, :],
                                                                                    start=(kc == 0), stop=(kc == KC - 1)) for kc in range(KC)],
                             reads=['scs', ('wt', k % 2)], writes=[('ps', pb)])
                        P.op('dve', lambda l=l, nn=nn, pb=pb: nc.vector.tensor_scalar(out=mo[:, l, nn, :], in0=ps[pb][:, 0:R], scalar1=mb[:, l, nn:nn + 1],
                                                                                      scalar2=None, op0=ALU.add), reads=[('ps', pb), 'mb'], writes=['mo'])
                    k += 1
            P.dma('sp', modo[:, :], mo[:].rearrange('p a b c -> p (a b c)'), reads=['mo'])
    return nc


class FusedBuilder(Builder):
    def __init__(self, nlat):
        super().__init__(nlat, CTX + nlat, do_B=False, do_A=False, final=False)
        self.fused = True

    def mod_phase(self):
        nc, P, ps = self.nc, self.P, self.ps
        R, nchunks, ngrp = 2, 144, 24
        cs = self.din('cs', [128, KC * R])
        wmod = self.din('wmod', [2 * ngrp, 128, KC * 768])
        modb = self.din('modb', [128, 2 * nchunks])
        self.modall = self.G.tile('modall', [128, 2, nchunks, R], F32)
        mo = self.modall
        with Scope(P) as S:
            cst = S.tile('cst', [128, KC, R], F32)
            scs = S.tile('scs', [128, KC, R], F32)
            mb = S.tile('mb', [128, 2, nchunks], F32)
            wt = [S.tile('wt', [128, KC, 768], F32) for _ in range(2)]
            P.dma('sp', cst[:].rearrange('p a b -> p (a b)'), cs[:, :], writes=['cst'])
            P.dma('sp', mb[:].rearrange('p a b -> p (a b)'), modb[:, :], writes=['mb'])
            P.op('act', lambda: nc.scalar.activation(out=scs[:].rearrange('p a b -> p (a b)'), in_=cst[:].rearrange('p a b -> p (a b)'), func=AF.Silu),
                 reads=['cst'], writes=['scs'])
            k = 0
            for l in range(2):
                for grp in range(ngrp):
                    w = wt[k % 2]
                    P.dma('sp', w[:].rearrange('p a b -> p (a b)'), wmod[l * ngrp + grp], writes=[('wt', k % 2)])
                    for n in range(6):
                        nn = grp * 6 + n
                        pb = nn % 2
                        P.op('pe', [lambda kc=kc, w=w, n=n, pb=pb: nc.tensor.matmul(ps[pb][:, 0:R], lhsT=w[:, kc, n * 128:(n + 1) * 128], rhs=scs[:, kc, :],
                                                                                    start=(kc == 0), stop=(kc == KC - 1)) for kc in range(KC)],
                             reads=['scs', ('wt', k % 2)], writes=[('ps', pb)])
                        P.op('dve', lambda l=l, nn=nn, pb=pb: nc.vector.tensor_scalar(out=mo[:, l, nn, :], in0=ps[pb][:, 0:R], scalar1=mb[:, l, nn:nn + 1],
                                                                                      scalar2=None, op0=ALU.add), reads=[('ps', pb), 'mb'], writes=['mo'])
                    k += 1

    def build(self):
        nc = self.nc
        NT = self.NT
        with ExitStack() as es:
            P = Prog(nc, es)
            self.P = P
            self.ps = [es.enter_context(nc.psum_tensor('ps%d' % i, [128, 512], F32)) for i in range(8)]
            G0 = Scope(P)
            G0.__enter__()
            self.G = G0
            self.prologue()
            x_in = self.din('xT_in', [D, NT])
            xs = [self.dscr('xs0', [D, NT], F32), self.dscr('xs1', [D, NT], F32)]
            self.mod_phase()
            stages = [dict(do_B=False, do_A=True, final=False, ctx_B=True, lb=None, la=0, src=x_in, dst=xs[0]),
                      dict(do_B=True, do_A=True, final=False, ctx_B=True, lb=0, la=1, src=xs[0], dst=xs[1]),
                      dict(do_B=True, do_A=False, final=True, ctx_B=False, lb=1, la=None, src=xs[1], dst=None)]
            self.prevA = None
            for si, st in enumerate(stages):
                self.do_B, self.do_A, self.final, self.ctx_B = st['do_B'], st['do_A'], st['final'], st['ctx_B']
                self.lb, self.la = st['lb'], st['la']
                self.sfx = '_s%d' % si
                self.xT_in, self.xT_out = st['src'], st['dst']
                G = Scope(P)
                G.__enter__()
                self.G = G
                self.stage_setup()
                P.barrier()
                self.run_stage()
                if self.do_A:
                    self.prevA = self.A
                G.__exit__(None, None, None)
            P.barrier(full=True)
            G0.__exit__(None, None, None)
        return nc


def run_fused(inp):
    x = np.asarray(inp['x'], dtype=np.float32)
    ctx = np.asarray(inp['ctx'], dtype=np.float32)
    c = np.asarray(inp['c'], dtype=np.float32)
    c_ctx = np.asarray(inp['c_ctx'], dtype=np.float32)
    Bsz, Lq, _ = x.shape
    fb = FusedBuilder(Lq)
    nc = fb.build()
    consts = make_consts(Lq, 0)
    LW = [_layer_params(inp, l) for l in range(2)]
    mod_b = np.asarray(inp['mod_b'], dtype=np.float32)
    wl = []
    for l in range(2):
        blk = np.asarray(inp['mod_w'][l], dtype=np.float32).reshape(KC, 128, 24, 768).transpose(2, 1, 0, 3)
        wl.append(_c(blk).reshape(24, 128, KC * 768))
    wmod = np.concatenate(wl, 0)
    modb = _c(np.stack([mod_b[l].reshape(144, 128).T for l in range(2)], 1)).reshape(128, 2 * 144)
    fg = _c(np.asarray(inp['final_norm_g'], dtype=np.float32).reshape(KC, 128).T)

    def a_in(l, sfx):
        W = LW[l]
        return {'normg_a' + sfx: W['normg'], 'wgu_a' + sfx: W['wgu1'], 'wd_a' + sfx: W['wd1'], 'wfm_a' + sfx: W['wfm'], 'wtm_a' + sfx: W['wtm'],
                'p_qkg' + sfx: W['qkg'], 'p_gatew' + sfx: W['gatew'], 'p_gateb' + sfx: W['gateb'], 'p_wsT_f' + sfx: W['wsT_f'],
                'p_bsb' + sfx: W['bsb'], 'p_gmg' + sfx: W['gmg'], 'p_cos' + sfx: consts['cos'], 'p_sin' + sfx: consts['sin']}

    def b_in(l, sfx):
        W = LW[l]
        return {'normg_b' + sfx: W['normg'], 'wout_b' + sfx: W['wout'], 'wgu_b' + sfx: W['wgu2'], 'wd_b' + sfx: W['wd2'], 'p_glag' + sfx: W['glag']}
    maps = []
    for b in range(Bsz):
        crow = np.stack([c[b], c_ctx], 0)
        m = {'c_ones': consts['ones'], 'c_tri': consts['tri'], 'c_pm': consts['pm'],
             'cs': _c(crow.reshape(2, KC, 128).transpose(2, 1, 0)).reshape(128, KC * 2), 'wmod': wmod, 'modb': modb,
             'xT_in': _c(np.concatenate([ctx[b].T, x[b].T], axis=1)), 'final_g': fg}
        m.update(a_in(0, '_s0'))
        m.update(b_in(0, '_s1'))
        m.update(a_in(1, '_s1'))
        m.update(b_in(1, '_s2'))
        assert set(m) == set(fb.inputs), (set(m) ^ set(fb.inputs))
        maps.append(m)
    res = _launch(nc, maps)
    out = np.empty((Bsz, Lq, D), np.float32)
    for b in range(Bsz):
        out[b] = res[b]['yT'].T
    return out


def _c(a):
    return np.ascontiguousarray(a)


def _layer_params(inp, l):
    f = lambda k: np.asarray(inp[k][l], dtype=np.float32)
    W = {}
    W['wgu1'] = lay_wgu(f('ffn1_w_gu'))
    W['wd1'] = lay_wd(f('ffn1_w_down'))
    W['wgu2'] = lay_wgu(f('ffn2_w_gu'))
    W['wd2'] = lay_wd(f('ffn2_w_down'))
    win = f('w_in')
    W['wfm'] = lay_cols(win, _fm_cols())
    W['wtm'] = lay_cols(win, _tm_cols())
    W['wout'] = lay_wout(f('w_out'))
    W['normg'] = _c(f('norm_g').reshape(3, KC, 128).transpose(2, 0, 1)).reshape(128, 3 * KC)
    W['qkg'] = _c(f('qk_norm_g').T)
    W['gatew'] = _c(f('gla_gate_w').transpose(1, 0, 2)).reshape(16, 512)
    W['gateb'] = _c(f('gla_gate_b').reshape(1, 512))
    W['wsT_f'] = _c(f('gmlp_w_s').transpose(2, 0, 1)).reshape(128, 512)
    W['bsb'] = _c(np.broadcast_to(f('gmlp_b_s').reshape(1, 512), (128, 512)))
    W['gmg'] = _c(f('gmlp_norm_g').reshape(4, 128).T)
    W['glag'] = _c(f('gla_norm_g').reshape(4, 128).T)
    return W


def _launch(nc, in_maps):
    res = run_bass_kernel_spmd(nc, in_maps, core_ids=list(range(len(in_maps))))
    return res.results


def run_model(inp, nseg):
    x = np.asarray(inp['x'], dtype=np.float32)
    ctx = np.asarray(inp['ctx'], dtype=np.float32)
    c = np.asarray(inp['c'], dtype=np.float32)
    c_ctx = np.asarray(inp['c_ctx'], dtype=np.float32)
    Bsz, Lq, _ = x.shape
    nlat = Lq // nseg
    ncores = Bsz * nseg
    NT = CTX + nlat
    NCH = NT // 128
    nkeys = CTX + Lq
    R = Bsz + 1
    nchunks = 144 // ncores
    ncol = nchunks * 128
    ngrp = nchunks // 6
    crow = np.concatenate([c, c_ctx[None, :]], 0)
    cs = _c(crow.reshape(R, KC, 128).transpose(2, 1, 0)).reshape(128, KC * R)
    mod_w = inp['mod_w']
    mod_b = np.asarray(inp['mod_b'], dtype=np.float32)
    maps = []
    for k in range(ncores):
        wl = []
        for l in range(2):
            blk = np.asarray(mod_w[l][:, k * ncol:(k + 1) * ncol], dtype=np.float32)
            blk = blk.reshape(KC, 128, ngrp, 768).transpose(2, 1, 0, 3)
            wl.append(_c(blk).reshape(ngrp, 128, KC * 768))
        mb = np.stack([mod_b[l, k * ncol:(k + 1) * ncol].reshape(nchunks, 128).T for l in range(2)], 1)
        maps.append({'cs': cs, 'wmod': np.concatenate(wl, 0), 'modb': _c(mb).reshape(128, 2 * nchunks)})
    res = _launch(build_mod(nchunks, R), maps)
    mod_all = np.zeros((2, R, 9 * D), np.float32)
    for k in range(ncores):
        mo = res[k]['modo'].reshape(128, 2, nchunks, R)
        for l in range(2):
            mod_all[l, :, k * ncol:(k + 1) * ncol] = mo[:, l].transpose(2, 1, 0).reshape(R, ncol)

    def modT(l, b):
        m = mod_all[l][[b, R - 1]]
        return _c(m.reshape(2, 9, KC, 128).transpose(3, 1, 2, 0)).reshape(128, 9 * KC * 2)

    consts = [make_consts(nlat, s) for s in range(nseg)]
    LW = [_layer_params(inp, l) for l in range(2)]

    def common(k):
        s = k % nseg
        return {'c_ones': consts[s]['ones'], 'c_tri': consts[s]['tri'], 'c_pm': consts[s]['pm']}

    def a_inputs(k, l):
        b, s = divmod(k, nseg)
        W = LW[l]
        return {'modT_a': modT(l, b), 'normg_a': W['normg'], 'wgu_a': W['wgu1'], 'wd_a': W['wd1'], 'wfm_a': W['wfm'], 'wtm_a': W['wtm'],
                'p_qkg': W['qkg'], 'p_gatew': W['gatew'], 'p_gateb': W['gateb'], 'p_wsT_f': W['wsT_f'], 'p_bsb': W['bsb'], 'p_gmg': W['gmg'],
                'p_cos': consts[s]['cos'], 'p_sin': consts[s]['sin']}

    def b_inputs(k, l, prev):
        b, s = divmod(k, nseg)
        W = LW[l]
        o = prev[k]
        grp = [prev[b * nseg + j] for j in range(nseg)]
        kT = np.concatenate([grp[0]['o_kT'][:, :, :CTX]] + [g_['o_kT'][:, :, CTX:] for g_ in grp], axis=2)
        v = np.concatenate([grp[0]['o_v'][:CTX]] + [g_['o_v'][CTX:] for g_ in grp], axis=0)
        gs = [g_['o_glasum'].reshape(128, 2, 2, 2, 129) for g_ in grp]
        ctxS = _c(gs[s][:, 0, :, :, 0:128]).reshape(128, 2 * 2 * 128)
        predD = np.ones((128, 2, 3, 2), np.float32)
        predB = np.zeros((128, 2, 3, 2, 128), np.float32)
        fw = list(range(0, s))
        bw = list(range(nseg - 1, s, -1))
        for d, lst in ((0, fw), (1, bw)):
            for i, j in enumerate(lst):
                slot = 3 - len(lst) + i
                predD[:, d, slot, :] = gs[j][:, 1, d, :, 128]
                predB[:, d, slot, :, :] = gs[j][:, 1, d, :, 0:128]
        return {'modT_b': modT(l, b), 'normg_b': W['normg'], 'wout_b': W['wout'], 'wgu_b': W['wgu2'], 'wd_b': W['wd2'],
                'i_qT': o['o_qT'], 'i_kT': _c(kT), 'i_v': _c(v), 'i_gq': o['o_gq'], 'i_gk': o['o_gk'], 'i_gv': o['o_gv'], 'i_S0': o['o_S0'],
                'i_grs': o['o_grs'], 'i_gm': o['o_gm'], 'i_dcum': o['o_dcum'], 'i_ctxS': ctxS,
                'i_predD': predD.reshape(128, -1), 'i_predB': predB.reshape(128, -1), 'p_glag': W['glag']}

    maps = []
    for k in range(ncores):
        b, s = divmod(k, nseg)
        xT = np.concatenate([ctx[b].T, x[b, s * nlat:(s + 1) * nlat].T], axis=1)
        m = common(k)
        m.update(a_inputs(k, 0))
        m['xT_in'] = _c(xT)
        maps.append(m)
    r1 = _launch(Builder(nlat, nkeys, do_B=False, do_A=True, final=False).build(), maps)
    maps = []
    for k in range(ncores):
        m = common(k)
        m.update(b_inputs(k, 0, r1))
        m.update(a_inputs(k, 1))
        m['xT_in'] = r1[k]['xT_out']
        maps.append(m)
    r2 = _launch(Builder(nlat, nkeys, do_B=True, do_A=True, final=False).build(), maps)
    fg = _c(np.asarray(inp['final_norm_g'], dtype=np.float32).reshape(KC, 128).T)
    maps = []
    for k in range(ncores):
        m = common(k)
        m.update(b_inputs(k, 1, r2))
        m['xT_in'] = r2[k]['xT_out']
        m['final_g'] = fg
        maps.append(m)
    r3 = _launch(Builder(nlat, nkeys, do_B=True, do_A=False, final=True, ctx_B=False).build(), maps)
    out = np.empty((Bsz, Lq, D), np.float32)
    for k in range(ncores):
        b, s = divmod(k, nseg)
        out[b, s * nlat:(s + 1) * nlat, :] = r3[k]['yT'].T
    return out


def kernel(**inputs):
    return run_fused(inputs)
```

```python
import numpy as np
import ml_dtypes
from contextlib import ExitStack
import concourse.bass as bass
import concourse.mybir as mybir
from concourse.bass_utils import run_bass_kernel_spmd

F32 = mybir.dt.float32
BF16 = mybir.dt.bfloat16
AF = mybir.ActivationFunctionType
ALU = mybir.AluOpType
NPBF = ml_dtypes.bfloat16

D = 2048
KC = 16
DFF = 5632
NJ = 44
CTX = 256
EPS = 1e-6
INW = 4128
NFM = 23
NTM = 3
DBG_SKIP = set()


class Prog:
    def __init__(self, nc, es, nch=10):
        self.nc = nc
        self.es = es
        self.eng = {'pe': nc.tensor, 'act': nc.scalar, 'dve': nc.vector, 'pool': nc.gpsimd, 'sp': nc.sync}
        self.semh = {}
        for e in self.eng:
            self.semh['e_' + e] = es.enter_context(nc.semaphore('e_' + e))
        self.cnt = {e: 0 for e in self.eng}
        self.known = {e: {} for e in self.eng}
        self.lastw = {}
        self.readers = {}
        self.chans = {}
        for q in ('sp', 'pool'):
            lst = []
            for i in range(nch):
                key = 'd_%s%d' % (q, i)
                self.semh[key] = es.enter_context(nc.semaphore(key))
                lst.append([key, 0])
            self.chans[q] = [lst, 0]
        self.uid = 0

    def _wait(self, eng, sk, val):
        if self.known[eng].get(sk, 0) >= val:
            return
        self.known[eng][sk] = val
        self.eng[eng].wait_ge(self.semh[sk], val)

    def _deps(self, eng, reads, writes, extra=()):
        need = {}

        def add(ev):
            sk, val, e = ev
            if e == eng and eng == 'pe':
                return
            if need.get(sk, 0) < val:
                need[sk] = val
        for r in reads:
            w = self.lastw.get(r)
            if w is not None:
                add(w)
        for w_ in writes:
            w = self.lastw.get(w_)
            if w is not None:
                add(w)
            rd = self.readers.get(w_)
            if rd:
                for sk, (val, e) in rd.items():
                    add((sk, val, e))
        for ev in extra:
            if ev is not None:
                add(ev)
        for sk, val in need.items():
            self._wait(eng, sk, val)

    def _record(self, ev, reads, writes):
        sk, val, e = ev
        for r in reads:
            self.readers.setdefault(r, {})[sk] = (val, e)
        for w in writes:
            self.lastw[w] = ev
            self.readers[w] = {}

    def op(self, eng, fns, reads=(), writes=()):
        if callable(fns):
            fns = [fns]
        self._deps(eng, reads, writes)
        ins = None
        for f in fns:
            ins = f()
        self.cnt[eng] += 1
        ins.then_inc(self.semh['e_' + eng], 1)
        ev = ('e_' + eng, self.cnt[eng], eng)
        self._record(ev, reads, writes)
        return ev

    def dma(self, q, out, in_, reads=(), writes=(), **kw):
        lst, idx = self.chans[q]
        ch = lst[idx % len(lst)]
        self.chans[q][1] = idx + 1
        prev = (ch[0], ch[1], 'dma') if ch[1] else None
        self._deps(q, reads, writes, extra=(prev,))
        ch[1] += 16
        self.eng[q].dma_start(out=out, in_=in_, **kw).then_inc(self.semh[ch[0]], 16)
        ev = (ch[0], ch[1], 'dma')
        self._record(ev, reads, writes)
        return ev

    def barrier(self, full=False):
        targets = [('e_' + e, self.cnt[e]) for e in self.eng if self.cnt[e] > 0]
        for q in self.chans:
            if q == 'pool' and not full:
                continue
            for ch in self.chans[q][0]:
                if ch[1] > 0:
                    targets.append((ch[0], ch[1]))
        for e in self.eng:
            for sk, val in targets:
                if sk == 'e_pe' and e == 'pe':
                    continue
                self._wait(e, sk, val)
        if full:
            self.lastw.clear()
        else:
            self.lastw = {k: ev for k, ev in self.lastw.items() if ev[0].startswith('d_pool')}
        self.readers.clear()

    def key(self, name):
        self.uid += 1
        return (name, self.uid)


class Scope:
    def __init__(self, P):
        self.P = P
        self.es = ExitStack()

    def __enter__(self):
        self.es.__enter__()
        return self

    def tile(self, name, shape, dt):
        self.P.uid += 1
        return self.es.enter_context(self.P.nc.sbuf_tensor('%s_%d' % (name, self.P.uid), list(shape), dt))

    def __exit__(self, *a):
        self.P.barrier()
        return self.es.__exit__(*a)


def _fm_cols():
    groups = []
    for h in range(8):
        groups.append(np.arange(h * 128, (h + 1) * 128))
    for g in range(2):
        groups.append(1024 + np.arange(g * 128, (g + 1) * 128))
    for g in range(2):
        groups.append(1536 + np.arange(g * 128, (g + 1) * 128))
    for g in range(2):
        groups.append(1792 + np.arange(g * 128, (g + 1) * 128))
    for g in range(4):
        groups.append(2560 + np.arange(g * 128, (g + 1) * 128))
    glr = np.full(128, -1)
    glr[:32] = 3072 + np.arange(32)
    groups.append(glr)
    for g in range(4):
        groups.append(3104 + np.arange(g * 128, (g + 1) * 128))
    return groups


def _tm_cols():
    return [np.concatenate([1280 + np.arange(256), 1792 + np.arange(256)]),
            2048 + np.arange(512),
            3616 + np.arange(512)]


def lay_wgu(w):
    g = w[:, :DFF].reshape(KC, 128, NJ, 128)
    u = w[:, DFF:].reshape(KC, 128, NJ, 128)
    s = np.stack([g, u], axis=3)
    return np.ascontiguousarray(s.transpose(2, 1, 0, 3, 4)).reshape(NJ, 128, KC * 256)


def lay_wd(w):
    s = w.reshape(NJ, 128, KC, 128)
    return np.ascontiguousarray(s.transpose(2, 1, 0, 3)).reshape(KC, 128, NJ * 128)


def lay_cols(w, groups):
    outs = []
    for cols in groups:
        sel = np.where(cols >= 0, cols, 0)
        blk = w[:, sel]
        if (cols < 0).any():
            blk = blk.copy()
            blk[:, cols < 0] = 0.0
        blk = blk.reshape(KC, 128, len(cols)).transpose(1, 0, 2)
        outs.append(np.ascontiguousarray(blk).reshape(128, KC * len(cols)))
    return np.stack(outs, 0)


def lay_wout(w):
    s = w.reshape(KC, 128, KC, 128)
    return np.ascontiguousarray(s.transpose(2, 1, 0, 3)).reshape(KC, 128, KC * 128)


def make_consts(nlat, seg):
    c = {}
    c['ones'] = np.ones((128, 128), np.float32)
    j = np.arange(128)[:, None]
    i = np.arange(128)[None, :]
    tri = np.stack([(j <= i), (j >= i), (j > i), (j < i)], 0).astype(np.float32)
    c['tri'] = np.ascontiguousarray(tri.transpose(1, 0, 2)).reshape(128, 4 * 128)
    pm = np.zeros((128, 128), np.float32)
    for m in range(128):
        blk = (m // 32) % 2
        k = m + 32 if blk == 0 else m - 32
        pm[k, m] = 1.0
    c['pm'] = pm
    t = seg * nlat + np.arange(nlat)
    row = (t // 64).astype(np.float32)
    col = (t % 64).astype(np.float32)
    nf = 32
    inv = (10000.0 ** (-np.arange(nf, dtype=np.float32) / nf)).astype(np.float32)
    ar = row[None, :] * inv[:, None]
    ac = col[None, :] * inv[:, None]
    cos = np.concatenate([np.cos(ar), np.cos(ar), np.cos(ac), np.cos(ac)], 0).astype(np.float32)
    sin = np.concatenate([-np.sin(ar), np.sin(ar), -np.sin(ac), np.sin(ac)], 0).astype(np.float32)
    c['cos'] = np.ascontiguousarray(cos)
    c['sin'] = np.ascontiguousarray(sin)
    return c


class Builder:
    def __init__(self, nlat, nkeys, do_B, do_A, final, ctx_B=True):
        self.nlat = nlat
        self.NT = CTX + nlat
        self.NCH = self.NT // 128
        self.nkeys = nkeys
        self.do_B, self.do_A, self.final, self.ctx_B = do_B, do_A, final, ctx_B
        self.nc = bass.Bass("TRN2", target_bir_lowering=False)
        self.inputs = []
        self.outputs = []
        self.in_specs = {}
        self.fused = False
        self.sfx = ''

    def din(self, name, shape, dt=F32):
        self.inputs.append(name)
        self.in_specs[name] = (list(shape), dt)
        return self.nc.dram_tensor(name, list(shape), dt, kind="ExternalInput").ap()

    def dout(self, name, shape, dt=F32):
        if self.fused and name != 'yT':
            return self.dscr(name, shape, dt)
        self.outputs.append(name)
        return self.nc.dram_tensor(name, list(shape), dt, kind="ExternalOutput").ap()

    def dscr(self, name, shape, dt=BF16):
        return self.nc.dram_tensor(name, list(shape), dt, kind="Internal").ap()

    def supertiles(self, include_ctx=True):
        st = []
        if include_ctx:
            st.append((0, CTX, True))
        for i in range(self.nlat // 512):
            st.append((CTX + i * 512, 512, False))
        return st

    def prologue(self):
        nc, P, G = self.nc, self.P, self.G
        ones_f = G.tile('ones_f', [128, 128], F32)
        self.ones_bf = G.tile('ones_bf', [128, 128], BF16)
        self.tri = G.tile('tri', [128, 4, 128], F32)
        self.pm = G.tile('pm', [128, 128], F32)
        c_ones = self.din('c_ones', [128, 128])
        c_tri = self.din('c_tri', [128, 512])
        c_pm = self.din('c_pm', [128, 128])
        P.dma('sp', ones_f[:], c_ones[:, :], writes=['ones_f'])
        P.dma('sp', self.tri[:].rearrange('p a b -> p (a b)'), c_tri[:, :], writes=['tri'])
        P.dma('sp', self.pm[:], c_pm[:, :], writes=['pm'])
        P.op('dve', lambda: nc.vector.tensor_copy(out=self.ones_bf[:], in_=ones_f[:]), reads=['ones_f'], writes=['ones_bf'])
        self.ones_f = ones_f
        self.eps_t = G.tile('eps_t', [128, 1], F32)
        self.one_t = G.tile('one_t', [128, 1], F32)
        P.op('dve', lambda: nc.vector.memset(self.eps_t[:], EPS), writes=['eps_t'])
        P.op('dve', lambda: nc.vector.memset(self.one_t[:], 1.0), writes=['one_t'])
        self.x_t = G.tile('x_t', [128, KC, 512], F32)

    def stage_setup(self):
        P, G = self.P, self.G
        if self.do_B:
            self.setup_layer('b')
            self.setup_B()
        if self.do_A:
            self.setup_layer('a')
            self.setup_A()
        if self.final:
            self.fin_g = G.tile('fin_g', [128, KC], F32)
            fg = self.din('final_g', [128, KC])
            P.dma('sp', self.fin_g[:], fg[:, :], writes=['fin_g'])
            self.y_out = self.dout('yT', [D, self.nlat])

    def run_stage(self):
        P = self.P
        if self.do_B:
            self.attention()
            self.gla_output()
        for (t0, T, is_ctx) in self.supertiles():
            if is_ctx and self.do_B and not self.do_A and not self.ctx_B:
                continue
            r = 1 if is_ctx else 0
            self.load_x(t0, T)
            if self.do_B and (self.ctx_B or not is_ctx):
                self.outproj(t0, T, r)
                self.ffn('b', 2, T, r)
            if self.do_A:
                self.ffn('a', 0, T, r)
                self.inproj(t0, T, r, is_ctx)
            if self.final:
                if not is_ctx:
                    self.final_norm(t0, T)
            else:
                self.store_x(t0, T)
            P.barrier()
        if self.do_A:
            self.gla_scan()

    def build(self):
        nc = self.nc
        NT = self.NT
        with ExitStack() as es:
            P = Prog(nc, es)
            self.P = P
            self.ps = [es.enter_context(nc.psum_tensor('ps%d' % i, [128, 512], F32)) for i in range(8)]
            G = Scope(P)
            G.__enter__()
            self.G = G
            self.prologue()
            self.xT_in = self.din('xT_in', [D, NT])
            if not self.final:
                self.xT_out = self.dout('xT_out', [D, NT])
            self.stage_setup()
            P.barrier()
            self.run_stage()
            P.barrier(full=True)
            G.__exit__(None, None, None)
        return nc

    def cast_weights(self, name, shape3, chunk_reads=None):
        src = self.din(name + self.sfx, shape3)
        dst = self.dscr(name + self.sfx + '_bf', shape3)
        for g in range(shape3[0]):
            self.P.dma('pool', dst[g], src[g], writes=[(name + self.sfx, g)], max_dma_last_dim=8192)
        return dst

    def setup_layer(self, tag):
        nc, P, G = self.nc, self.P, self.G
        L = {}
        ng = G.tile('ng' + tag, [128, 3, KC], F32)
        d_ng = self.din('normg_' + tag + self.sfx, [128, 3 * KC])
        if self.fused:
            lyr = self.lb if tag == 'b' else self.la
            modT = self.modall[:].rearrange('p l (a b) c -> p l a b c', a=9)[:, lyr]
        else:
            modT = G.tile('modT' + tag, [128, 9, KC, 2], F32)
            d_mod = self.din('modT_' + tag, [128, 9 * KC * 2])
            P.dma('sp', modT[:].rearrange('p a b c -> p (a b c)'), d_mod[:, :], writes=['modT' + tag])
        P.dma('sp', ng[:].rearrange('p a b -> p (a b)'), d_ng[:, :], writes=['ng' + tag])
        L['gs'] = G.tile('gs' + tag, [128, 3, 2, KC], F32)
        L['sh'] = G.tile('sh' + tag, [128, 3, 2, KC], F32)
        L['hg'] = G.tile('hg' + tag, [128, 3, 2, KC], F32)
        for i3 in range(3):
            for r in range(2):
                P.op('dve', lambda i3=i3, r=r: nc.vector.scalar_tensor_tensor(
                    out=L['gs'][:, i3, r, :], in0=modT[:, 3 * i3 + 1, :, r], scalar=1.0, in1=ng[:, i3, :],
                    op0=ALU.add, op1=ALU.mult), reads=['modT' + tag, 'ng' + tag], writes=[('gs', tag, i3, r)])
                P.op('dve', lambda i3=i3, r=r: nc.vector.tensor_copy(out=L['sh'][:, i3, r, :], in_=modT[:, 3 * i3, :, r]),
                     reads=['modT' + tag], writes=[('sh', tag, i3, r)])
                P.op('dve', lambda i3=i3, r=r: nc.vector.tensor_scalar(
                    out=L['hg'][:, i3, r, :], in0=modT[:, 3 * i3 + 2, :, r], scalar1=(1.0 if i3 == 1 else 0.5), scalar2=None,
                    op0=ALU.mult), reads=['modT' + tag], writes=[('hg', tag, i3, r)])
        setattr(self, 'L' + tag, L)

    def setup_A(self):
        nc, P, G = self.nc, self.P, self.G
        NT = self.NT
        A = {'sfx': self.sfx}
        A['wgu'] = self.cast_weights('wgu_a', [NJ, 128, KC * 256])
        A['wd'] = self.cast_weights('wd_a', [KC, 128, NJ * 128])
        A['wfm'] = self.cast_weights('wfm_a', [NFM, 128, KC * 128])
        A['wtm'] = self.cast_weights('wtm_a', [NTM, 128, KC * 512])
        A['qkg'] = G.tile('qkg', [128, 2], F32)
        A['gatew'] = G.tile('gatew', [16, 2, 256], F32)
        A['gateb'] = G.tile('gateb', [1, 2, 256], F32)
        A['wsT_f'] = G.tile('wsT_f', [128, 4, 128], F32)
        A['wsT'] = G.tile('wsT', [128, 4, 128], BF16)
        A['bsb'] = G.tile('bsb', [128, 4, 128], F32)
        A['gmg'] = G.tile('gmg', [128, 4], F32)
        A['cosd'] = self.din('p_cos' + self.sfx, [128, self.nlat])
        A['sind'] = self.din('p_sin' + self.sfx, [128, self.nlat])
        for nm, shp in [('qkg', [128, 2]), ('gatew', [16, 512]), ('gateb', [1, 512]), ('wsT_f', [128, 512]),
                        ('bsb', [128, 512]), ('gmg', [128, 4])]:
            d = self.din('p_' + nm + self.sfx, shp)
            t = A[nm]
            ap = t[:] if len(t.shape) == 2 else t[:].rearrange('p a b -> p (a b)')
            P.dma('sp', ap, d[:, :], writes=[nm])
        P.op('dve', lambda: nc.vector.tensor_copy(out=A['wsT'][:], in_=A['wsT_f'][:]), reads=['wsT_f'], writes=['wsT'])
        A['qT'] = self.dout('o_qT' + self.sfx, [8, 128, NT], BF16)
        A['kT'] = self.dout('o_kT' + self.sfx, [2, 128, NT], BF16)
        A['v'] = self.dout('o_v' + self.sfx, [NT, 256], BF16)
        A['gq'] = self.dout('o_gq' + self.sfx, [2, 2, 128, NT], BF16)
        A['gk'] = self.dout('o_gk' + self.sfx, [2, 2, 128, NT], BF16)
        A['gv'] = self.dout('o_gv' + self.sfx, [NT, 512], BF16)
        A['kh'] = self.dscr('s_kh' + self.sfx, [2, NT, 256], BF16)
        A['S0'] = self.dout('o_S0' + self.sfx, [2, self.NCH, 128, 256], F32)
        A['grs'] = self.dout('o_grs' + self.sfx, [4, 128, NT], BF16)
        A['gm'] = self.dout('o_gm' + self.sfx, [4, 128, NT], BF16)
        A['glasum'] = self.dout('o_glasum' + self.sfx, [128, 2 * 2 * 2 * 129], F32)
        A['dcum'] = self.dout('o_dcum' + self.sfx, [128, 2 * self.NCH * 2], F32)
        A['eb'] = G.tile('eb', [128, 2, self.NCH, 2], F32)
        self.A = A

    def setup_B(self):
        nc, P, G = self.nc, self.P, self.G
        NT = self.NT
        B = {'sfx': self.sfx}
        B['wout'] = self.cast_weights('wout_b', [KC, 128, KC * 128])
        B['wgu'] = self.cast_weights('wgu_b', [NJ, 128, KC * 256])
        B['wd'] = self.cast_weights('wd_b', [KC, 128, NJ * 128])
        if self.fused:
            pa = self.prevA
            for k_ in ('qT', 'kT', 'v', 'gq', 'gk', 'gv', 'S0', 'grs', 'gm', 'dcum', 'glasum'):
                B[k_] = pa[k_]
            B['glag'] = G.tile('glag', [128, 4], F32)
            d = self.din('p_glag' + self.sfx, [128, 4])
            P.dma('sp', B['glag'][:], d[:, :], writes=['glag'])
            B['mix'] = self.dscr('s_mix' + self.sfx, [12, 128, NT], BF16)
            self.B = B
            return
        B['qT'] = self.din('i_qT', [8, 128, NT], BF16)
        B['kT'] = self.din('i_kT', [2, 128, self.nkeys], BF16)
        B['v'] = self.din('i_v', [self.nkeys, 256], BF16)
        B['gq'] = self.din('i_gq', [2, 2, 128, NT], BF16)
        B['gk'] = self.din('i_gk', [2, 2, 128, NT], BF16)
        B['gv'] = self.din('i_gv', [NT, 512], BF16)
        B['S0'] = self.din('i_S0', [2, self.NCH, 128, 256], F32)
        B['grs'] = self.din('i_grs', [4, 128, NT], BF16)
        B['gm'] = self.din('i_gm', [4, 128, NT], BF16)
        B['dcum'] = self.din('i_dcum', [128, 2 * self.NCH * 2], F32)
        B['ctxS'] = self.din('i_ctxS', [128, 2 * 2 * 128], F32)
        B['predD'] = self.din('i_predD', [128, 2 * 3 * 2], F32)
        B['predB'] = self.din('i_predB', [128, 2 * 3 * 2 * 128], F32)
        B['glag'] = G.tile('glag', [128, 4], F32)
        d = self.din('p_glag', [128, 4])
        P.dma('sp', B['glag'][:], d[:, :], writes=['glag'])
        B['mix'] = self.dscr('s_mix', [12, 128, NT], BF16)
        self.B = B

    def load_x(self, t0, T):
        P = self.P
        src = self.xT_in.rearrange('(kc p) t -> p kc t', p=128)
        for h in range(2):
            P.dma('sp', self.x_t[:, h * 8:(h + 1) * 8, :T], src[:, h * 8:(h + 1) * 8, t0:t0 + T],
                  writes=[('x', m) for m in range(h * 8, h * 8 + 8)])

    def store_x(self, t0, T):
        P = self.P
        dst = self.xT_out.rearrange('(kc p) t -> p kc t', p=128)
        for h in range(2):
            P.dma('sp', dst[:, h * 8:(h + 1) * 8, t0:t0 + T], self.x_t[:, h * 8:(h + 1) * 8, :T],
                  reads=[('x', m) for m in range(h * 8, h * 8 + 8)])

    def modnorm(self, S, hT, T, gs, sh):
        nc, P, ps = self.nc, self.P, self.ps
        x_t = self.x_t
        sq = [S.tile('sq', [128, 512], BF16) for _ in range(2)]
        tmp = [S.tile('mtmp', [128, 512], F32) for _ in range(2)]
        rs = S.tile('rs', [128, 512], F32)
        rstd = S.tile('rstd', [128, 512], F32)
        for kc in range(KC):
            b = kc % 2
            P.op('act', lambda kc=kc, b=b: nc.scalar.activation(out=sq[b][:, :T], in_=x_t[:, kc, :T], func=AF.Square),
                 reads=[('x', kc)], writes=[('sq', b)])
            P.op('pe', lambda kc=kc, b=b: nc.tensor.matmul(ps[0][:, :T], lhsT=self.ones_bf[:], rhs=sq[b][:, :T],
                                                           start=(kc == 0), stop=(kc == KC - 1)),
                 reads=[('sq', b)], writes=[('ps', 0)])
        P.op('act', lambda: nc.scalar.activation(out=rs[:, :T], in_=ps[0][:, :T], func=AF.Sqrt, scale=1.0 / D, bias=self.eps_t[:, 0:1]),
             reads=[('ps', 0)], writes=['rs'])
        P.op('dve', lambda: nc.vector.reciprocal(out=rstd[:, :T], in_=rs[:, :T]), reads=['rs'], writes=['rstd'])
        for kc in range(KC):
            b = kc % 2
            P.op('dve', lambda kc=kc, b=b: nc.vector.scalar_tensor_tensor(
                out=tmp[b][:, :T], in0=x_t[:, kc, :T], scalar=gs[:, kc:kc + 1], in1=rstd[:, :T], op0=ALU.mult, op1=ALU.mult),
                reads=[('x', kc), 'rstd'], writes=[('mtmp', b)])
            P.op('act', lambda kc=kc, b=b: nc.scalar.activation(out=hT[:, kc, :T], in_=tmp[b][:, :T], func=AF.Identity,
                                                                bias=sh[:, kc:kc + 1], scale=1.0),
                 reads=[('mtmp', b)], writes=[('hT', kc)])

    def ffn(self, tag, i3, T, r):
        nc, P, ps = self.nc, self.P, self.ps
        L = getattr(self, 'L' + tag)
        W = self.A if tag == 'a' else self.B
        wgu, wd = W['wgu'], W['wd']
        wn_gu = 'wgu_' + tag + W['sfx']
        wn_d = 'wd_' + tag + W['sfx']
        x_t = self.x_t
        with Scope(P) as S:
            hT = S.tile('hT', [128, KC, 512], BF16)
            act = S.tile('act', [128, NJ, 512], BF16)
            gu = [S.tile('gu', [128, KC, 256], BF16) for _ in range(3)]
            wdt = [S.tile('wdt', [128, NJ, 128], BF16) for _ in range(2)]
            sgt = [S.tile('sgt', [128, 512], F32) for _ in range(2)]

            def load_gu(j):
                P.dma('sp', gu[j % 3][:].rearrange('p a b -> p (a b)'), wgu[j], reads=[(wn_gu, j)], writes=[('gu', j % 3)])

            def load_wd(m):
                P.dma('sp', wdt[m % 2][:].rearrange('p a b -> p (a b)'), wd[m], reads=[(wn_d, m)], writes=[('wdt', m % 2)])
            load_gu(0)
            load_gu(1)
            self.modnorm(S, hT, T, L['gs'][:, i3, r, :], L['sh'][:, i3, r, :])
            hkeys = [('hT', kc) for kc in range(KC)]
            for j in range(NJ):
                if j + 2 < NJ:
                    load_gu(j + 2)
                if j == NJ - 4:
                    load_wd(0)
                if j == NJ - 2:
                    load_wd(1)
                w = gu[j % 3]
                pg = ps[1 + 2 * (j % 2)]
                pu = ps[2 + 2 * (j % 2)]
                P.op('pe', [lambda kc=kc, w=w, pg=pg: nc.tensor.matmul(pg[:, :T], lhsT=w[:, kc, 0:128], rhs=hT[:, kc, :T],
                                                                       start=(kc == 0), stop=(kc == KC - 1)) for kc in range(KC)],
                     reads=hkeys + [('gu', j % 3)], writes=[('ps', 1 + 2 * (j % 2))])
                P.op('pe', [lambda kc=kc, w=w, pu=pu: nc.tensor.matmul(pu[:, :T], lhsT=w[:, kc, 128:256], rhs=hT[:, kc, :T],
                                                                       start=(kc == 0), stop=(kc == KC - 1)) for kc in range(KC)],
                     reads=hkeys + [('gu', j % 3)], writes=[('ps', 2 + 2 * (j % 2))])
                P.op('act', lambda j=j, pg=pg: nc.scalar.activation(out=sgt[j % 2][:, :T], in_=pg[:, :T], func=AF.Silu),
                     reads=[('ps', 1 + 2 * (j % 2))], writes=[('sgt', j % 2)])
                P.op('dve', lambda j=j, pu=pu: nc.vector.tensor_tensor(out=act[:, j, :T], in0=sgt[j % 2][:, :T], in1=pu[:, :T], op=ALU.mult),
                     reads=[('sgt', j % 2), ('ps', 2 + 2 * (j % 2))], writes=[('act', j)])
            akeys = [('act', j) for j in range(NJ)]
            hg = L['hg']
            for m in range(KC):
                w = wdt[m % 2]
                po = ps[5 + (m % 2)]
                P.op('pe', [lambda j=j, w=w, po=po: nc.tensor.matmul(po[:, :T], lhsT=w[:, j, :], rhs=act[:, j, :T],
                                                                     start=(j == 0), stop=(j == NJ - 1)) for j in range(NJ)],
                     reads=akeys + [('wdt', m % 2)], writes=[('ps', 5 + (m % 2))])
                P.op('dve', lambda m=m, po=po: nc.vector.scalar_tensor_tensor(
                    out=x_t[:, m, :T], in0=po[:, :T], scalar=hg[:, i3, r, m:m + 1], in1=x_t[:, m, :T], op0=ALU.mult, op1=ALU.add),
                    reads=[('ps', 5 + (m % 2)), ('x', m)], writes=[('x', m)])
                if m + 2 < KC:
                    load_wd(m + 2)

    def outproj(self, t0, T, r):
        nc, P, ps = self.nc, self.P, self.ps
        B, L = self.B, self.Lb
        x_t = self.x_t
        with Scope(P) as S:
            mix = S.tile('mixt', [128, KC, 512], BF16)
            wt = [S.tile('wo', [128, KC, 128], BF16) for _ in range(3)]
            P.dma('sp', mix[:, 0:12, :T], B['mix'][:, :, t0:t0 + T].rearrange('g p t -> p g t'), reads=['mix_scr'], writes=['mixt'])
            P.dma('sp', mix[:, 12:16, :T], B['gm'][:, :, t0:t0 + T].rearrange('g p t -> p g t'), writes=['mixt2'])

            def load_w(m):
                P.dma('sp', wt[m % 3][:].rearrange('p a b -> p (a b)'), B['wout'][m], reads=[('wout_b' + B['sfx'], m)], writes=[('wo', m % 3)])
            load_w(0)
            load_w(1)
            for m in range(KC):
                if m + 2 < KC:
                    load_w(m + 2)
                w = wt[m % 3]
                po = ps[5 + (m % 2)]
                P.op('pe', [lambda kc=kc, w=w, po=po: nc.tensor.matmul(po[:, :T], lhsT=w[:, kc, :], rhs=mix[:, kc, :T],
                                                                       start=(kc == 0), stop=(kc == KC - 1)) for kc in range(KC)],
                     reads=['mixt', 'mixt2', ('wo', m % 3)], writes=[('ps', 5 + (m % 2))])
                P.op('dve', lambda m=m, po=po: nc.vector.scalar_tensor_tensor(
                    out=x_t[:, m, :T], in0=po[:, :T], scalar=L['hg'][:, 1, r, m:m + 1], in1=x_t[:, m, :T], op0=ALU.mult, op1=ALU.add),
                    reads=[('ps', 5 + (m % 2)), ('x', m)], writes=[('x', m)])

    def final_norm(self, t0, T):
        nc, P, ps = self.nc, self.P, self.ps
        x_t = self.x_t
        with Scope(P) as S:
            sq = [S.tile('sq', [128, 512], BF16) for _ in range(2)]
            rs = S.tile('rs', [128, 512], F32)
            rstd = S.tile('rstd', [128, 512], F32)
            for kc in range(KC):
                b = kc % 2
                P.op('act', lambda kc=kc, b=b: nc.scalar.activation(out=sq[b][:, :T], in_=x_t[:, kc, :T], func=AF.Square),
                     reads=[('x', kc)], writes=[('sq', b)])
                P.op('pe', lambda kc=kc, b=b: nc.tensor.matmul(ps[0][:, :T], lhsT=self.ones_bf[:], rhs=sq[b][:, :T],
                                                               start=(kc == 0), stop=(kc == KC - 1)),
                     reads=[('sq', b)], writes=[('ps', 0)])
            P.op('act', lambda: nc.scalar.activation(out=rs[:, :T], in_=ps[0][:, :T], func=AF.Sqrt, scale=1.0 / D, bias=self.eps_t[:, 0:1]),
                 reads=[('ps', 0)], writes=['rs'])
            P.op('dve', lambda: nc.vector.reciprocal(out=rstd[:, :T], in_=rs[:, :T]), reads=['rs'], writes=['rstd'])
            for kc in range(KC):
                P.op('dve', lambda kc=kc: nc.vector.scalar_tensor_tensor(
                    out=x_t[:, kc, :T], in0=x_t[:, kc, :T], scalar=self.fin_g[:, kc:kc + 1], in1=rstd[:, :T], op0=ALU.mult, op1=ALU.mult),
                    reads=[('x', kc), 'rstd', 'fin_g'], writes=[('x', kc)])
            dst = self.y_out.rearrange('(kc p) t -> p kc t', p=128)
            tl = t0 - CTX
            for h in range(2):
                P.dma('sp', dst[:, h * 8:(h + 1) * 8, tl:tl + T], x_t[:, h * 8:(h + 1) * 8, :T],
                      reads=[('x', m) for m in range(h * 8, h * 8 + 8)])

    def inproj(self, t0, T, r, is_ctx):
        nc, P, ps = self.nc, self.P, self.ps
        A, L = self.A, self.La
        nchk = T // 128
        tl = t0 - CTX
        SC = 128.0 ** -0.5
        with Scope(P) as S:
            hT = S.tile('hT', [128, KC, 512], BF16)
            wf = [S.tile('wf', [128, KC, 128], BF16) for _ in range(3)]
            wt = [S.tile('wt', [128, KC, 512], BF16) for _ in range(2)]
            zsq = S.tile('zsq', [128, 512], BF16)
            hrs = S.tile('hrs', [128, 512], F32)
            hrstd = S.tile('hrstd', [128, 512], F32)
            qn = S.tile('qn', [128, 512], F32)
            t1 = S.tile('t1', [128, 512], F32)
            t2 = S.tile('t2', [128, 512], F32)
            qf = [S.tile('qf', [128, 512], BF16) for _ in range(2)]
            gqT = [S.tile('gqT', [128, 512], F32) for _ in range(2)]
            gkT = [S.tile('gkT', [128, 512], F32) for _ in range(2)]
            lr = [S.tile('lr', [16, 512], F32) for _ in range(2)]
            uT = [S.tile('uT', [128, 512], F32) for _ in range(4)]
            grt = [S.tile('grt', [128, 512], BF16) for _ in range(2)]
            if not is_ctx:
                cs = S.tile('cs', [128, 512], F32)
                sn = S.tile('sn', [128, 512], F32)
                P.dma('sp', cs[:, :T], A['cosd'][:, tl:tl + T], writes=['cs'])
                P.dma('sp', sn[:, :T], A['sind'][:, tl:tl + T], writes=['sn'])

            def load_f(g):
                P.dma('sp', wf[g % 3][:].rearrange('p a b -> p (a b)'), A['wfm'][g], reads=[('wfm_a' + A['sfx'], g)], writes=[('wf', g % 3)])

            def load_t(g):
                P.dma('sp', wt[g % 2][:].rearrange('p a b -> p (a b)'), A['wtm'][g], reads=[('wtm_a' + A['sfx'], g)], writes=[('wt', g % 2)])
            load_f(0)
            load_f(1)
            load_t(0)
            load_t(1)
            self.modnorm(S, hT, T, L['gs'][:, 1, r, :], L['sh'][:, 1, r, :])
            hkeys = [('hT', kc) for kc in range(KC)]
            for g in range(NFM):
                if g + 2 < NFM:
                    load_f(g + 2)
                w = wf[g % 3]
                pb = 1 + (g % 2)
                pz = ps[pb]
                if g == 18 and 'glr' in DBG_SKIP:
                    continue
                if g == 18:
                    for d in range(2):
                        pzd = ps[1 + d]
                        P.op('pe', [lambda kc=kc, w=w, pzd=pzd, d=d: nc.tensor.matmul(
                            pzd[0:16, :T], lhsT=w[:, kc, d * 16:(d + 1) * 16], rhs=hT[:, kc, :T],
                            start=(kc == 0), stop=(kc == KC - 1)) for kc in range(KC)],
                            reads=hkeys + [('wf', g % 3)], writes=[('ps', 1 + d)])
                        P.op('act', lambda d=d, pzd=pzd: nc.scalar.copy(out=lr[d][:, :T], in_=pzd[0:16, :T]),
                             reads=[('ps', 1 + d)], writes=[('lr', d)])
                    continue
                P.op('pe', [lambda kc=kc, w=w, pz=pz: nc.tensor.matmul(pz[:, :T], lhsT=w[:, kc, :], rhs=hT[:, kc, :T],
                                                                       start=(kc == 0), stop=(kc == KC - 1)) for kc in range(KC)],
                     reads=hkeys + [('wf', g % 3)], writes=[('ps', pb)])
                if 'fmpost' in DBG_SKIP:
                    continue
                if g < 10:
                    which = 0 if g < 8 else 1
                    P.op('act', lambda pz=pz: nc.scalar.activation(out=zsq[:, :T], in_=pz[:, :T], func=AF.Square),
                         reads=[('ps', pb)], writes=['zsq'])
                    P.op('pe', lambda: nc.tensor.matmul(ps[3][:, :T], lhsT=self.ones_bf[:], rhs=zsq[:, :T], start=True, stop=True),
                         reads=['zsq'], writes=[('ps', 3)])
                    P.op('act', lambda: nc.scalar.activation(out=hrs[:, :T], in_=ps[3][:, :T], func=AF.Sqrt, scale=1.0 / 128,
                                                             bias=self.eps_t[:, 0:1]), reads=[('ps', 3)], writes=['hrs'])
                    P.op('dve', lambda: nc.vector.reciprocal(out=hrstd[:, :T], in_=hrs[:, :T]), reads=['hrs'], writes=['hrstd'])
                    P.op('dve', lambda pz=pz, which=which: nc.vector.scalar_tensor_tensor(
                        out=qn[:, :T], in0=pz[:, :T], scalar=A['qkg'][:, which:which + 1], in1=hrstd[:, :T], op0=ALU.mult, op1=ALU.mult),
                        reads=[('ps', pb), 'hrstd'], writes=['qn'])
                    q_o = qf[g % 2]
                    if is_ctx:
                        P.op('pool', lambda q_o=q_o: nc.gpsimd.tensor_copy(out=q_o[:, :T], in_=qn[:, :T]), reads=['qn'], writes=[('qf', g % 2)])
                    else:
                        P.op('pe', lambda: nc.tensor.matmul(ps[4][:, :T], lhsT=self.pm[:], rhs=qn[:, :T], start=True, stop=True),
                             reads=['qn'], writes=[('ps', 4)])
                        P.op('pool', lambda: nc.gpsimd.tensor_tensor(out=t1[:, :T], in0=qn[:, :T], in1=cs[:, :T], op=ALU.mult),
                             reads=['qn', 'cs'], writes=['t1'])
                        P.op('dve', lambda: nc.vector.tensor_tensor(out=t2[:, :T], in0=ps[4][:, :T], in1=sn[:, :T], op=ALU.mult),
                             reads=[('ps', 4), 'sn'], writes=['t2'])
                        P.op('pool', lambda q_o=q_o: nc.gpsimd.tensor_tensor(out=q_o[:, :T], in0=t1[:, :T], in1=t2[:, :T], op=ALU.add),
                             reads=['t1', 't2'], writes=[('qf', g % 2)])
                    dst = A['qT'][g, :, t0:t0 + T] if g < 8 else A['kT'][g - 8, :, t0:t0 + T]
                    P.dma('sp', dst, q_o[:, :T], reads=[('qf', g % 2)])
                elif g < 12:
                    P.op('act', lambda pz=pz, g=g: nc.scalar.copy(out=gqT[g - 10][:, :T], in_=pz[:, :T]), reads=[('ps', pb)], writes=[('gqT', g - 10)])
                elif g < 14:
                    P.op('act', lambda pz=pz, g=g: nc.scalar.copy(out=gkT[g - 12][:, :T], in_=pz[:, :T]), reads=[('ps', pb)], writes=[('gkT', g - 12)])
                elif g < 18:
                    go = grt[g % 2]
                    P.op('act', lambda pz=pz, go=go: nc.scalar.activation(out=go[:, :T], in_=pz[:, :T], func=AF.Silu),
                         reads=[('ps', pb)], writes=[('grt', g % 2)])
                    P.dma('sp', A['grs'][g - 14, :, t0:t0 + T], go[:, :T], reads=[('grt', g % 2)])
                else:
                    P.op('act', lambda pz=pz, g=g: nc.scalar.copy(out=uT[g - 19][:, :T], in_=pz[:, :T]), reads=[('ps', pb)], writes=[('uT', g - 19)])
            vt = [S.tile('vt', [128, 256], BF16) for _ in range(2)]
            gktm = [S.tile('gktm', [128, 256], F32) for _ in range(4)]
            gvt = [S.tile('gvt', [128, 512], BF16) for _ in range(2)]
            vn = S.tile('vn', [128, 512], BF16)
            junk = S.tile('junk', [128, 512], BF16)
            ssq = S.tile('ssq', [128, 1], F32)
            srs = S.tile('srs', [128, 1], F32)
            srstd = S.tile('srstd', [128, 1], F32)
            tz = S.tile('tz', [128, 128], F32)
            gmo = S.tile('gmo', [128, 4, 512], BF16)
            ee = S.tile('ee', [128, 256], F32)
            sp = S.tile('sp', [128, 256], F32)
            E1 = S.tile('E1', [128, 2, 128], F32)
            E2 = S.tile('E2', [128, 2, 128], F32)
            E3 = S.tile('E3', [128, 256], F32)
            gqo = [S.tile('gqo', [128, 2, 512], BF16) for _ in range(2)]
            gko = [S.tile('gko', [128, 2, 512], BF16) for _ in range(2)]
            kht = [S.tile('kht', [128, 256], BF16) for _ in range(2)]
            for gt in range(NTM):
                if 'tm' in DBG_SKIP:
                    break
                if gt == 2 and 'gmlp' in DBG_SKIP:
                    break
                w = wt[gt % 2]
                for c in range(nchk):
                    cc = slice(c * 128, (c + 1) * 128)
                    pb = 5 + (c % 2)
                    pz = ps[pb]
                    P.op('pe', [lambda kc=kc, w=w, pz=pz, cc=cc: nc.tensor.matmul(pz[:, :], lhsT=hT[:, kc, cc], rhs=w[:, kc, :],
                                                                                  start=(kc == 0), stop=(kc == KC - 1)) for kc in range(KC)],
                         reads=hkeys + [('wt', gt % 2)], writes=[('ps', pb)])
                    tok = slice(t0 + c * 128, t0 + (c + 1) * 128)
                    if 'tmpost' in DBG_SKIP or ('tm%dpost' % gt) in DBG_SKIP:
                        continue
                    if gt == 0:
                        v_o = vt[c % 2]
                        if 'tm0a' not in DBG_SKIP:
                            P.op('act', lambda pz=pz, v_o=v_o: nc.scalar.copy(out=v_o[:], in_=pz[:, 0:256]), reads=[('ps', pb)], writes=[('vt', c % 2)])
                            P.dma('sp', A['v'][tok, :], v_o[:], reads=[('vt', c % 2)])
                        if 'tm0b' not in DBG_SKIP:
                            P.op('act', lambda pz=pz, c=c: nc.scalar.copy(out=gktm[c][:], in_=pz[:, 256:512]), reads=[('ps', pb)], writes=[('gktm', c)])
                    elif gt == 1:
                        g_o = gvt[c % 2]
                        P.op('act', lambda pz=pz, g_o=g_o: nc.scalar.copy(out=g_o[:], in_=pz[:, :]), reads=[('ps', pb)], writes=[('gvt', c % 2)])
                        P.dma('sp', A['gv'][tok, :], g_o[:], reads=[('gvt', c % 2)])
                    else:
                        P.op('act', lambda pz=pz: nc.scalar.activation(out=junk[:], in_=pz[:, :], func=AF.Square, accum_out=ssq[:]),
                             reads=[('ps', pb)], writes=['junk', 'ssq'])
                        P.op('act', lambda: nc.scalar.activation(out=srs[:], in_=ssq[:], func=AF.Sqrt, scale=1.0 / 512, bias=self.eps_t[:, 0:1]),
                             reads=['ssq'], writes=['srs'])
                        P.op('dve', lambda: nc.vector.reciprocal(out=srstd[:], in_=srs[:]), reads=['srs'], writes=['srstd'])
                        P.op('dve', lambda pz=pz: nc.vector.tensor_scalar(out=vn[:], in0=pz[:, :], scalar1=srstd[:, 0:1], scalar2=None, op0=ALU.mult),
                             reads=[('ps', pb), 'srstd'], writes=['vn'])
                        P.op('pe', [lambda gi=gi: nc.tensor.matmul(ps[7][:, gi * 128:(gi + 1) * 128], lhsT=vn[:, gi * 128:(gi + 1) * 128],
                                                                   rhs=A['wsT'][:, gi, :], start=True, stop=True) for gi in range(4)],
                             reads=['vn'], writes=[('ps', 7)])
                        for gi in range(4):
                            P.op('dve', lambda gi=gi: nc.vector.scalar_tensor_tensor(
                                out=tz[:], in0=ps[7][:, gi * 128:(gi + 1) * 128], scalar=A['gmg'][:, gi:gi + 1], in1=A['bsb'][:, gi, :],
                                op0=ALU.mult, op1=ALU.add), reads=[('ps', 7)], writes=['tz'])
                            P.op('pool', lambda gi=gi, cc=cc: nc.gpsimd.tensor_tensor(out=gmo[:, gi, cc], in0=tz[:], in1=uT[gi][:, cc], op=ALU.mult),
                                 reads=['tz', ('uT', gi)], writes=['gmo'])
                if gt + 2 < NTM:
                    load_t(gt + 2)
            if 'gmlp' not in DBG_SKIP and 'tm' not in DBG_SKIP:
                P.dma('sp', A['gm'][:, :, t0:t0 + T].rearrange('g p t -> p g t'), gmo[:, :, :T], reads=['gmo'])
            for c in range(nchk):
                if 'gla' in DBG_SKIP:
                    break
                cc = slice(c * 128, (c + 1) * 128)
                cg = (t0 // 128) + c
                tok = slice(t0 + c * 128, t0 + (c + 1) * 128)
                for d in range(2):
                    P.op('pe', [lambda d=d, cc=cc: nc.tensor.matmul(ps[1][:, 0:256], lhsT=lr[d][0:16, cc], rhs=A['gatew'][0:16, d, :], start=True, stop=False),
                                lambda d=d: nc.tensor.matmul(ps[1][:, 0:256], lhsT=self.ones_f[0:1, 0:128], rhs=A['gateb'][0:1, d, :], start=False, stop=True)],
                         reads=[('lr', d)], writes=[('ps', 1)])
                    P.op('act', lambda: nc.scalar.activation(out=ee[:], in_=ps[1][:, 0:256], func=AF.Exp, scale=-1.0), reads=[('ps', 1)], writes=['ee'])
                    P.op('act', lambda: nc.scalar.activation(out=sp[:], in_=ee[:], func=AF.Ln, bias=self.one_t[:, 0:1], scale=1.0), reads=['ee'], writes=['sp'])
                    P.op('pe', [lambda half=half, d=d: nc.tensor.matmul(ps[2][:, half * 128:(half + 1) * 128], lhsT=sp[:, half * 128:(half + 1) * 128],
                                                                        rhs=self.tri[:, d, :], start=True, stop=True) for half in range(2)],
                         reads=['sp'], writes=[('ps', 2)])
                    P.op('pe', lambda d=d: nc.tensor.matmul(ps[3][:, 0:256], lhsT=self.tri[:, 2 + d, :], rhs=sp[:], start=True, stop=True),
                         reads=['sp'], writes=[('ps', 3)])
                    P.op('act', lambda: nc.scalar.activation(out=E1[:].rearrange('p a b -> p (a b)'), in_=ps[2][:, 0:256], func=AF.Exp, scale=-1.0 / 16),
                         reads=[('ps', 2)], writes=['E1'])
                    P.op('act', lambda: nc.scalar.activation(out=E2[:].rearrange('p a b -> p (a b)'), in_=ps[2][:, 0:256], func=AF.Exp, scale=1.0 / 16),
                         reads=[('ps', 2)], writes=['E2'])
                    P.op('act', lambda: nc.scalar.activation(out=E3[:], in_=ps[3][:, 0:256], func=AF.Exp, scale=-1.0 / 16),
                         reads=[('ps', 3)], writes=['E3'])
                    for half in range(2):
                        P.op('dve', lambda half=half, d=d, cc=cc: nc.vector.scalar_tensor_tensor(
                            out=gqo[d][:, half, cc], in0=gqT[half][:, cc], scalar=0.125, in1=E1[:, half, :], op0=ALU.mult, op1=ALU.mult),
                            reads=[('gqT', half), 'E1'], writes=[('gqo', d)])
                        P.op('pool', lambda half=half, d=d, cc=cc: nc.gpsimd.tensor_tensor(
                            out=gko[d][:, half, cc], in0=gkT[half][:, cc], in1=E2[:, half, :], op=ALU.mult),
                            reads=[('gkT', half), 'E2'], writes=[('gko', d)])
                    k_o = kht[d]
                    P.op('dve', lambda c=c, k_o=k_o: nc.vector.tensor_tensor(out=k_o[:], in0=gktm[c][:], in1=E3[:], op=ALU.mult),
                         reads=[('gktm', c), 'E3'], writes=[('kht', d)])
                    P.dma('sp', A['kh'][d, tok, :], k_o[:], reads=[('kht', d)])
                    col = 127 if d == 0 else 0
                    P.op('dve', lambda d=d, cg=cg, col=col: nc.vector.tensor_copy(out=A['eb'][:, d, cg, :], in_=E1[:, :, col]),
                         reads=['E1'], writes=['eb'])
            for d in range(2):
                if 'gla' in DBG_SKIP:
                    break
                P.dma('sp', A['gq'][d, :, :, t0:t0 + T].rearrange('h p t -> p h t'), gqo[d][:, :, :T], reads=[('gqo', d)])
                P.dma('sp', A['gk'][d, :, :, t0:t0 + T].rearrange('h p t -> p h t'), gko[d][:, :, :T], reads=[('gko', d)])

    def gla_scan(self):
        nc, P, ps = self.nc, self.P, self.ps
        A = self.A
        NCH = self.NCH
        with Scope(P) as S:
            Sst = [S.tile('Sst', [128, 2, 128], F32) for _ in range(2)]
            Drun = [S.tile('Drun', [128, 2], F32) for _ in range(2)]
            dcum = S.tile('dcum', [128, 2, NCH, 2], F32)
            gsum = S.tile('gsum', [128, 2, 2, 2, 129], F32)
            stage = [S.tile('stage', [128, 256], F32) for _ in range(4)]
            kht = [S.tile('skh', [128, 256], BF16) for _ in range(4)]
            gvt = [S.tile('sgv', [128, 512], BF16) for _ in range(4)]
            it = 0
            for kind in range(2):
                lo, hi = (0, 2) if kind == 0 else (2, NCH)
                orders = [list(range(lo, hi)), list(range(hi - 1, lo - 1, -1))]
                for d in range(2):
                    P.op('dve', lambda d=d: nc.vector.memset(Sst[d][:], 0.0), writes=[('Sst', d)])
                    P.op('dve', lambda d=d: nc.vector.memset(Drun[d][:], 1.0), writes=[('Drun', d)])
                for step in range(hi - lo):
                    for d in range(2):
                        c = orders[d][step]
                        tok = slice(c * 128, (c + 1) * 128)
                        b = it % 4
                        it += 1
                        P.dma('sp', kht[b][:], A['kh'][d, tok, :], writes=[('skh', b)])
                        P.dma('sp', gvt[b][:], A['gv'][tok, :], writes=[('sgv', b)])
                        P.op('act', lambda d=d, b=b: nc.scalar.copy(out=stage[b][:], in_=Sst[d][:].rearrange('p a b -> p (a b)')),
                             reads=[('Sst', d)], writes=[('stage', b)])
                        P.dma('sp', A['S0'][d, c], stage[b][:], reads=[('stage', b)])
                        P.op('dve', lambda d=d, c=c: nc.vector.tensor_copy(out=dcum[:, d, c, :], in_=Drun[d][:]), reads=[('Drun', d)], writes=['dcum'])
                        pb = 1 + d
                        P.op('pe', [lambda h=h, b=b, pb=pb: nc.tensor.matmul(
                            ps[pb][(h % 2) * 64:(h % 2) * 64 + 64, (h // 2) * 128:(h // 2) * 128 + 128],
                            lhsT=kht[b][:, h * 64:(h + 1) * 64], rhs=gvt[b][:, h * 128:(h + 1) * 128], start=True, stop=True) for h in range(4)],
                            reads=[('skh', b), ('sgv', b)], writes=[('ps', pb)])
                        for half in range(2):
                            P.op('dve', lambda d=d, c=c, half=half, pb=pb: nc.vector.scalar_tensor_tensor(
                                out=Sst[d][:, half, :], in0=Sst[d][:, half, :], scalar=A['eb'][:, d, c, half:half + 1],
                                in1=ps[pb][:, half * 128:(half + 1) * 128], op0=ALU.mult, op1=ALU.add),
                                reads=[('ps', pb), ('Sst', d)], writes=[('Sst', d)])
                        P.op('dve', lambda d=d, c=c: nc.vector.tensor_tensor(out=Drun[d][:], in0=Drun[d][:], in1=A['eb'][:, d, c, :], op=ALU.mult),
                             reads=[('Drun', d)], writes=[('Drun', d)])
                for d in range(2):
                    P.op('dve', lambda d=d, kind=kind: nc.vector.tensor_copy(out=gsum[:, kind, d, :, 0:128], in_=Sst[d][:]),
                         reads=[('Sst', d)], writes=['gsum'])
                    P.op('dve', lambda d=d, kind=kind: nc.vector.tensor_copy(out=gsum[:, kind, d, :, 128], in_=Drun[d][:]),
                         reads=[('Drun', d)], writes=['gsum'])
            P.dma('sp', A['glasum'][:, :], gsum[:].rearrange('p a b c e -> p (a b c e)'), reads=['gsum'])
            P.dma('sp', A['dcum'][:, :], dcum[:].rearrange('p a b c -> p (a b c)'), reads=['dcum'])

    def attention(self):
        nc, P, ps = self.nc, self.P, self.ps
        B = self.B
        nkeys = self.nkeys
        nkc = nkeys // 128
        SC = 128.0 ** -0.5
        with Scope(P) as S:
            KT = S.tile('KT', [128, 2, nkeys], BF16)
            V = S.tile('V', [128, nkc, 256], BF16)
            step = 2048
            for g in range(2):
                for a in range(0, nkeys, step):
                    b_ = min(nkeys, a + step)
                    P.dma('sp', KT[:, g, a:b_], B['kT'][g, :, a:b_], writes=['KT'])
            vsrc = B['v'].rearrange('(kc p) c -> p kc c', p=128)
            for a in range(0, nkc, 16):
                b_ = min(nkc, a + 16)
                P.dma('sp', V[:, a:b_, :], vsrc[:, a:b_, :], writes=['V'])
            q4 = [S.tile('q4', [128, 4, 128], BF16) for _ in range(2)]
            pT = [S.tile('pT', [128, 512], BF16) for _ in range(3)]
            rl = S.tile('rl', [128, 512], F32)
            oT = [S.tile('oT', [128, 512], BF16) for _ in range(2)]
            acc = [S.tile('acc', [128, 512], F32) for _ in range(2)]
            it = 0
            for qt in range(self.NCH):
                is_ctx = qt < 2
                if is_ctx and not self.ctx_B:
                    continue
                kcs = [0, 1] if is_ctx else list(range(nkc))
                tok = slice(qt * 128, (qt + 1) * 128)
                for g in range(2):
                    b = it % 2
                    it += 1
                    q = q4[b]
                    P.dma('sp', q[:], B['qT'][g * 4:(g + 1) * 4, :, tok].rearrange('h p t -> p h t'), writes=[('q4', b)])
                    qr = q[:].rearrange('p h t -> p (h t)')
                    po, pl = ps[2 + b], ps[4 + b]
                    n = len(kcs)

                    def emit_s(ki):
                        kc = kcs[ki]
                        P.op('pe', lambda kc=kc, ki=ki: nc.tensor.matmul(ps[ki % 2][:, :], lhsT=KT[:, g, kc * 128:(kc + 1) * 128], rhs=qr, start=True, stop=True),
                             reads=['KT', ('q4', b)], writes=[('ps', ki % 2)])
                    emit_s(0)
                    for ki in range(n):
                        kc = kcs[ki]
                        p_ = pT[ki % 3]
                        P.op('act', lambda ki=ki, p_=p_: nc.scalar.activation(out=p_[:], in_=ps[ki % 2][:, :], func=AF.Exp, scale=SC),
                             reads=[('ps', ki % 2)], writes=[('pT', ki % 3)])
                        if ki + 1 < n:
                            emit_s(ki + 1)
                        P.op('pe', lambda kc=kc, ki=ki, p_=p_: nc.tensor.matmul(po[:, :], lhsT=V[:, kc, g * 128:(g + 1) * 128], rhs=p_[:], start=(ki == 0), stop=(ki == n - 1)),
                             reads=['V', ('pT', ki % 3)], writes=[('ps', 2 + b)])
                        if ki == 0:
                            P.op('dve', lambda p_=p_: nc.vector.tensor_copy(out=acc[b][:], in_=p_[:]), reads=[('pT', ki % 3)], writes=[('acc', b)])
                        else:
                            P.op('dve', lambda p_=p_: nc.vector.tensor_tensor(out=acc[b][:], in0=acc[b][:], in1=p_[:], op=ALU.add),
                                 reads=[('pT', ki % 3), ('acc', b)], writes=[('acc', b)])
                    P.op('pe', lambda pl=pl: nc.tensor.matmul(pl[:, :], lhsT=self.ones_f[:], rhs=acc[b][:], start=True, stop=True),
                         reads=[('acc', b)], writes=[('ps', 4 + b)])
                    P.op('dve', lambda pl=pl: nc.vector.reciprocal(out=rl[:], in_=pl[:, :]), reads=[('ps', 4 + b)], writes=['rl'])
                    o_ = oT[b]
                    P.op('dve', lambda po=po, o_=o_: nc.vector.tensor_tensor(out=o_[:], in0=po[:, :], in1=rl[:], op=ALU.mult),
                         reads=[('ps', 2 + b), 'rl'], writes=[('oT', b)])
                    P.dma('sp', B['mix'][g * 4:(g + 1) * 4, :, tok].rearrange('h p t -> p h t'), o_[:].rearrange('p (h t) -> p h t', h=4),
                          reads=[('oT', b)], writes=['mix_scr'])

    def gla_output(self):
        nc, P, ps = self.nc, self.P, self.ps
        B = self.B
        NCH = self.NCH
        with Scope(P) as S:
            Sst = S.tile('Sst', [128, 2, 2, 128], F32)
            pD = S.tile('pD', [128, 2, 3, 2], F32)
            pB = S.tile('pB', [128, 2, 3, 2, 128], F32)
            dcum = S.tile('dcum', [128, 2, NCH, 2], F32)
            if self.fused:
                gsv = B['glasum'].rearrange('p (k d h e) -> p k d h e', k=2, d=2, h=2)
                for d in range(2):
                    P.dma('sp', Sst[:, d, :, :], gsv[:, 0, d, :, 0:128], writes=['Sst'])
            else:
                P.dma('sp', Sst[:].rearrange('p a b c -> p (a b c)'), B['ctxS'][:, :], writes=['Sst'])
                P.dma('sp', pD[:].rearrange('p a b c -> p (a b c)'), B['predD'][:, :], writes=['pD'])
                P.dma('sp', pB[:].rearrange('p a b c e -> p (a b c e)'), B['predB'][:, :], writes=['pB'])
            P.dma('sp', dcum[:].rearrange('p a b c -> p (a b c)'), B['dcum'][:, :], writes=['dcum'])
            for d in range(2):
                if self.fused:
                    break
                for slot in range(3):
                    for half in range(2):
                        P.op('dve', lambda d=d, slot=slot, half=half: nc.vector.scalar_tensor_tensor(
                            out=Sst[:, d, half, :], in0=Sst[:, d, half, :], scalar=pD[:, d, slot, half:half + 1], in1=pB[:, d, slot, half, :],
                            op0=ALU.mult, op1=ALU.add), reads=['Sst', 'pD', 'pB'], writes=['Sst'])
            NB = 2
            gq = [S.tile('gq', [128, 2, 2, 128], BF16) for _ in range(NB)]
            gk = [S.tile('gk', [128, 2, 2, 128], BF16) for _ in range(NB)]
            gv = [S.tile('gv', [128, 512], BF16) for _ in range(NB)]
            S0 = [S.tile('S0', [128, 2, 256], F32) for _ in range(NB)]
            grs = [S.tile('grs', [128, 4, 128], BF16) for _ in range(NB)]
            Sc = [S.tile('Sc', [128, 2, 2, 128], BF16) for _ in range(NB)]
            Am = [S.tile('Am', [128, 128], BF16) for _ in range(4)]
            osq = S.tile('osq', [128, 128], BF16)
            ors = S.tile('ors', [128, 128], F32)
            orstd = S.tile('orstd', [128, 128], F32)
            on = S.tile('on', [128, 128], F32)
            og = [S.tile('og', [128, 4, 128], BF16) for _ in range(NB)]
            ai = 0
            for ci, c in enumerate(range(NCH)):
                is_ctx = c < 2
                if is_ctx and not self.ctx_B:
                    continue
                b = ci % NB
                tok = slice(c * 128, (c + 1) * 128)
                P.dma('sp', gq[b][:], B['gq'][:, :, :, tok].rearrange('d h p t -> p d h t'), writes=[('gq', b)])
                P.dma('sp', gk[b][:], B['gk'][:, :, :, tok].rearrange('d h p t -> p d h t'), writes=[('gk', b)])
                P.dma('sp', gv[b][:], B['gv'][tok, :], writes=[('gv', b)])
                P.dma('sp', S0[b][:], B['S0'][:, c].rearrange('d p f -> p d f'), writes=[('S0', b)])
                P.dma('sp', grs[b][:], B['grs'][:, :, tok].rearrange('g p t -> p g t'), writes=[('grs', b)])
                for d in range(2):
                    for half in range(2):
                        if is_ctx:
                            P.op('dve', lambda d=d, half=half, b=b: nc.vector.tensor_copy(out=Sc[b][:, d, half, :], in_=S0[b][:, d, half * 128:(half + 1) * 128]),
                                 reads=[('S0', b)], writes=[('Sc', b)])
                        else:
                            P.op('dve', lambda d=d, half=half, b=b, c=c: nc.vector.scalar_tensor_tensor(
                                out=Sc[b][:, d, half, :], in0=Sst[:, d, half, :], scalar=dcum[:, d, c, half:half + 1],
                                in1=S0[b][:, d, half * 128:(half + 1) * 128], op0=ALU.mult, op1=ALU.add),
                                reads=[('S0', b), 'Sst', 'dcum'], writes=[('Sc', b)])
                for h in range(4):
                    half = h // 2
                    hs = slice((h % 2) * 64, (h % 2) * 64 + 64)
                    ams = []
                    for d in range(2):
                        pa = ps[d]
                        P.op('pe', lambda d=d, b=b, half=half, hs=hs, pa=pa: nc.tensor.matmul(
                            pa[:, 0:128], lhsT=gk[b][hs, d, half, :], rhs=gq[b][hs, d, half, :], start=True, stop=True),
                            reads=[('gq', b), ('gk', b)], writes=[('ps', d)])
                        a_ = Am[ai % 4]
                        ams.append((a_, ai % 4))
                        P.op('dve', lambda d=d, pa=pa, a_=a_: nc.vector.tensor_tensor(out=a_[:], in0=pa[:, 0:128], in1=self.tri[:, d, :], op=ALU.mult),
                             reads=[('ps', d)], writes=[('Am', ai % 4)])
                        ai += 1
                    po = ps[2 + (h % 2)]
                    fl = []
                    for d in range(2):
                        a_, _ = ams[d]
                        fl.append(lambda d=d, a_=a_, b=b, h=h, po=po: nc.tensor.matmul(po[:, 0:128], lhsT=gv[b][:, h * 128:(h + 1) * 128], rhs=a_[:],
                                                                                       start=(d == 0), stop=False))
                        fl.append(lambda d=d, b=b, half=half, hs=hs, po=po: nc.tensor.matmul(po[:, 0:128], lhsT=Sc[b][hs, d, half, :], rhs=gq[b][hs, d, half, :],
                                                                                             start=False, stop=(d == 1)))
                    P.op('pe', fl, reads=[('gv', b), ('Sc', b), ('gq', b)] + [('Am', k) for _, k in ams], writes=[('ps', 2 + (h % 2))])
                    P.op('act', lambda po=po: nc.scalar.activation(out=osq[:], in_=po[:, 0:128], func=AF.Square), reads=[('ps', 2 + (h % 2))], writes=['osq'])
                    P.op('pe', lambda: nc.tensor.matmul(ps[4][:, 0:128], lhsT=self.ones_bf[:], rhs=osq[:], start=True, stop=True), reads=['osq'], writes=[('ps', 4)])
                    P.op('act', lambda: nc.scalar.activation(out=ors[:], in_=ps[4][:, 0:128], func=AF.Sqrt, scale=1.0 / 128, bias=self.eps_t[:, 0:1]),
                         reads=[('ps', 4)], writes=['ors'])
                    P.op('dve', lambda: nc.vector.reciprocal(out=orstd[:], in_=ors[:]), reads=['ors'], writes=['orstd'])
                    P.op('dve', lambda po=po, h=h: nc.vector.scalar_tensor_tensor(out=on[:], in0=po[:, 0:128], scalar=B['glag'][:, h:h + 1], in1=orstd[:],
                                                                                  op0=ALU.mult, op1=ALU.mult), reads=[('ps', 2 + (h % 2)), 'orstd'], writes=['on'])
                    P.op('pool', lambda h=h, b=b: nc.gpsimd.tensor_tensor(out=og[b][:, h, :], in0=on[:], in1=grs[b][:, h, :], op=ALU.mult),
                         reads=['on', ('grs', b)], writes=[('og', b)])
                P.dma('sp', B['mix'][8:12, :, tok].rearrange('g p t -> p g t'), og[b][:], reads=[('og', b)], writes=['mix_scr'])


def build_mod(nchunks, R):
    nc = bass.Bass("TRN2", target_bir_lowering=False)
    ngrp = nchunks // 6
    cs = nc.dram_tensor('cs', [128, KC * R], F32, kind="ExternalInput").ap()
    wmod = nc.dram_tensor('wmod', [2 * ngrp, 128, KC * 768], F32, kind="ExternalInput").ap()
    modb = nc.dram_tensor('modb', [128, 2 * nchunks], F32, kind="ExternalInput").ap()
    modo = nc.dram_tensor('modo', [128, 2 * nchunks * R], F32, kind="ExternalOutput").ap()
    with ExitStack() as es:
        P = Prog(nc, es)
        ps = [es.enter_context(nc.psum_tensor('ps%d' % i, [128, 512], F32)) for i in range(2)]
        with Scope(P) as S:
            cst = S.tile('cst', [128, KC, R], F32)
            scs = S.tile('scs', [128, KC, R], F32)
            mb = S.tile('mb', [128, 2, nchunks], F32)
            mo = S.tile('mo', [128, 2, nchunks, R], F32)
            wt = [S.tile('wt', [128, KC, 768], F32) for _ in range(2)]
            P.dma('sp', cst[:].rearrange('p a b -> p (a b)'), cs[:, :], writes=['cst'])
            P.dma('sp', mb[:].rearrange('p a b -> p (a b)'), modb[:, :], writes=['mb'])
            P.op('act', lambda: nc.scalar.activation(out=scs[:].rearrange('p a b -> p (a b)'), in_=cst[:].rearrange('p a b -> p (a b)'), func=AF.Silu),
                 reads=['cst'], writes=['scs'])
            k = 0
            for l in range(2):
                for grp in range(ngrp):
                    w = wt[k % 2]
                    P.dma('sp', w[:].rearrange('p a b -> p (a b)'), wmod[l * ngrp + grp], writes=[('wt', k % 2)])
                    for n in range(6):
                        nn = grp * 6 + n
                        pb = nn % 2
                        P.op('pe', [lambda kc=kc, w=w, n=n, pb=pb: nc.tensor.matmul(ps[pb][:, 0:R], lhsT=w[:, kc, n * 128:(n + 1) * 128], rhs=scs[:, kc, :],
                                                                                    start=(kc == 0), stop=(kc == KC - 1)) for kc in range(KC)],
                             reads=['scs', ('wt', k % 2)], writes=[('ps', pb)])
                        P.op('dve', lambda l=l, nn=nn, pb=pb: nc.vector.tensor_scalar(out=mo[:, l, nn, :], in0=ps[pb][:, 0:R], scalar1=mb[:, l, nn:nn + 1],
                                                                                      scalar2=None, op0=ALU.add), reads=[('ps', pb), 'mb'], writes=['mo'])
                    k += 1
            P.dma('sp', modo[:, :], mo[:].rearrange('p a b c -> p (a b c)'), reads=['mo'])
    return nc


class FusedBuilder(Builder):
    def __init__(self, nlat):
        super().__init__(nlat, CTX + nlat, do_B=False, do_A=False, final=False)
        self.fused = True

    def mod_phase(self):
        nc, P, ps = self.nc, self.P, self.ps
        R, nchunks, ngrp = 2, 144, 24
        cs = self.din('cs', [128, KC * R])
        wmod = self.din('wmod', [2 * ngrp, 128, KC * 768])
        modb = self.din('modb', [128, 2 * nchunks])
        self.modall = self.G.tile('modall', [128, 2, nchunks, R], F32)
        mo = self.modall
        with Scope(P) as S:
            cst = S.tile('cst', [128, KC, R], F32)
            scs = S.tile('scs', [128, KC, R], F32)
            mb = S.tile('mb', [128, 2, nchunks], F32)
            wt = [S.tile('wt', [128, KC, 768], F32) for _ in range(2)]
            P.dma('sp', cst[:].rearrange('p a b -> p (a b)'), cs[:, :], writes=['cst'])
            P.dma('sp', mb[:].rearrange('p a b -> p (a b)'), modb[:, :], writes=['mb'])
            P.op('act', lambda: nc.scalar.activation(out=scs[:].rearrange('p a b -> p (a b)'), in_=cst[:].rearrange('p a b -> p (a b)'), func=AF.Silu),
                 reads=['cst'], writes=['scs'])
            k = 0
            for l in range(2):
                for grp in range(ngrp):
                    w = wt[k % 2]
                    P.dma('sp', w[:].rearrange('p a b -> p (a b)'), wmod[l * ngrp + grp], writes=[('wt', k % 2)])
                    for n in range(6):
                        nn = grp * 6 + n
                        pb = nn % 2
                        P.op('pe', [lambda kc=kc, w=w, n=n, pb=pb: nc.tensor.matmul(ps[pb][:, 0:R], lhsT=w[:, kc, n * 128:(n + 1) * 128], rhs=scs[:, kc, :],
                                                                                    start=(kc == 0), stop=(kc == KC - 1)) for kc in range(KC)],
                             reads=['scs', ('wt', k % 2)], writes=[('ps', pb)])
                        P.op('dve', lambda l=l, nn=nn, pb=pb: nc.vector.tensor_scalar(out=mo[:, l, nn, :], in0=ps[pb][:, 0:R], scalar1=mb[:, l, nn:nn + 1],
                                                                                      scalar2=None, op0=ALU.add), reads=[('ps', pb), 'mb'], writes=['mo'])
                    k += 1

    def build(self):
        nc = self.nc
        NT = self.NT
        with ExitStack() as es:
            P = Prog(nc, es)
            self.P = P
            self.ps = [es.enter_context(nc.psum_tensor('ps%d' % i, [128, 512], F32)) for i in range(8)]
            G0 = Scope(P)
            G0.__enter__()
            self.G = G0
            self.prologue()
            x_in = self.din('xT_in', [D, NT])
            xs = [self.dscr('xs0', [D, NT], F32), self.dscr('xs1', [D, NT], F32)]
            self.mod_phase()
            stages = [dict(do_B=False, do_A=True, final=False, ctx_B=True, lb=None, la=0, src=x_in, dst=xs[0]),
                      dict(do_B=True, do_A=True, final=False, ctx_B=True, lb=0, la=1, src=xs[0], dst=xs[1]),
                      dict(do_B=True, do_A=False, final=True, ctx_B=False, lb=1, la=None, src=xs[1], dst=None)]
            self.prevA = None
            for si, st in enumerate(stages):
                self.do_B, self.do_A, self.final, self.ctx_B = st['do_B'], st['do_A'], st['final'], st['ctx_B']
                self.lb, self.la = st['lb'], st['la']
                self.sfx = '_s%d' % si
                self.xT_in, self.xT_out = st['src'], st['dst']
                G = Scope(P)
                G.__enter__()
                self.G = G
                self.stage_setup()
                P.barrier()
                self.run_stage()
                if self.do_A:
                    self.prevA = self.A
                G.__exit__(None, None, None)
            P.barrier(full=True)
            G0.__exit__(None, None, None)
        return nc


def run_fused(inp):
    x = np.asarray(inp['x'], dtype=np.float32)
    ctx = np.asarray(inp['ctx'], dtype=np.float32)
    c = np.asarray(inp['c'], dtype=np.float32)
    c_ctx = np.asarray(inp['c_ctx'], dtype=np.float32)
    Bsz, Lq, _ = x.shape
    fb = FusedBuilder(Lq)
    nc = fb.build()
    consts = make_consts(Lq, 0)
    LW = [_layer_params(inp, l) for l in range(2)]
    mod_b = np.asarray(inp['mod_b'], dtype=np.float32)
    wl = []
    for l in range(2):
        blk = np.asarray(inp['mod_w'][l], dtype=np.float32).reshape(KC, 128, 24, 768).transpose(2, 1, 0, 3)
        wl.append(_c(blk).reshape(24, 128, KC * 768))
    wmod = np.concatenate(wl, 0)
    modb = _c(np.stack([mod_b[l].reshape(144, 128).T for l in range(2)], 1)).reshape(128, 2 * 144)
    fg = _c(np.asarray(inp['final_norm_g'], dtype=np.float32).reshape(KC, 128).T)

    def a_in(l, sfx):
        W = LW[l]
        return {'normg_a' + sfx: W['normg'], 'wgu_a' + sfx: W['wgu1'], 'wd_a' + sfx: W['wd1'], 'wfm_a' + sfx: W['wfm'], 'wtm_a' + sfx: W['wtm'],
                'p_qkg' + sfx: W['qkg'], 'p_gatew' + sfx: W['gatew'], 'p_gateb' + sfx: W['gateb'], 'p_wsT_f' + sfx: W['wsT_f'],
                'p_bsb' + sfx: W['bsb'], 'p_gmg' + sfx: W['gmg'], 'p_cos' + sfx: consts['cos'], 'p_sin' + sfx: consts['sin']}

    def b_in(l, sfx):
        W = LW[l]
        return {'normg_b' + sfx: W['normg'], 'wout_b' + sfx: W['wout'], 'wgu_b' + sfx: W['wgu2'], 'wd_b' + sfx: W['wd2'], 'p_glag' + sfx: W['glag']}
    maps = []
    for b in range(Bsz):
        crow = np.stack([c[b], c_ctx], 0)
        m = {'c_ones': consts['ones'], 'c_tri': consts['tri'], 'c_pm': consts['pm'],
             'cs': _c(crow.reshape(2, KC, 128).transpose(2, 1, 0)).reshape(128, KC * 2), 'wmod': wmod, 'modb': modb,
             'xT_in': _c(np.concatenate([ctx[b].T, x[b].T], axis=1)), 'final_g': fg}
        m.update(a_in(0, '_s0'))
        m.update(b_in(0, '_s1'))
        m.update(a_in(1, '_s1'))
        m.update(b_in(1, '_s2'))
        assert set(m) == set(fb.inputs), (set(m) ^ set(fb.inputs))
        maps.append(m)
    res = _launch(nc, maps)
    out = np.empty((Bsz, Lq, D), np.float32)
    for b in range(Bsz):
        out[b] = res[b]['yT'].T
    return out


def _c(a):
    return np.ascontiguousarray(a)


def _layer_params(inp, l):
    f = lambda k: np.asarray(inp[k][l], dtype=np.float32)
    W = {}
    W['wgu1'] = lay_wgu(f('ffn1_w_gu'))
    W['wd1'] = lay_wd(f('ffn1_w_down'))
    W['wgu2'] = lay_wgu(f('ffn2_w_gu'))
    W['wd2'] = lay_wd(f('ffn2_w_down'))
    win = f('w_in')
    W['wfm'] = lay_cols(win, _fm_cols())
    W['wtm'] = lay_cols(win, _tm_cols())
    W['wout'] = lay_wout(f('w_out'))
    W['normg'] = _c(f('norm_g').reshape(3, KC, 128).transpose(2, 0, 1)).reshape(128, 3 * KC)
    W['qkg'] = _c(f('qk_norm_g').T)
    W['gatew'] = _c(f('gla_gate_w').transpose(1, 0, 2)).reshape(16, 512)
    W['gateb'] = _c(f('gla_gate_b').reshape(1, 512))
    W['wsT_f'] = _c(f('gmlp_w_s').transpose(2, 0, 1)).reshape(128, 512)
    W['bsb'] = _c(np.broadcast_to(f('gmlp_b_s').reshape(1, 512), (128, 512)))
    W['gmg'] = _c(f('gmlp_norm_g').reshape(4, 128).T)
    W['glag'] = _c(f('gla_norm_g').reshape(4, 128).T)
    return W


def _launch(nc, in_maps):
    res = run_bass_kernel_spmd(nc, in_maps, core_ids=list(range(len(in_maps))))
    return res.results


def run_model(inp, nseg):
    x = np.asarray(inp['x'], dtype=np.float32)
    ctx = np.asarray(inp['ctx'], dtype=np.float32)
    c = np.asarray(inp['c'], dtype=np.float32)
    c_ctx = np.asarray(inp['c_ctx'], dtype=np.float32)
    Bsz, Lq, _ = x.shape
    nlat = Lq // nseg
    ncores = Bsz * nseg
    NT = CTX + nlat
    NCH = NT // 128
    nkeys = CTX + Lq
    R = Bsz + 1
    nchunks = 144 // ncores
    ncol = nchunks * 128
    ngrp = nchunks // 6
    crow = np.concatenate([c, c_ctx[None, :]], 0)
    cs = _c(crow.reshape(R, KC, 128).transpose(2, 1, 0)).reshape(128, KC * R)
    mod_w = inp['mod_w']
    mod_b = np.asarray(inp['mod_b'], dtype=np.float32)
    maps = []
    for k in range(ncores):
        wl = []
        for l in range(2):
            blk = np.asarray(mod_w[l][:, k * ncol:(k + 1) * ncol], dtype=np.float32)
            blk = blk.reshape(KC, 128, ngrp, 768).transpose(2, 1, 0, 3)
            wl.append(_c(blk).reshape(ngrp, 128, KC * 768))
        mb = np.stack([mod_b[l, k * ncol:(k + 1) * ncol].reshape(nchunks, 128).T for l in range(2)], 1)
        maps.append({'cs': cs, 'wmod': np.concatenate(wl, 0), 'modb': _c(mb).reshape(128, 2 * nchunks)})
    res = _launch(build_mod(nchunks, R), maps)
    mod_all = np.zeros((2, R, 9 * D), np.float32)
    for k in range(ncores):
        mo = res[k]['modo'].reshape(128, 2, nchunks, R)
        for l in range(2):
            mod_all[l, :, k * ncol:(k + 1) * ncol] = mo[:, l].transpose(2, 1, 0).reshape(R, ncol)

    def modT(l, b):
        m = mod_all[l][[b, R - 1]]
        return _c(m.reshape(2, 9, KC, 128).transpose(3, 1, 2, 0)).reshape(128, 9 * KC * 2)

    consts = [make_consts(nlat, s) for s in range(nseg)]
    LW = [_layer_params(inp, l) for l in range(2)]

    def common(k):
        s = k % nseg
        return {'c_ones': consts[s]['ones'], 'c_tri': consts[s]['tri'], 'c_pm': consts[s]['pm']}

    def a_inputs(k, l):
        b, s = divmod(k, nseg)
        W = LW[l]
        return {'modT_a': modT(l, b), 'normg_a': W['normg'], 'wgu_a': W['wgu1'], 'wd_a': W['wd1'], 'wfm_a': W['wfm'], 'wtm_a': W['wtm'],
                'p_qkg': W['qkg'], 'p_gatew': W['gatew'], 'p_gateb': W['gateb'], 'p_wsT_f': W['wsT_f'], 'p_bsb': W['bsb'], 'p_gmg': W['gmg'],
                'p_cos': consts[s]['cos'], 'p_sin': consts[s]['sin']}

    def b_inputs(k, l, prev):
        b, s = divmod(k, nseg)
        W = LW[l]
        o = prev[k]
        grp = [prev[b * nseg + j] for j in range(nseg)]
        kT = np.concatenate([grp[0]['o_kT'][:, :, :CTX]] + [g_['o_kT'][:, :, CTX:] for g_ in grp], axis=2)
        v = np.concatenate([grp[0]['o_v'][:CTX]] + [g_['o_v'][CTX:] for g_ in grp], axis=0)
        gs = [g_['o_glasum'].reshape(128, 2, 2, 2, 129) for g_ in grp]
        ctxS = _c(gs[s][:, 0, :, :, 0:128]).reshape(128, 2 * 2 * 128)
        predD = np.ones((128, 2, 3, 2), np.float32)
        predB = np.zeros((128, 2, 3, 2, 128), np.float32)
        fw = list(range(0, s))
        bw = list(range(nseg - 1, s, -1))
        for d, lst in ((0, fw), (1, bw)):
            for i, j in enumerate(lst):
                slot = 3 - len(lst) + i
                predD[:, d, slot, :] = gs[j][:, 1, d, :, 128]
                predB[:, d, slot, :, :] = gs[j][:, 1, d, :, 0:128]
        return {'modT_b': modT(l, b), 'normg_b': W['normg'], 'wout_b': W['wout'], 'wgu_b': W['wgu2'], 'wd_b': W['wd2'],
                'i_qT': o['o_qT'], 'i_kT': _c(kT), 'i_v': _c(v), 'i_gq': o['o_gq'], 'i_gk': o['o_gk'], 'i_gv': o['o_gv'], 'i_S0': o['o_S0'],
                'i_grs': o['o_grs'], 'i_gm': o['o_gm'], 'i_dcum': o['o_dcum'], 'i_ctxS': ctxS,
                'i_predD': predD.reshape(128, -1), 'i_predB': predB.reshape(128, -1), 'p_glag': W['glag']}

    maps = []
    for k in range(ncores):
        b, s = divmod(k, nseg)
        xT = np.concatenate([ctx[b].T, x[b, s * nlat:(s + 1) * nlat].T], axis=1)
        m = common(k)
        m.update(a_inputs(k, 0))
        m['xT_in'] = _c(xT)
        maps.append(m)
    r1 = _launch(Builder(nlat, nkeys, do_B=False, do_A=True, final=False).build(), maps)
    maps = []
    for k in range(ncores):
        m = common(k)
        m.update(b_inputs(k, 0, r1))
        m.update(a_inputs(k, 1))
        m['xT_in'] = r1[k]['xT_out']
        maps.append(m)
    r2 = _launch(Builder(nlat, nkeys, do_B=True, do_A=True, final=False).build(), maps)
    fg = _c(np.asarray(inp['final_norm_g'], dtype=np.float32).reshape(KC, 128).T)
    maps = []
    for k in range(ncores):
        m = common(k)
        m.update(b_inputs(k, 1, r2))
        m['xT_in'] = r2[k]['xT_out']
        m['final_g'] = fg
        maps.append(m)
    r3 = _launch(Builder(nlat, nkeys, do_B=True, do_A=False, final=True, ctx_B=False).build(), maps)
    out = np.empty((Bsz, Lq, D), np.float32)
    for k in range(ncores):
        b, s = divmod(k, nseg)
        out[b, s * nlat:(s + 1) * nlat, :] = r3[k]['yT'].T
    return out


def kernel(**inputs):
    return run_fused(inputs)
```

```python
import numpy as np
import ml_dtypes
from contextlib import ExitStack
import concourse.bass as bass
import concourse.mybir as mybir
from concourse.bass_utils import run_bass_kernel_spmd

F32 = mybir.dt.float32
BF16 = mybir.dt.bfloat16
AF = mybir.ActivationFunctionType
ALU = mybir.AluOpType
NPBF = ml_dtypes.bfloat16

D = 2048
KC = 16
DFF = 5632
NJ = 44
CTX = 256
EPS = 1e-6
INW = 4128
NFM = 23
NTM = 3
DBG_SKIP = set()


class Prog:
    def __init__(self, nc, es, nch=10):
        self.nc = nc
        self.es = es
        self.eng = {'pe': nc.tensor, 'act': nc.scalar, 'dve': nc.vector, 'pool': nc.gpsimd, 'sp': nc.sync}
        self.semh = {}
        for e in self.eng:
            self.semh['e_' + e] = es.enter_context(nc.semaphore('e_' + e))
        self.cnt = {e: 0 for e in self.eng}
        self.known = {e: {} for e in self.eng}
        self.lastw = {}
        self.readers = {}
        self.chans = {}
        for q in ('sp', 'pool'):
            lst = []
            for i in range(nch):
                key = 'd_%s%d' % (q, i)
                self.semh[key] = es.enter_context(nc.semaphore(key))
                lst.append([key, 0])
            self.chans[q] = [lst, 0]
        self.uid = 0

    def _wait(self, eng, sk, val):
        if self.known[eng].get(sk, 0) >= val:
            return
        self.known[eng][sk] = val
        self.eng[eng].wait_ge(self.semh[sk], val)

    def _deps(self, eng, reads, writes, extra=()):
        need = {}

        def add(ev):
            sk, val, e = ev
            if e == eng and eng == 'pe':
                return
            if need.get(sk, 0) < val:
                need[sk] = val
        for r in reads:
            w = self.lastw.get(r)
            if w is not None:
                add(w)
        for w_ in writes:
            w = self.lastw.get(w_)
            if w is not None:
                add(w)
            rd = self.readers.get(w_)
            if rd:
                for sk, (val, e) in rd.items():
                    add((sk, val, e))
        for ev in extra:
            if ev is not None:
                add(ev)
        for sk, val in need.items():
            self._wait(eng, sk, val)

    def _record(self, ev, reads, writes):
        sk, val, e = ev
        for r in reads:
            self.readers.setdefault(r, {})[sk] = (val, e)
        for w in writes:
            self.lastw[w] = ev
            self.readers[w] = {}

    def op(self, eng, fns, reads=(), writes=()):
        if callable(fns):
            fns = [fns]
        self._deps(eng, reads, writes)
        ins = None
        for f in fns:
            ins = f()
        self.cnt[eng] += 1
        ins.then_inc(self.semh['e_' + eng], 1)
        ev = ('e_' + eng, self.cnt[eng], eng)
        self._record(ev, reads, writes)
        return ev

    def dma(self, q, out, in_, reads=(), writes=(), **kw):
        lst, idx = self.chans[q]
        ch = lst[idx % len(lst)]
        self.chans[q][1] = idx + 1
        prev = (ch[0], ch[1], 'dma') if ch[1] else None
        self._deps(q, reads, writes, extra=(prev,))
        ch[1] += 16
        self.eng[q].dma_start(out=out, in_=in_, **kw).then_inc(self.semh[ch[0]], 16)
        ev = (ch[0], ch[1], 'dma')
        self._record(ev, reads, writes)
        return ev

    def barrier(self, full=False):
        targets = [('e_' + e, self.cnt[e]) for e in self.eng if self.cnt[e] > 0]
        for q in self.chans:
            if q == 'pool' and not full:
                continue
            for ch in self.chans[q][0]:
                if ch[1] > 0:
                    targets.append((ch[0], ch[1]))
        for e in self.eng:
            for sk, val in targets:
                if sk == 'e_pe' and e == 'pe':
                    continue
                self._wait(e, sk, val)
        if full:
            self.lastw.clear()
        else:
            self.lastw = {k: ev for k, ev in self.lastw.items() if ev[0].startswith('d_pool')}
        self.readers.clear()

    def key(self, name):
        self.uid += 1
        return (name, self.uid)


class Scope:
    def __init__(self, P):
        self.P = P
        self.es = ExitStack()

    def __enter__(self):
        self.es.__enter__()
        return self

    def tile(self, name, shape, dt):
        self.P.uid += 1
        return self.es.enter_context(self.P.nc.sbuf_tensor('%s_%d' % (name, self.P.uid), list(shape), dt))

    def __exit__(self, *a):
        self.P.barrier()
        return self.es.__exit__(*a)


def _fm_cols():
    groups = []
    for h in range(8):
        groups.append(np.arange(h * 128, (h + 1) * 128))
    for g in range(2):
        groups.append(1024 + np.arange(g * 128, (g + 1) * 128))
    for g in range(2):
        groups.append(1536 + np.arange(g * 128, (g + 1) * 128))
    for g in range(2):
        groups.append(1792 + np.arange(g * 128, (g + 1) * 128))
    for g in range(4):
        groups.append(2560 + np.arange(g * 128, (g + 1) * 128))
    glr = np.full(128, -1)
    glr[:32] = 3072 + np.arange(32)
    groups.append(glr)
    for g in range(4):
        groups.append(3104 + np.arange(g * 128, (g + 1) * 128))
    return groups


def _tm_cols():
    return [np.concatenate([1280 + np.arange(256), 1792 + np.arange(256)]),
            2048 + np.arange(512),
            3616 + np.arange(512)]


def lay_wgu(w):
    g = w[:, :DFF].reshape(KC, 128, NJ, 128)
    u = w[:, DFF:].reshape(KC, 128, NJ, 128)
    s = np.stack([g, u], axis=3)
    return np.ascontiguousarray(s.transpose(2, 1, 0, 3, 4)).reshape(NJ, 128, KC * 256)


def lay_wd(w):
    s = w.reshape(NJ, 128, KC, 128)
    return np.ascontiguousarray(s.transpose(2, 1, 0, 3)).reshape(KC, 128, NJ * 128)


def lay_cols(w, groups):
    outs = []
    for cols in groups:
        sel = np.where(cols >= 0, cols, 0)
        blk = w[:, sel]
        if (cols < 0).any():
            blk = blk.copy()
            blk[:, cols < 0] = 0.0
        blk = blk.reshape(KC, 128, len(cols)).transpose(1, 0, 2)
        outs.append(np.ascontiguousarray(blk).reshape(128, KC * len(cols)))
    return np.stack(outs, 0)


def lay_wout(w):
    s = w.reshape(KC, 128, KC, 128)
    return np.ascontiguousarray(s.transpose(2, 1, 0, 3)).reshape(KC, 128, KC * 128)


def make_consts(nlat, seg):
    c = {}
    c['ones'] = np.ones((128, 128), np.float32)
    j = np.arange(128)[:, None]
    i = np.arange(128)[None, :]
    tri = np.stack([(j <= i), (j >= i), (j > i), (j < i)], 0).astype(np.float32)
    c['tri'] = np.ascontiguousarray(tri.transpose(1, 0, 2)).reshape(128, 4 * 128)
    pm = np.zeros((128, 128), np.float32)
    for m in range(128):
        blk = (m // 32) % 2
        k = m + 32 if blk == 0 else m - 32
        pm[k, m] = 1.0
    c['pm'] = pm
    t = seg * nlat + np.arange(nlat)
    row = (t // 64).astype(np.float32)
    col = (t % 64).astype(np.float32)
    nf = 32
    inv = (10000.0 ** (-np.arange(nf, dtype=np.float32) / nf)).astype(np.float32)
    ar = row[None, :] * inv[:, None]
    ac = col[None, :] * inv[:, None]
    cos = np.concatenate([np.cos(ar), np.cos(ar), np.cos(ac), np.cos(ac)], 0).astype(np.float32)
    sin = np.concatenate([-np.sin(ar), np.sin(ar), -np.sin(ac), np.sin(ac)], 0).astype(np.float32)
    c['cos'] = np.ascontiguousarray(cos)
    c['sin'] = np.ascontiguousarray(sin)
    return c


class Builder:
    def __init__(self, nlat, nkeys, do_B, do_A, final, ctx_B=True):
        self.nlat = nlat
        self.NT = CTX + nlat
        self.NCH = self.NT // 128
        self.nkeys = nkeys
        self.do_B, self.do_A, self.final, self.ctx_B = do_B, do_A, final, ctx_B
        self.nc = bass.Bass("TRN2", target_bir_lowering=False)
        self.inputs = []
        self.outputs = []
        self.in_specs = {}
        self.fused = False
        self.sfx = ''

    def din(self, name, shape, dt=F32):
        self.inputs.append(name)
        self.in_specs[name] = (list(shape), dt)
        return self.nc.dram_tensor(name, list(shape), dt, kind="ExternalInput").ap()

    def dout(self, name, shape, dt=F32):
        if self.fused and name != 'yT':
            return self.dscr(name, shape, dt)
        self.outputs.append(name)
        return self.nc.dram_tensor(name, list(shape), dt, kind="ExternalOutput").ap()

    def dscr(self, name, shape, dt=BF16):
        return self.nc.dram_tensor(name, list(shape), dt, kind="Internal").ap()

    def supertiles(self, include_ctx=True):
        st = []
        if include_ctx:
            st.append((0, CTX, True))
        for i in range(self.nlat // 512):
            st.append((CTX + i * 512, 512, False))
        return st

    def prologue(self):
        nc, P, G = self.nc, self.P, self.G
        ones_f = G.tile('ones_f', [128, 128], F32)
        self.ones_bf = G.tile('ones_bf', [128, 128], BF16)
        self.tri = G.tile('tri', [128, 4, 128], F32)
        self.pm = G.tile('pm', [128, 128], F32)
        c_ones = self.din('c_ones', [128, 128])
        c_tri = self.din('c_tri', [128, 512])
        c_pm = self.din('c_pm', [128, 128])
        P.dma('sp', ones_f[:], c_ones[:, :], writes=['ones_f'])
        P.dma('sp', self.tri[:].rearrange('p a b -> p (a b)'), c_tri[:, :], writes=['tri'])
        P.dma('sp', self.pm[:], c_pm[:, :], writes=['pm'])
        P.op('dve', lambda: nc.vector.tensor_copy(out=self.ones_bf[:], in_=ones_f[:]), reads=['ones_f'], writes=['ones_bf'])
        self.ones_f = ones_f
        self.eps_t = G.tile('eps_t', [128, 1], F32)
        self.one_t = G.tile('one_t', [128, 1], F32)
        P.op('dve', lambda: nc.vector.memset(self.eps_t[:], EPS), writes=['eps_t'])
        P.op('dve', lambda: nc.vector.memset(self.one_t[:], 1.0), writes=['one_t'])
        self.x_t = G.tile('x_t', [128, KC, 512], F32)

    def stage_setup(self):
        P, G = self.P, self.G
        if self.do_B:
            self.setup_layer('b')
            self.setup_B()
        if self.do_A:
            self.setup_layer('a')
            self.setup_A()
        if self.final:
            self.fin_g = G.tile('fin_g', [128, KC], F32)
            fg = self.din('final_g', [128, KC])
            P.dma('sp', self.fin_g[:], fg[:, :], writes=['fin_g'])
            self.y_out = self.dout('yT', [D, self.nlat])

    def run_stage(self):
        P = self.P
        if self.do_B:
            self.attention()
            self.gla_output()
        for (t0, T, is_ctx) in self.supertiles():
            if is_ctx and self.do_B and not self.do_A and not self.ctx_B:
                continue
            r = 1 if is_ctx else 0
            self.load_x(t0, T)
            if self.do_B and (self.ctx_B or not is_ctx):
                self.outproj(t0, T, r)
                self.ffn('b', 2, T, r)
            if self.do_A:
                self.ffn('a', 0, T, r)
                self.inproj(t0, T, r, is_ctx)
            if self.final:
                if not is_ctx:
                    self.final_norm(t0, T)
            else:
                self.store_x(t0, T)
            P.barrier()
        if self.do_A:
            self.gla_scan()

    def build(self):
        nc = self.nc
        NT = self.NT
        with ExitStack() as es:
            P = Prog(nc, es)
            self.P = P
            self.ps = [es.enter_context(nc.psum_tensor('ps%d' % i, [128, 512], F32)) for i in range(8)]
            G = Scope(P)
            G.__enter__()
            self.G = G
            self.prologue()
            self.xT_in = self.din('xT_in', [D, NT])
            if not self.final:
                self.xT_out = self.dout('xT_out', [D, NT])
            self.stage_setup()
            P.barrier()
            self.run_stage()
            P.barrier(full=True)
            G.__exit__(None, None, None)
        return nc

    def cast_weights(self, name, shape3, chunk_reads=None):
        src = self.din(name + self.sfx, shape3)
        dst = self.dscr(name + self.sfx + '_bf', shape3)
        for g in range(shape3[0]):
            self.P.dma('pool', dst[g], src[g], writes=[(name + self.sfx, g)], max_dma_last_dim=8192)
        return dst

    def setup_layer(self, tag):
        nc, P, G = self.nc, self.P, self.G
        L = {}
        ng = G.tile('ng' + tag, [128, 3, KC], F32)
        d_ng = self.din('normg_' + tag + self.sfx, [128, 3 * KC])
        if self.fused:
            lyr = self.lb if tag == 'b' else self.la
            modT = self.modall[:].rearrange('p l (a b) c -> p l a b c', a=9)[:, lyr]
        else:
            modT = G.tile('modT' + tag, [128, 9, KC, 2], F32)
            d_mod = self.din('modT_' + tag, [128, 9 * KC * 2])
            P.dma('sp', modT[:].rearrange('p a b c -> p (a b c)'), d_mod[:, :], writes=['modT' + tag])
        P.dma('sp', ng[:].rearrange('p a b -> p (a b)'), d_ng[:, :], writes=['ng' + tag])
        L['gs'] = G.tile('gs' + tag, [128, 3, 2, KC], F32)
        L['sh'] = G.tile('sh' + tag, [128, 3, 2, KC], F32)
        L['hg'] = G.tile('hg' + tag, [128, 3, 2, KC], F32)
        for i3 in range(3):
            for r in range(2):
                P.op('dve', lambda i3=i3, r=r: nc.vector.scalar_tensor_tensor(
                    out=L['gs'][:, i3, r, :], in0=modT[:, 3 * i3 + 1, :, r], scalar=1.0, in1=ng[:, i3, :],
                    op0=ALU.add, op1=ALU.mult), reads=['modT' + tag, 'ng' + tag], writes=[('gs', tag, i3, r)])
                P.op('dve', lambda i3=i3, r=r: nc.vector.tensor_copy(out=L['sh'][:, i3, r, :], in_=modT[:, 3 * i3, :, r]),
                     reads=['modT' + tag], writes=[('sh', tag, i3, r)])
                P.op('dve', lambda i3=i3, r=r: nc.vector.tensor_scalar(
                    out=L['hg'][:, i3, r, :], in0=modT[:, 3 * i3 + 2, :, r], scalar1=(1.0 if i3 == 1 else 0.5), scalar2=None,
                    op0=ALU.mult), reads=['modT' + tag], writes=[('hg', tag, i3, r)])
        setattr(self, 'L' + tag, L)

    def setup_A(self):
        nc, P, G = self.nc, self.P, self.G
        NT = self.NT
        A = {'sfx': self.sfx}
        A['wgu'] = self.cast_weights('wgu_a', [NJ, 128, KC * 256])
        A['wd'] = self.cast_weights('wd_a', [KC, 128, NJ * 128])
        A['wfm'] = self.cast_weights('wfm_a', [NFM, 128, KC * 128])
        A['wtm'] = self.cast_weights('wtm_a', [NTM, 128, KC * 512])
        A['qkg'] = G.tile('qkg', [128, 2], F32)
        A['gatew'] = G.tile('gatew', [16, 2, 256], F32)
        A['gateb'] = G.tile('gateb', [1, 2, 256], F32)
        A['wsT_f'] = G.tile('wsT_f', [128, 4, 128], F32)
        A['wsT'] = G.tile('wsT', [128, 4, 128], BF16)
        A['bsb'] = G.tile('bsb', [128, 4, 128], F32)
        A['gmg'] = G.tile('gmg', [128, 4], F32)
        A['cosd'] = self.din('p_cos' + self.sfx, [128, self.nlat])
        A['sind'] = self.din('p_sin' + self.sfx, [128, self.nlat])
        for nm, shp in [('qkg', [128, 2]), ('gatew', [16, 512]), ('gateb', [1, 512]), ('wsT_f', [128, 512]),
                        ('bsb', [128, 512]), ('gmg', [128, 4])]:
            d = self.din('p_' + nm + self.sfx, shp)
            t = A[nm]
            ap = t[:] if len(t.shape) == 2 else t[:].rearrange('p a b -> p (a b)')
            P.dma('sp', ap, d[:, :], writes=[nm])
        P.op('dve', lambda: nc.vector.tensor_copy(out=A['wsT'][:], in_=A['wsT_f'][:]), reads=['wsT_f'], writes=['wsT'])
        A['qT'] = self.dout('o_qT' + self.sfx, [8, 128, NT], BF16)
        A['kT'] = self.dout('o_kT' + self.sfx, [2, 128, NT], BF16)
        A['v'] = self.dout('o_v' + self.sfx, [NT, 256], BF16)
        A['gq'] = self.dout('o_gq' + self.sfx, [2, 2, 128, NT], BF16)
        A['gk'] = self.dout('o_gk' + self.sfx, [2, 2, 128, NT], BF16)
        A['gv'] = self.dout('o_gv' + self.sfx, [NT, 512], BF16)
        A['kh'] = self.dscr('s_kh' + self.sfx, [2, NT, 256], BF16)
        A['S0'] = self.dout('o_S0' + self.sfx, [2, self.NCH, 128, 256], F32)
        A['grs'] = self.dout('o_grs' + self.sfx, [4, 128, NT], BF16)
        A['gm'] = self.dout('o_gm' + self.sfx, [4, 128, NT], BF16)
        A['glasum'] = self.dout('o_glasum' + self.sfx, [128, 2 * 2 * 2 * 129], F32)
        A['dcum'] = self.dout('o_dcum' + self.sfx, [128, 2 * self.NCH * 2], F32)
        A['eb'] = G.tile('eb', [128, 2, self.NCH, 2], F32)
        self.A = A

    def setup_B(self):
        nc, P, G = self.nc, self.P, self.G
        NT = self.NT
        B = {'sfx': self.sfx}
        B['wout'] = self.cast_weights('wout_b', [KC, 128, KC * 128])
        B['wgu'] = self.cast_weights('wgu_b', [NJ, 128, KC * 256])
        B['wd'] = self.cast_weights('wd_b', [KC, 128, NJ * 128])
        if self.fused:
            pa = self.prevA
            for k_ in ('qT', 'kT', 'v', 'gq', 'gk', 'gv', 'S0', 'grs', 'gm', 'dcum', 'glasum'):
                B[k_] = pa[k_]
            B['glag'] = G.tile('glag', [128, 4], F32)
            d = self.din('p_glag' + self.sfx, [128, 4])
            P.dma('sp', B['glag'][:], d[:, :], writes=['glag'])
            B['mix'] = self.dscr('s_mix' + self.sfx, [12, 128, NT], BF16)
            self.B = B
            return
        B['qT'] = self.din('i_qT', [8, 128, NT], BF16)
        B['kT'] = self.din('i_kT', [2, 128, self.nkeys], BF16)
        B['v'] = self.din('i_v', [self.nkeys, 256], BF16)
        B['gq'] = self.din('i_gq', [2, 2, 128, NT], BF16)
        B['gk'] = self.din('i_gk', [2, 2, 128, NT], BF16)
        B['gv'] = self.din('i_gv', [NT, 512], BF16)
        B['S0'] = self.din('i_S0', [2, self.NCH, 128, 256], F32)
        B['grs'] = self.din('i_grs', [4, 128, NT], BF16)
        B['gm'] = self.din('i_gm', [4, 128, NT], BF16)
        B['dcum'] = self.din('i_dcum', [128, 2 * self.NCH * 2], F32)
        B['ctxS'] = self.din('i_ctxS', [128, 2 * 2 * 128], F32)
        B['predD'] = self.din('i_predD', [128, 2 * 3 * 2], F32)
        B['predB'] = self.din('i_predB', [128, 2 * 3 * 2 * 128], F32)
        B['glag'] = G.tile('glag', [128, 4], F32)
        d = self.din('p_glag', [128, 4])
        P.dma('sp', B['glag'][:], d[:, :], writes=['glag'])
        B['mix'] = self.dscr('s_mix', [12, 128, NT], BF16)
        self.B = B

    def load_x(self, t0, T):
        P = self.P
        src = self.xT_in.rearrange('(kc p) t -> p kc t', p=128)
        for h in range(2):
            P.dma('sp', self.x_t[:, h * 8:(h + 1) * 8, :T], src[:, h * 8:(h + 1) * 8, t0:t0 + T],
                  writes=[('x', m) for m in range(h * 8, h * 8 + 8)])

    def store_x(self, t0, T):
        P = self.P
        dst = self.xT_out.rearrange('(kc p) t -> p kc t', p=128)
        for h in range(2):
            P.dma('sp', dst[:, h * 8:(h + 1) * 8, t0:t0 + T], self.x_t[:, h * 8:(h + 1) * 8, :T],
                  reads=[('x', m) for m in range(h * 8, h * 8 + 8)])

    def modnorm(self, S, hT, T, gs, sh):
        nc, P, ps = self.nc, self.P, self.ps
        x_t = self.x_t
        sq = [S.tile('sq', [128, 512], BF16) for _ in range(2)]
        tmp = [S.tile('mtmp', [128, 512], F32) for _ in range(2)]
        rs = S.tile('rs', [128, 512], F32)
        rstd = S.tile('rstd', [128, 512], F32)
        for kc in range(KC):
            b = kc % 2
            P.op('act', lambda kc=kc, b=b: nc.scalar.activation(out=sq[b][:, :T], in_=x_t[:, kc, :T], func=AF.Square),
                 reads=[('x', kc)], writes=[('sq', b)])
            P.op('pe', lambda kc=kc, b=b: nc.tensor.matmul(ps[0][:, :T], lhsT=self.ones_bf[:], rhs=sq[b][:, :T],
                                                           start=(kc == 0), stop=(kc == KC - 1)),
                 reads=[('sq', b)], writes=[('ps', 0)])
        P.op('act', lambda: nc.scalar.activation(out=rs[:, :T], in_=ps[0][:, :T], func=AF.Sqrt, scale=1.0 / D, bias=self.eps_t[:, 0:1]),
             reads=[('ps', 0)], writes=['rs'])
        P.op('dve', lambda: nc.vector.reciprocal(out=rstd[:, :T], in_=rs[:, :T]), reads=['rs'], writes=['rstd'])
        for kc in range(KC):
            b = kc % 2
            P.op('dve', lambda kc=kc, b=b: nc.vector.scalar_tensor_tensor(
                out=tmp[b][:, :T], in0=x_t[:, kc, :T], scalar=gs[:, kc:kc + 1], in1=rstd[:, :T], op0=ALU.mult, op1=ALU.mult),
                reads=[('x', kc), 'rstd'], writes=[('mtmp', b)])
            P.op('act', lambda kc=kc, b=b: nc.scalar.activation(out=hT[:, kc, :T], in_=tmp[b][:, :T], func=AF.Identity,
                                                                bias=sh[:, kc:kc + 1], scale=1.0),
                 reads=[('mtmp', b)], writes=[('hT', kc)])

    def ffn(self, tag, i3, T, r):
        nc, P, ps = self.nc, self.P, self.ps
        L = getattr(self, 'L' + tag)
        W = self.A if tag == 'a' else self.B
        wgu, wd = W['wgu'], W['wd']
        wn_gu = 'wgu_' + tag + W['sfx']
        wn_d = 'wd_' + tag + W['sfx']
        x_t = self.x_t
        with Scope(P) as S:
            hT = S.tile('hT', [128, KC, 512], BF16)
            act = S.tile('act', [128, NJ, 512], BF16)
            gu = [S.tile('gu', [128, KC, 256], BF16) for _ in range(3)]
            wdt = [S.tile('wdt', [128, NJ, 128], BF16) for _ in range(2)]
            sgt = [S.tile('sgt', [128, 512], F32) for _ in range(2)]

            def load_gu(j):
                P.dma('sp', gu[j % 3][:].rearrange('p a b -> p (a b)'), wgu[j], reads=[(wn_gu, j)], writes=[('gu', j % 3)])

            def load_wd(m):
                P.dma('sp', wdt[m % 2][:].rearrange('p a b -> p (a b)'), wd[m], reads=[(wn_d, m)], writes=[('wdt', m % 2)])
            load_gu(0)
            load_gu(1)
            self.modnorm(S, hT, T, L['gs'][:, i3, r, :], L['sh'][:, i3, r, :])
            hkeys = [('hT', kc) for kc in range(KC)]
            for j in range(NJ):
                if j + 2 < NJ:
                    load_gu(j + 2)
                if j == NJ - 4:
                    load_wd(0)
                if j == NJ - 2:
                    load_wd(1)
                w = gu[j % 3]
                pg = ps[1 + 2 * (j % 2)]
                pu = ps[2 + 2 * (j % 2)]
                P.op('pe', [lambda kc=kc, w=w, pg=pg: nc.tensor.matmul(pg[:, :T], lhsT=w[:, kc, 0:128], rhs=hT[:, kc, :T],
                                                                       start=(kc == 0), stop=(kc == KC - 1)) for kc in range(KC)],
                     reads=hkeys + [('gu', j % 3)], writes=[('ps', 1 + 2 * (j % 2))])
                P.op('pe', [lambda kc=kc, w=w, pu=pu: nc.tensor.matmul(pu[:, :T], lhsT=w[:, kc, 128:256], rhs=hT[:, kc, :T],
                                                                       start=(kc == 0), stop=(kc == KC - 1)) for kc in range(KC)],
                     reads=hkeys + [('gu', j % 3)], writes=[('ps', 2 + 2 * (j % 2))])
                P.op('act', lambda j=j, pg=pg: nc.scalar.activation(out=sgt[j % 2][:, :T], in_=pg[:, :T], func=AF.Silu),
                     reads=[('ps', 1 + 2 * (j % 2))], writes=[('sgt', j % 2)])
                P.op('dve', lambda j=j, pu=pu: nc.vector.tensor_tensor(out=act[:, j, :T], in0=sgt[j % 2][:, :T], in1=pu[:, :T], op=ALU.mult),
                     reads=[('sgt', j % 2), ('ps', 2 + 2 * (j % 2))], writes=[('act', j)])
            akeys = [('act', j) for j in range(NJ)]
            hg = L['hg']
            for m in range(KC):
                w = wdt[m % 2]
                po = ps[5 + (m % 2)]
                P.op('pe', [lambda j=j, w=w, po=po: nc.tensor.matmul(po[:, :T], lhsT=w[:, j, :], rhs=act[:, j, :T],
                                                                     start=(j == 0), stop=(j == NJ - 1)) for j in range(NJ)],
                     reads=akeys + [('wdt', m % 2)], writes=[('ps', 5 + (m % 2))])
                P.op('dve', lambda m=m, po=po: nc.vector.scalar_tensor_tensor(
                    out=x_t[:, m, :T], in0=po[:, :T], scalar=hg[:, i3, r, m:m + 1], in1=x_t[:, m, :T], op0=ALU.mult, op1=ALU.add),
                    reads=[('ps', 5 + (m % 2)), ('x', m)], writes=[('x', m)])
                if m + 2 < KC:
                    load_wd(m + 2)

    def outproj(self, t0, T, r):
        nc, P, ps = self.nc, self.P, self.ps
        B, L = self.B, self.Lb
        x_t = self.x_t
        with Scope(P) as S:
            mix = S.tile('mixt', [128, KC, 512], BF16)
            wt = [S.tile('wo', [128, KC, 128], BF16) for _ in range(3)]
            P.dma('sp', mix[:, 0:12, :T], B['mix'][:, :, t0:t0 + T].rearrange('g p t -> p g t'), reads=['mix_scr'], writes=['mixt'])
            P.dma('sp', mix[:, 12:16, :T], B['gm'][:, :, t0:t0 + T].rearrange('g p t -> p g t'), writes=['mixt2'])

            def load_w(m):
                P.dma('sp', wt[m % 3][:].rearrange('p a b -> p (a b)'), B['wout'][m], reads=[('wout_b' + B['sfx'], m)], writes=[('wo', m % 3)])
            load_w(0)
            load_w(1)
            for m in range(KC):
                if m + 2 < KC:
                    load_w(m + 2)
                w = wt[m % 3]
                po = ps[5 + (m % 2)]
                P.op('pe', [lambda kc=kc, w=w, po=po: nc.tensor.matmul(po[:, :T], lhsT=w[:, kc, :], rhs=mix[:, kc, :T],
                                                                       start=(kc == 0), stop=(kc == KC - 1)) for kc in range(KC)],
                     reads=['mixt', 'mixt2', ('wo', m % 3)], writes=[('ps', 5 + (m % 2))])
                P.op('dve', lambda m=m, po=po: nc.vector.scalar_tensor_tensor(
                    out=x_t[:, m, :T], in0=po[:, :T], scalar=L['hg'][:, 1, r, m:m + 1], in1=x_t[:, m, :T], op0=ALU.mult, op1=ALU.add),
                    reads=[('ps', 5 + (m % 2)), ('x', m)], writes=[('x', m)])

    def final_norm(self, t0, T):
        nc, P, ps = self.nc, self.P, self.ps
        x_t = self.x_t
        with Scope(P) as S:
            sq = [S.tile('sq', [128, 512], BF16) for _ in range(2)]
            rs = S.tile('rs', [128, 512], F32)
            rstd = S.tile('rstd', [128, 512], F32)
            for kc in range(KC):
                b = kc % 2
                P.op('act', lambda kc=kc, b=b: nc.scalar.activation(out=sq[b][:, :T], in_=x_t[:, kc, :T], func=AF.Square),
                     reads=[('x', kc)], writes=[('sq', b)])
                P.op('pe', lambda kc=kc, b=b: nc.tensor.matmul(ps[0][:, :T], lhsT=self.ones_bf[:], rhs=sq[b][:, :T],
                                                               start=(kc == 0), stop=(kc == KC - 1)),
                     reads=[('sq', b)], writes=[('ps', 0)])
            P.op('act', lambda: nc.scalar.activation(out=rs[:, :T], in_=ps[0][:, :T], func=AF.Sqrt, scale=1.0 / D, bias=self.eps_t[:, 0:1]),
                 reads=[('ps', 0)], writes=['rs'])
            P.op('dve', lambda: nc.vector.reciprocal(out=rstd[:, :T], in_=rs[:, :T]), reads=['rs'], writes=['rstd'])
            for kc in range(KC):
                P.op('dve', lambda kc=kc: nc.vector.scalar_tensor_tensor(
                    out=x_t[:, kc, :T], in0=x_t[:, kc, :T], scalar=self.fin_g[:, kc:kc + 1], in1=rstd[:, :T], op0=ALU.mult, op1=ALU.mult),
                    reads=[('x', kc), 'rstd', 'fin_g'], writes=[('x', kc)])
            dst = self.y_out.rearrange('(kc p) t -> p kc t', p=128)
            tl = t0 - CTX
            for h in range(2):
                P.dma('sp', dst[:, h * 8:(h + 1) * 8, tl:tl + T], x_t[:, h * 8:(h + 1) * 8, :T],
                      reads=[('x', m) for m in range(h * 8, h * 8 + 8)])

    def inproj(self, t0, T, r, is_ctx):
        nc, P, ps = self.nc, self.P, self.ps
        A, L = self.A, self.La
        nchk = T // 128
        tl = t0 - CTX
        SC = 128.0 ** -0.5
        with Scope(P) as S:
            hT = S.tile('hT', [128, KC, 512], BF16)
            wf = [S.tile('wf', [128, KC, 128], BF16) for _ in range(3)]
            wt = [S.tile('wt', [128, KC, 512], BF16) for _ in range(2)]
            zsq = S.tile('zsq', [128, 512], BF16)
            hrs = S.tile('hrs', [128, 512], F32)
            hrstd = S.tile('hrstd', [128, 512], F32)
            qn = S.tile('qn', [128, 512], F32)
            t1 = S.tile('t1', [128, 512], F32)
            t2 = S.tile('t2', [128, 512], F32)
            qf = [S.tile('qf', [128, 512], BF16) for _ in range(2)]
            gqT = [S.tile('gqT', [128, 512], F32) for _ in range(2)]
            gkT = [S.tile('gkT', [128, 512], F32) for _ in range(2)]
            lr = [S.tile('lr', [16, 512], F32) for _ in range(2)]
            uT = [S.tile('uT', [128, 512], F32) for _ in range(4)]
            grt = [S.tile('grt', [128, 512], BF16) for _ in range(2)]
            if not is_ctx:
                cs = S.tile('cs', [128, 512], F32)
                sn = S.tile('sn', [128, 512], F32)
                P.dma('sp', cs[:, :T], A['cosd'][:, tl:tl + T], writes=['cs'])
                P.dma('sp', sn[:, :T], A['sind'][:, tl:tl + T], writes=['sn'])

            def load_f(g):
                P.dma('sp', wf[g % 3][:].rearrange('p a b -> p (a b)'), A['wfm'][g], reads=[('wfm_a' + A['sfx'], g)], writes=[('wf', g % 3)])

            def load_t(g):
                P.dma('sp', wt[g % 2][:].rearrange('p a b -> p (a b)'), A['wtm'][g], reads=[('wtm_a' + A['sfx'], g)], writes=[('wt', g % 2)])
            load_f(0)
            load_f(1)
            load_t(0)
            load_t(1)
            self.modnorm(S, hT, T, L['gs'][:, 1, r, :], L['sh'][:, 1, r, :])
            hkeys = [('hT', kc) for kc in range(KC)]
            for g in range(NFM):
                if g + 2 < NFM:
                    load_f(g + 2)
                w = wf[g % 3]
                pb = 1 + (g % 2)
                pz = ps[pb]
                if g == 18 and 'glr' in DBG_SKIP:
                    continue
                if g == 18:
                    for d in range(2):
                        pzd = ps[1 + d]
                        P.op('pe', [lambda kc=kc, w=w, pzd=pzd, d=d: nc.tensor.matmul(
                            pzd[0:16, :T], lhsT=w[:, kc, d * 16:(d + 1) * 16], rhs=hT[:, kc, :T],
                            start=(kc == 0), stop=(kc == KC - 1)) for kc in range(KC)],
                            reads=hkeys + [('wf', g % 3)], writes=[('ps', 1 + d)])
                        P.op('act', lambda d=d, pzd=pzd: nc.scalar.copy(out=lr[d][:, :T], in_=pzd[0:16, :T]),
                             reads=[('ps', 1 + d)], writes=[('lr', d)])
                    continue
                P.op('pe', [lambda kc=kc, w=w, pz=pz: nc.tensor.matmul(pz[:, :T], lhsT=w[:, kc, :], rhs=hT[:, kc, :T],
                                                                       start=(kc == 0), stop=(kc == KC - 1)) for kc in range(KC)],
                     reads=hkeys + [('wf', g % 3)], writes=[('ps', pb)])
                if 'fmpost' in DBG_SKIP:
                    continue
                if g < 10:
                    which = 0 if g < 8 else 1
                    P.op('act', lambda pz=pz: nc.scalar.activation(out=zsq[:, :T], in_=pz[:, :T], func=AF.Square),
                         reads=[('ps', pb)], writes=['zsq'])
                    P.op('pe', lambda: nc.tensor.matmul(ps[3][:, :T], lhsT=self.ones_bf[:], rhs=zsq[:, :T], start=True, stop=True),
                         reads=['zsq'], writes=[('ps', 3)])
                    P.op('act', lambda: nc.scalar.activation(out=hrs[:, :T], in_=ps[3][:, :T], func=AF.Sqrt, scale=1.0 / 128,
                                                             bias=self.eps_t[:, 0:1]), reads=[('ps', 3)], writes=['hrs'])
                    P.op('dve', lambda: nc.vector.reciprocal(out=hrstd[:, :T], in_=hrs[:, :T]), reads=['hrs'], writes=['hrstd'])
                    P.op('dve', lambda pz=pz, which=which: nc.vector.scalar_tensor_tensor(
                        out=qn[:, :T], in0=pz[:, :T], scalar=A['qkg'][:, which:which + 1], in1=hrstd[:, :T], op0=ALU.mult, op1=ALU.mult),
                        reads=[('ps', pb), 'hrstd'], writes=['qn'])
                    q_o = qf[g % 2]
                    if is_ctx:
                        P.op('pool', lambda q_o=q_o: nc.gpsimd.tensor_copy(out=q_o[:, :T], in_=qn[:, :T]), reads=['qn'], writes=[('qf', g % 2)])
                    else:
                        P.op('pe', lambda: nc.tensor.matmul(ps[4][:, :T], lhsT=self.pm[:], rhs=qn[:, :T], start=True, stop=True),
                             reads=['qn'], writes=[('ps', 4)])
                        P.op('pool', lambda: nc.gpsimd.tensor_tensor(out=t1[:, :T], in0=qn[:, :T], in1=cs[:, :T], op=ALU.mult),
                             reads=['qn', 'cs'], writes=['t1'])
                        P.op('dve', lambda: nc.vector.tensor_tensor(out=t2[:, :T], in0=ps[4][:, :T], in1=sn[:, :T], op=ALU.mult),
                             reads=[('ps', 4), 'sn'], writes=['t2'])
                        P.op('pool', lambda q_o=q_o: nc.gpsimd.tensor_tensor(out=q_o[:, :T], in0=t1[:, :T], in1=t2[:, :T], op=ALU.add),
                             reads=['t1', 't2'], writes=[('qf', g % 2)])
                    dst = A['qT'][g, :, t0:t0 + T] if g < 8 else A['kT'][g - 8, :, t0:t0 + T]
                    P.dma('sp', dst, q_o[:, :T], reads=[('qf', g % 2)])
                elif g < 12:
                    P.op('act', lambda pz=pz, g=g: nc.scalar.copy(out=gqT[g - 10][:, :T], in_=pz[:, :T]), reads=[('ps', pb)], writes=[('gqT', g - 10)])
                elif g < 14:
                    P.op('act', lambda pz=pz, g=g: nc.scalar.copy(out=gkT[g - 12][:, :T], in_=pz[:, :T]), reads=[('ps', pb)], writes=[('gkT', g - 12)])
                elif g < 18:
                    go = grt[g % 2]
                    P.op('act', lambda pz=pz, go=go: nc.scalar.activation(out=go[:, :T], in_=pz[:, :T], func=AF.Silu),
                         reads=[('ps', pb)], writes=[('grt', g % 2)])
                    P.dma('sp', A['grs'][g - 14, :, t0:t0 + T], go[:, :T], reads=[('grt', g % 2)])
                else:
                    P.op('act', lambda pz=pz, g=g: nc.scalar.copy(out=uT[g - 19][:, :T], in_=pz[:, :T]), reads=[('ps', pb)], writes=[('uT', g - 19)])
            vt = [S.tile('vt', [128, 256], BF16) for _ in range(2)]
            gktm = [S.tile('gktm', [128, 256], F32) for _ in range(4)]
            gvt = [S.tile('gvt', [128, 512], BF16) for _ in range(2)]
            vn = S.tile('vn', [128, 512], BF16)
            junk = S.tile('junk', [128, 512], BF16)
            ssq = S.tile('ssq', [128, 1], F32)
            srs = S.tile('srs', [128, 1], F32)
            srstd = S.tile('srstd', [128, 1], F32)
            tz = S.tile('tz', [128, 128], F32)
            gmo = S.tile('gmo', [128, 4, 512], BF16)
            ee = S.tile('ee', [128, 256], F32)
            sp = S.tile('sp', [128, 256], F32)
            E1 = S.tile('E1', [128, 2, 128], F32)
            E2 = S.tile('E2', [128, 2, 128], F32)
            E3 = S.tile('E3', [128, 256], F32)
            gqo = [S.tile('gqo', [128, 2, 512], BF16) for _ in range(2)]
            gko = [S.tile('gko', [128, 2, 512], BF16) for _ in range(2)]
            kht = [S.tile('kht', [128, 256], BF16) for _ in range(2)]
            for gt in range(NTM):
                if 'tm' in DBG_SKIP:
                    break
                if gt == 2 and 'gmlp' in DBG_SKIP:
                    break
                w = wt[gt % 2]
                for c in range(nchk):
                    cc = slice(c * 128, (c + 1) * 128)
                    pb = 5 + (c % 2)
                    pz = ps[pb]
                    P.op('pe', [lambda kc=kc, w=w, pz=pz, cc=cc: nc.tensor.matmul(pz[:, :], lhsT=hT[:, kc, cc], rhs=w[:, kc, :],
                                                                                  start=(kc == 0), stop=(kc == KC - 1)) for kc in range(KC)],
                         reads=hkeys + [('wt', gt % 2)], writes=[('ps', pb)])
                    tok = slice(t0 + c * 128, t0 + (c + 1) * 128)
                    if 'tmpost' in DBG_SKIP or ('tm%dpost' % gt) in DBG_SKIP:
                        continue
                    if gt == 0:
                        v_o = vt[c % 2]
                        if 'tm0a' not in DBG_SKIP:
                            P.op('act', lambda pz=pz, v_o=v_o: nc.scalar.copy(out=v_o[:], in_=pz[:, 0:256]), reads=[('ps', pb)], writes=[('vt', c % 2)])
                            P.dma('sp', A['v'][tok, :], v_o[:], reads=[('vt', c % 2)])
                        if 'tm0b' not in DBG_SKIP:
                            P.op('act', lambda pz=pz, c=c: nc.scalar.copy(out=gktm[c][:], in_=pz[:, 256:512]), reads=[('ps', pb)], writes=[('gktm', c)])
                    elif gt == 1:
                        g_o = gvt[c % 2]
                        P.op('act', lambda pz=pz, g_o=g_o: nc.scalar.copy(out=g_o[:], in_=pz[:, :]), reads=[('ps', pb)], writes=[('gvt', c % 2)])
                        P.dma('sp', A['gv'][tok, :], g_o[:], reads=[('gvt', c % 2)])
                    else:
                        P.op('act', lambda pz=pz: nc.scalar.activation(out=junk[:], in_=pz[:, :], func=AF.Square, accum_out=ssq[:]),
                             reads=[('ps', pb)], writes=['junk', 'ssq'])
                        P.op('act', lambda: nc.scalar.activation(out=srs[:], in_=ssq[:], func=AF.Sqrt, scale=1.0 / 512, bias=self.eps_t[:, 0:1]),
                             reads=['ssq'], writes=['srs'])
                        P.op('dve', lambda: nc.vector.reciprocal(out=srstd[:], in_=srs[:]), reads=['srs'], writes=['srstd'])
                        P.op('dve', lambda pz=pz: nc.vector.tensor_scalar(out=vn[:], in0=pz[:, :], scalar1=srstd[:, 0:1], scalar2=None, op0=ALU.mult),
                             reads=[('ps', pb), 'srstd'], writes=['vn'])
                        P.op('pe', [lambda gi=gi: nc.tensor.matmul(ps[7][:, gi * 128:(gi + 1) * 128], lhsT=vn[:, gi * 128:(gi + 1) * 128],
                                                                   rhs=A['wsT'][:, gi, :], start=True, stop=True) for gi in range(4)],
                             reads=['vn'], writes=[('ps', 7)])
                        for gi in range(4):
                            P.op('dve', lambda gi=gi: nc.vector.scalar_tensor_tensor(
                                out=tz[:], in0=ps[7][:, gi * 128:(gi + 1) * 128], scalar=A['gmg'][:, gi:gi + 1], in1=A['bsb'][:, gi, :],
                                op0=ALU.mult, op1=ALU.add), reads=[('ps', 7)], writes=['tz'])
                            P.op('pool', lambda gi=gi, cc=cc: nc.gpsimd.tensor_tensor(out=gmo[:, gi, cc], in0=tz[:], in1=uT[gi][:, cc], op=ALU.mult),
                                 reads=['tz', ('uT', gi)], writes=['gmo'])
                if gt + 2 < NTM:
                    load_t(gt + 2)
            if 'gmlp' not in DBG_SKIP and 'tm' not in DBG_SKIP:
                P.dma('sp', A['gm'][:, :, t0:t0 + T].rearrange('g p t -> p g t'), gmo[:, :, :T], reads=['gmo'])
            for c in range(nchk):
                if 'gla' in DBG_SKIP:
                    break
                cc = slice(c * 128, (c + 1) * 128)
                cg = (t0 // 128) + c
                tok = slice(t0 + c * 128, t0 + (c + 1) * 128)
                for d in range(2):
                    P.op('pe', [lambda d=d, cc=cc: nc.tensor.matmul(ps[1][:, 0:256], lhsT=lr[d][0:16, cc], rhs=A['gatew'][0:16, d, :], start=True, stop=False),
                                lambda d=d: nc.tensor.matmul(ps[1][:, 0:256], lhsT=self.ones_f[0:1, 0:128], rhs=A['gateb'][0:1, d, :], start=False, stop=True)],
                         reads=[('lr', d)], writes=[('ps', 1)])
                    P.op('act', lambda: nc.scalar.activation(out=ee[:], in_=ps[1][:, 0:256], func=AF.Exp, scale=-1.0), reads=[('ps', 1)], writes=['ee'])
                    P.op('act', lambda: nc.scalar.activation(out=sp[:], in_=ee[:], func=AF.Ln, bias=self.one_t[:, 0:1], scale=1.0), reads=['ee'], writes=['sp'])
                    P.op('pe', [lambda half=half, d=d: nc.tensor.matmul(ps[2][:, half * 128:(half + 1) * 128], lhsT=sp[:, half * 128:(half + 1) * 128],
                                                                        rhs=self.tri[:, d, :], start=True, stop=True) for half in range(2)],
                         reads=['sp'], writes=[('ps', 2)])
                    P.op('pe', lambda d=d: nc.tensor.matmul(ps[3][:, 0:256], lhsT=self.tri[:, 2 + d, :], rhs=sp[:], start=True, stop=True),
                         reads=['sp'], writes=[('ps', 3)])
                    P.op('act', lambda: nc.scalar.activation(out=E1[:].rearrange('p a b -> p (a b)'), in_=ps[2][:, 0:256], func=AF.Exp, scale=-1.0 / 16),
                         reads=[('ps', 2)], writes=['E1'])
                    P.op('act', lambda: nc.scalar.activation(out=E2[:].rearrange('p a b -> p (a b)'), in_=ps[2][:, 0:256], func=AF.Exp, scale=1.0 / 16),
                         reads=[('ps', 2)], writes=['E2'])
                    P.op('act', lambda: nc.scalar.activation(out=E3[:], in_=ps[3][:, 0:256], func=AF.Exp, scale=-1.0 / 16),
                         reads=[('ps', 3)], writes=['E3'])
                    for half in range(2):
                        P.op('dve', lambda half=half, d=d, cc=cc: nc.vector.scalar_tensor_tensor(
                            out=gqo[d][:, half, cc], in0=gqT[half][:, cc], scalar=0.125, in1=E1[:, half, :], op0=ALU.mult, op1=ALU.mult),
                            reads=[('gqT', half), 'E1'], writes=[('gqo', d)])
                        P.op('pool', lambda half=half, d=d, cc=cc: nc.gpsimd.tensor_tensor(
                            out=gko[d][:, half, cc], in0=gkT[half][:, cc], in1=E2[:, half, :], op=ALU.mult),
                            reads=[('gkT', half), 'E2'], writes=[('gko', d)])
                    k_o = kht[d]
                    P.op('dve', lambda c=c, k_o=k_o: nc.vector.tensor_tensor(out=k_o[:], in0=gktm[c][:], in1=E3[:], op=ALU.mult),
                         reads=[('gktm', c), 'E3'], writes=[('kht', d)])
                    P.dma('sp', A['kh'][d, tok, :], k_o[:], reads=[('kht', d)])
                    col = 127 if d == 0 else 0
                    P.op('dve', lambda d=d, cg=cg, col=col: nc.vector.tensor_copy(out=A['eb'][:, d, cg, :], in_=E1[:, :, col]),
                         reads=['E1'], writes=['eb'])
            for d in range(2):
                if 'gla' in DBG_SKIP:
                    break
                P.dma('sp', A['gq'][d, :, :, t0:t0 + T].rearrange('h p t -> p h t'), gqo[d][:, :, :T], reads=[('gqo', d)])
                P.dma('sp', A['gk'][d, :, :, t0:t0 + T].rearrange('h p t -> p h t'), gko[d][:, :, :T], reads=[('gko', d)])

    def gla_scan(self):
        nc, P, ps = self.nc, self.P, self.ps
        A = self.A
        NCH = self.NCH
        with Scope(P) as S:
            Sst = [S.tile('Sst', [128, 2, 128], F32) for _ in range(2)]
            Drun = [S.tile('Drun', [128, 2], F32) for _ in range(2)]
            dcum = S.tile('dcum', [128, 2, NCH, 2], F32)
            gsum = S.tile('gsum', [128, 2, 2, 2, 129], F32)
            stage = [S.tile('stage', [128, 256], F32) for _ in range(4)]
            kht = [S.tile('skh', [128, 256], BF16) for _ in range(4)]
            gvt = [S.tile('sgv', [128, 512], BF16) for _ in range(4)]
            it = 0
            for kind in range(2):
                lo, hi = (0, 2) if kind == 0 else (2, NCH)
                orders = [list(range(lo, hi)), list(range(hi - 1, lo - 1, -1))]
                for d in range(2):
                    P.op('dve', lambda d=d: nc.vector.memset(Sst[d][:], 0.0), writes=[('Sst', d)])
                    P.op('dve', lambda d=d: nc.vector.memset(Drun[d][:], 1.0), writes=[('Drun', d)])
                for step in range(hi - lo):
                    for d in range(2):
                        c = orders[d][step]
                        tok = slice(c * 128, (c + 1) * 128)
                        b = it % 4
                        it += 1
                        P.dma('sp', kht[b][:], A['kh'][d, tok, :], writes=[('skh', b)])
                        P.dma('sp', gvt[b][:], A['gv'][tok, :], writes=[('sgv', b)])
                        P.op('act', lambda d=d, b=b: nc.scalar.copy(out=stage[b][:], in_=Sst[d][:].rearrange('p a b -> p (a b)')),
                             reads=[('Sst', d)], writes=[('stage', b)])
                        P.dma('sp', A['S0'][d, c], stage[b][:], reads=[('stage', b)])
                        P.op('dve', lambda d=d, c=c: nc.vector.tensor_copy(out=dcum[:, d, c, :], in_=Drun[d][:]), reads=[('Drun', d)], writes=['dcum'])
                        pb = 1 + d
                        P.op('pe', [lambda h=h, b=b, pb=pb: nc.tensor.matmul(
                            ps[pb][(h % 2) * 64:(h % 2) * 64 + 64, (h // 2) * 128:(h // 2) * 128 + 128],
                            lhsT=kht[b][:, h * 64:(h + 1) * 64], rhs=gvt[b][:, h * 128:(h + 1) * 128], start=True, stop=True) for h in range(4)],
                            reads=[('skh', b), ('sgv', b)], writes=[('ps', pb)])
                        for half in range(2):
                            P.op('dve', lambda d=d, c=c, half=half, pb=pb: nc.vector.scalar_tensor_tensor(
                                out=Sst[d][:, half, :], in0=Sst[d][:, half, :], scalar=A['eb'][:, d, c, half:half + 1],
                                in1=ps[pb][:, half * 128:(half + 1) * 128], op0=ALU.mult, op1=ALU.add),
                                reads=[('ps', pb), ('Sst', d)], writes=[('Sst', d)])
                        P.op('dve', lambda d=d, c=c: nc.vector.tensor_tensor(out=Drun[d][:], in0=Drun[d][:], in1=A['eb'][:, d, c, :], op=ALU.mult),
                             reads=[('Drun', d)], writes=[('Drun', d)])
                for d in range(2):
                    P.op('dve', lambda d=d, kind=kind: nc.vector.tensor_copy(out=gsum[:, kind, d, :, 0:128], in_=Sst[d][:]),
                         reads=[('Sst', d)], writes=['gsum'])
                    P.op('dve', lambda d=d, kind=kind: nc.vector.tensor_copy(out=gsum[:, kind, d, :, 128], in_=Drun[d][:]),
                         reads=[('Drun', d)], writes=['gsum'])
            P.dma('sp', A['glasum'][:, :], gsum[:].rearrange('p a b c e -> p (a b c e)'), reads=['gsum'])
            P.dma('sp', A['dcum'][:, :], dcum[:].rearrange('p a b c -> p (a b c)'), reads=['dcum'])

    def attention(self):
        nc, P, ps = self.nc, self.P, self.ps
        B = self.B
        nkeys = self.nkeys
        nkc = nkeys // 128
        SC = 128.0 ** -0.5
        with Scope(P) as S:
            KT = S.tile('KT', [128, 2, nkeys], BF16)
            V = S.tile('V', [128, nkc, 256], BF16)
            step = 2048
            for g in range(2):
                for a in range(0, nkeys, step):
                    b_ = min(nkeys, a + step)
                    P.dma('sp', KT[:, g, a:b_], B['kT'][g, :, a:b_], writes=['KT'])
            vsrc = B['v'].rearrange('(kc p) c -> p kc c', p=128)
            for a in range(0, nkc, 16):
                b_ = min(nkc, a + 16)
                P.dma('sp', V[:, a:b_, :], vsrc[:, a:b_, :], writes=['V'])
            q4 = [S.tile('q4', [128, 4, 128], BF16) for _ in range(2)]
            pT = [S.tile('pT', [128, 512], BF16) for _ in range(3)]
            rl = S.tile('rl', [128, 512], F32)
            oT = [S.tile('oT', [128, 512], BF16) for _ in range(2)]
            acc = [S.tile('acc', [128, 512], F32) for _ in range(2)]
            it = 0
            for qt in range(self.NCH):
                is_ctx = qt < 2
                if is_ctx and not self.ctx_B:
                    continue
                kcs = [0, 1] if is_ctx else list(range(nkc))
                tok = slice(qt * 128, (qt + 1) * 128)
                for g in range(2):
                    b = it % 2
                    it += 1
                    q = q4[b]
                    P.dma('sp', q[:], B['qT'][g * 4:(g + 1) * 4, :, tok].rearrange('h p t -> p h t'), writes=[('q4', b)])
                    qr = q[:].rearrange('p h t -> p (h t)')
                    po, pl = ps[2 + b], ps[4 + b]
                    n = len(kcs)

                    def emit_s(ki):
                        kc = kcs[ki]
                        P.op('pe', lambda kc=kc, ki=ki: nc.tensor.matmul(ps[ki % 2][:, :], lhsT=KT[:, g, kc * 128:(kc + 1) * 128], rhs=qr, start=True, stop=True),
                             reads=['KT', ('q4', b)], writes=[('ps', ki % 2)])
                    emit_s(0)
                    for ki in range(n):
                        kc = kcs[ki]
                        p_ = pT[ki % 3]
                        P.op('act', lambda ki=ki, p_=p_: nc.scalar.activation(out=p_[:], in_=ps[ki % 2][:, :], func=AF.Exp, scale=SC),
                             reads=[('ps', ki % 2)], writes=[('pT', ki % 3)])
                        if ki + 1 < n:
                            emit_s(ki + 1)
                        P.op('pe', lambda kc=kc, ki=ki, p_=p_: nc.tensor.matmul(po[:, :], lhsT=V[:, kc, g * 128:(g + 1) * 128], rhs=p_[:], start=(ki == 0), stop=(ki == n - 1)),
                             reads=['V', ('pT', ki % 3)], writes=[('ps', 2 + b)])
                        if ki == 0:
                            P.op('dve', lambda p_=p_: nc.vector.tensor_copy(out=acc[b][:], in_=p_[:]), reads=[('pT', ki % 3)], writes=[('acc', b)])
                        else:
                            P.op('dve', lambda p_=p_: nc.vector.tensor_tensor(out=acc[b][:], in0=acc[b][:], in1=p_[:], op=ALU.add),
                                 reads=[('pT', ki % 3), ('acc', b)], writes=[('acc', b)])
                    P.op('pe', lambda pl=pl: nc.tensor.matmul(pl[:, :], lhsT=self.ones_f[:], rhs=acc[b][:], start=True, stop=True),
                         reads=[('acc', b)], writes=[('ps', 4 + b)])
                    P.op('dve', lambda pl=pl: nc.vector.reciprocal(out=rl[:], in_=pl[:, :]), reads=[('ps', 4 + b)], writes=['rl'])
                    o_ = oT[b]
                    P.op('dve', lambda po=po, o_=o_: nc.vector.tensor_tensor(out=o_[:], in0=po[:, :], in1=rl[:], op=ALU.mult),
                         reads=[('ps', 2 + b), 'rl'], writes=[('oT', b)])
                    P.dma('sp', B['mix'][g * 4:(g + 1) * 4, :, tok].rearrange('h p t -> p h t'), o_[:].rearrange('p (h t) -> p h t', h=4),
                          reads=[('oT', b)], writes=['mix_scr'])

    def gla_output(self):
        nc, P, ps = self.nc, self.P, self.ps
        B = self.B
        NCH = self.NCH
        with Scope(P) as S:
            Sst = S.tile('Sst', [128, 2, 2, 128], F32)
            pD = S.tile('pD', [128, 2, 3, 2], F32)
            pB = S.tile('pB', [128, 2, 3, 2, 128], F32)
            dcum = S.tile('dcum', [128, 2, NCH, 2], F32)
            if self.fused:
                gsv = B['glasum'].rearrange('p (k d h e) -> p k d h e', k=2, d=2, h=2)
                for d in range(2):
                    P.dma('sp', Sst[:, d, :, :], gsv[:, 0, d, :, 0:128], writes=['Sst'])
            else:
                P.dma('sp', Sst[:].rearrange('p a b c -> p (a b c)'), B['ctxS'][:, :], writes=['Sst'])
                P.dma('sp', pD[:].rearrange('p a b c -> p (a b c)'), B['predD'][:, :], writes=['pD'])
                P.dma('sp', pB[:].rearrange('p a b c e -> p (a b c e)'), B['predB'][:, :], writes=['pB'])
            P.dma('sp', dcum[:].rearrange('p a b c -> p (a b c)'), B['dcum'][:, :], writes=['dcum'])
            for d in range(2):
                if self.fused:
                    break
                for slot in range(3):
                    for half in range(2):
                        P.op('dve', lambda d=d, slot=slot, half=half: nc.vector.scalar_tensor_tensor(
                            out=Sst[:, d, half, :], in0=Sst[:, d, half, :], scalar=pD[:, d, slot, half:half + 1], in1=pB[:, d, slot, half, :],
                            op0=ALU.mult, op1=ALU.add), reads=['Sst', 'pD', 'pB'], writes=['Sst'])
            NB = 2
            gq = [S.tile('gq', [128, 2, 2, 128], BF16) for _ in range(NB)]
            gk = [S.tile('gk', [128, 2, 2, 128], BF16) for _ in range(NB)]
            gv = [S.tile('gv', [128, 512], BF16) for _ in range(NB)]
            S0 = [S.tile('S0', [128, 2, 256], F32) for _ in range(NB)]
            grs = [S.tile('grs', [128, 4, 128], BF16) for _ in range(NB)]
            Sc = [S.tile('Sc', [128, 2, 2, 128], BF16) for _ in range(NB)]
            Am = [S.tile('Am', [128, 128], BF16) for _ in range(4)]
            osq = S.tile('osq', [128, 128], BF16)
            ors = S.tile('ors', [128, 128], F32)
            orstd = S.tile('orstd', [128, 128], F32)
            on = S.tile('on', [128, 128], F32)
            og = [S.tile('og', [128, 4, 128], BF16) for _ in range(NB)]
            ai = 0
            for ci, c in enumerate(range(NCH)):
                is_ctx = c < 2
                if is_ctx and not self.ctx_B:
                    continue
                b = ci % NB
                tok = slice(c * 128, (c + 1) * 128)
                P.dma('sp', gq[b][:], B['gq'][:, :, :, tok].rearrange('d h p t -> p d h t'), writes=[('gq', b)])
                P.dma('sp', gk[b][:], B['gk'][:, :, :, tok].rearrange('d h p t -> p d h t'), writes=[('gk', b)])
                P.dma('sp', gv[b][:], B['gv'][tok, :], writes=[('gv', b)])
                P.dma('sp', S0[b][:], B['S0'][:, c].rearrange('d p f -> p d f'), writes=[('S0', b)])
                P.dma('sp', grs[b][:], B['grs'][:, :, tok].rearrange('g p t -> p g t'), writes=[('grs', b)])
                for d in range(2):
                    for half in range(2):
                        if is_ctx:
                            P.op('dve', lambda d=d, half=half, b=b: nc.vector.tensor_copy(out=Sc[b][:, d, half, :], in_=S0[b][:, d, half * 128:(half + 1) * 128]),
                                 reads=[('S0', b)], writes=[('Sc', b)])
                        else:
                            P.op('dve', lambda d=d, half=half, b=b, c=c: nc.vector.scalar_tensor_tensor(
                                out=Sc[b][:, d, half, :], in0=Sst[:, d, half, :], scalar=dcum[:, d, c, half:half + 1],
                                in1=S0[b][:, d, half * 128:(half + 1) * 128], op0=ALU.mult, op1=ALU.add),
                                reads=[('S0', b), 'Sst', 'dcum'], writes=[('Sc', b)])
                for h in range(4):
                    half = h // 2
                    hs = slice((h % 2) * 64, (h % 2) * 64 + 64)
                    ams = []
                    for d in range(2):
                        pa = ps[d]
                        P.op('pe', lambda d=d, b=b, half=half, hs=hs, pa=pa: nc.tensor.matmul(
                            pa[:, 0:128], lhsT=gk[b][hs, d, half, :], rhs=gq[b][hs, d, half, :], start=True, stop=True),
                            reads=[('gq', b), ('gk', b)], writes=[('ps', d)])
                        a_ = Am[ai % 4]
                        ams.append((a_, ai % 4))
                        P.op('dve', lambda d=d, pa=pa, a_=a_: nc.vector.tensor_tensor(out=a_[:], in0=pa[:, 0:128], in1=self.tri[:, d, :], op=ALU.mult),
                             reads=[('ps', d)], writes=[('Am', ai % 4)])
                        ai += 1
                    po = ps[2 + (h % 2)]
                    fl = []
                    for d in range(2):
                        a_, _ = ams[d]
                        fl.append(lambda d=d, a_=a_, b=b, h=h, po=po: nc.tensor.matmul(po[:, 0:128], lhsT=gv[b][:, h * 128:(h + 1) * 128], rhs=a_[:],
                                                                                       start=(d == 0), stop=False))
                        fl.append(lambda d=d, b=b, half=half, hs=hs, po=po: nc.tensor.matmul(po[:, 0:128], lhsT=Sc[b][hs, d, half, :], rhs=gq[b][hs, d, half, :],
                                                                                             start=False, stop=(d == 1)))
                    P.op('pe', fl, reads=[('gv', b), ('Sc', b), ('gq', b)] + [('Am', k) for _, k in ams], writes=[('ps', 2 + (h % 2))])
                    P.op('act', lambda po=po: nc.scalar.activation(out=osq[:], in_=po[:, 0:128], func=AF.Square), reads=[('ps', 2 + (h % 2))], writes=['osq'])
                    P.op('pe', lambda: nc.tensor.matmul(ps[4][:, 0:128], lhsT=self.ones_bf[:], rhs=osq[:], start=True, stop=True), reads=['osq'], writes=[('ps', 4)])
                    P.op('act', lambda: nc.scalar.activation(out=ors[:], in_=ps[4][:, 0:128], func=AF.Sqrt, scale=1.0 / 128, bias=self.eps_t[:, 0:1]),
                         reads=[('ps', 4)], writes=['ors'])
                    P.op('dve', lambda: nc.vector.reciprocal(out=orstd[:], in_=ors[:]), reads=['ors'], writes=['orstd'])
                    P.op('dve', lambda po=po, h=h: nc.vector.scalar_tensor_tensor(out=on[:], in0=po[:, 0:128], scalar=B['glag'][:, h:h + 1], in1=orstd[:],
                                                                                  op0=ALU.mult, op1=ALU.mult), reads=[('ps', 2 + (h % 2)), 'orstd'], writes=['on'])
                    P.op('pool', lambda h=h, b=b: nc.gpsimd.tensor_tensor(out=og[b][:, h, :], in0=on[:], in1=grs[b][:, h, :], op=ALU.mult),
                         reads=['on', ('grs', b)], writes=[('og', b)])
                P.dma('sp', B['mix'][8:12, :, tok].rearrange('g p t -> p g t'), og[b][:], reads=[('og', b)], writes=['mix_scr'])


def build_mod(nchunks, R):
    nc = bass.Bass("TRN2", target_bir_lowering=False)
    ngrp = nchunks // 6
    cs = nc.dram_tensor('cs', [128, KC * R], F32, kind="ExternalInput").ap()
    wmod = nc.dram_tensor('wmod', [2 * ngrp, 128, KC * 768], F32, kind="ExternalInput").ap()
    modb = nc.dram_tensor('modb', [128, 2 * nchunks], F32, kind="ExternalInput").ap()
    modo = nc.dram_tensor('modo', [128, 2 * nchunks * R], F32, kind="ExternalOutput").ap()
    with ExitStack() as es:
        P = Prog(nc, es)
        ps = [es.enter_context(nc.psum_tensor('ps%d' % i, [128, 512], F32)) for i in range(2)]
        with Scope(P) as S:
            cst = S.tile('cst', [128, KC, R], F32)
            scs = S.tile('scs', [128, KC, R], F32)
            mb = S.tile('mb', [128, 2, nchunks], F32)
            mo = S.tile('mo', [128, 2, nchunks, R], F32)
            wt = [S.tile('wt', [128, KC, 768], F32) for _ in range(2)]
            P.dma('sp', cst[:].rearrange('p a b -> p (a b)'), cs[:, :], writes=['cst'])
            P.dma('sp', mb[:].rearrange('p a b -> p (a b)'), modb[:, :], writes=['mb'])
            P.op('act', lambda: nc.scalar.activation(out=scs[:].rearrange('p a b -> p (a b)'), in_=cst[:].rearrange('p a b -> p (a b)'), func=AF.Silu),
                 reads=['cst'], writes=['scs'])
            k = 0
            for l in range(2):
                for grp in range(ngrp):
                    w = wt[k % 2]
                    P.dma('sp', w[:].rearrange('p a b -> p (a b)'), wmod[l * ngrp + grp], writes=[('wt', k % 2)])
                    for n in range(6):
                        nn = grp * 6 + n
                        pb = nn % 2
                        P.op('pe', [lambda kc=kc, w=w, n=n, pb=pb: nc.tensor.matmul(ps[pb][:, 0:R], lhsT=w[:, kc, n * 128:(n + 1) * 128], rhs=scs[:, kc, :],
                                                                                    start=(kc == 0), stop=(kc == KC - 1)) for kc in range(KC)],
                             reads=['scs', ('wt', k % 2)], writes=[('ps', pb)])
                        P.op('dve', lambda l=l, nn=nn, pb=pb: nc.vector.tensor_scalar(out=mo[:, l, nn, :], in0=ps[pb][:, 0:R], scalar1=mb[:, l, nn:nn + 1],
                                                                                      scalar2=None, op0=ALU.add), reads=[('ps', pb), 'mb'], writes=['mo'])
                    k += 1
            P.dma('sp', modo[:, :], mo[:].rearrange('p a b c -> p (a b c)'), reads=['mo'])
    return nc


class FusedBuilder(Builder):
    def __init__(self, nlat):
        super().__init__(nlat, CTX + nlat, do_B=False, do_A=False, final=False)
        self.fused = True

    def mod_phase(self):
        nc, P, ps = self.nc, self.P, self.ps
        R, nchunks, ngrp = 2, 144, 24
        cs = self.din('cs', [128, KC * R])
        wmod = self.din('wmod', [2 * ngrp, 128, KC * 768])
        modb = self.din('modb', [128, 2 * nchunks])
        self.modall = self.G.tile('modall', [128, 2, nchunks, R], F32)
        mo = self.modall
        with Scope(P) as S:
            cst = S.tile('cst', [128, KC, R], F32)
            scs = S.tile('scs', [128, KC, R], F32)
            mb = S.tile('mb', [128, 2, nchunks], F32)
            wt = [S.tile('wt', [128, KC, 768], F32) for _ in range(2)]
            P.dma('sp', cst[:].rearrange('p a b -> p (a b)'), cs[:, :], writes=['cst'])
            P.dma('sp', mb[:].rearrange('p a b -> p (a b)'), modb[:, :], writes=['mb'])
            P.op('act', lambda: nc.scalar.activation(out=scs[:].rearrange('p a b -> p (a b)'), in_=cst[:].rearrange('p a b -> p (a b)'), func=AF.Silu),
                 reads=['cst'], writes=['scs'])
            k = 0
            for l in range(2):
                for grp in range(ngrp):
                    w = wt[k % 2]
                    P.dma('sp', w[:].rearrange('p a b -> p (a b)'), wmod[l * ngrp + grp], writes=[('wt', k % 2)])
                    for n in range(6):
                        nn = grp * 6 + n
                        pb = nn % 2
                        P.op('pe', [lambda kc=kc, w=w, n=n, pb=pb: nc.tensor.matmul(ps[pb][:, 0:R], lhsT=w[:, kc, n * 128:(n + 1) * 128], rhs=scs[:, kc, :],
                                                                                    start=(kc == 0), stop=(kc == KC - 1)) for kc in range(KC)],
                             reads=['scs', ('wt', k % 2)], writes=[('ps', pb)])
                        P.op('dve', lambda l=l, nn=nn, pb=pb: nc.vector.tensor_scalar(out=mo[:, l, nn, :], in0=ps[pb][:, 0:R], scalar1=mb[:, l, nn:nn + 1],
                                                                                      scalar2=None, op0=ALU.add), reads=[('ps', pb), 'mb'], writes=['mo'])
                    k += 1

    def build(self):
        nc = self.nc
        NT = self.NT
        with ExitStack() as es:
            P = Prog(nc, es)
            self.P = P
            self.ps = [es.enter_context(nc.psum_tensor('ps%d' % i, [128, 512], F32)) for i in range(8)]
            G0 = Scope(P)
            G0.__enter__()
            self.G = G0
            self.prologue()
            x_in = self.din('xT_in', [D, NT])
            xs = [self.dscr('xs0', [D, NT], F32), self.dscr('xs1', [D, NT], F32)]
            self.mod_phase()
            stages = [dict(do_B=False, do_A=True, final=False, ctx_B=True, lb=None, la=0, src=x_in, dst=xs[0]),
                      dict(do_B=True, do_A=True, final=False, ctx_B=True, lb=0, la=1, src=xs[0], dst=xs[1]),
                      dict(do_B=True, do_A=False, final=True, ctx_B=False, lb=1, la=None, src=xs[1], dst=None)]
            self.prevA = None
            for si, st in enumerate(stages):
                self.do_B, self.do_A, self.final, self.ctx_B = st['do_B'], st['do_A'], st['final'], st['ctx_B']
                self.lb, self.la = st['lb'], st['la']
                self.sfx = '_s%d' % si
                self.xT_in, self.xT_out = st['src'], st['dst']
                G = Scope(P)
                G.__enter__()
                self.G = G
                self.stage_setup()
                P.barrier()
                self.run_stage()
                if self.do_A:
                    self.prevA = self.A
                G.__exit__(None, None, None)
            P.barrier(full=True)
            G0.__exit__(None, None, None)
        return nc


def run_fused(inp):
    x = np.asarray(inp['x'], dtype=np.float32)
    ctx = np.asarray(inp['ctx'], dtype=np.float32)
    c = np.asarray(inp['c'], dtype=np.float32)
    c_ctx = np.asarray(inp['c_ctx'], dtype=np.float32)
    Bsz, Lq, _ = x.shape
    fb = FusedBuilder(Lq)
    nc = fb.build()
    consts = make_consts(Lq, 0)
    LW = [_layer_params(inp, l) for l in range(2)]
    mod_b = np.asarray(inp['mod_b'], dtype=np.float32)
    wl = []
    for l in range(2):
        blk = np.asarray(inp['mod_w'][l], dtype=np.float32).reshape(KC, 128, 24, 768).transpose(2, 1, 0, 3)
        wl.append(_c(blk).reshape(24, 128, KC * 768))
    wmod = np.concatenate(wl, 0)
    modb = _c(np.stack([mod_b[l].reshape(144, 128).T for l in range(2)], 1)).reshape(128, 2 * 144)
    fg = _c(np.asarray(inp['final_norm_g'], dtype=np.float32).reshape(KC, 128).T)

    def a_in(l, sfx):
        W = LW[l]
        return {'normg_a' + sfx: W['normg'], 'wgu_a' + sfx: W['wgu1'], 'wd_a' + sfx: W['wd1'], 'wfm_a' + sfx: W['wfm'], 'wtm_a' + sfx: W['wtm'],
                'p_qkg' + sfx: W['qkg'], 'p_gatew' + sfx: W['gatew'], 'p_gateb' + sfx: W['gateb'], 'p_wsT_f' + sfx: W['wsT_f'],
                'p_bsb' + sfx: W['bsb'], 'p_gmg' + sfx: W['gmg'], 'p_cos' + sfx: consts['cos'], 'p_sin' + sfx: consts['sin']}

    def b_in(l, sfx):
        W = LW[l]
        return {'normg_b' + sfx: W['normg'], 'wout_b' + sfx: W['wout'], 'wgu_b' + sfx: W['wgu2'], 'wd_b' + sfx: W['wd2'], 'p_glag' + sfx: W['glag']}
    maps = []
    for b in range(Bsz):
        crow = np.stack([c[b], c_ctx], 0)
        m = {'c_ones': consts['ones'], 'c_tri': consts['tri'], 'c_pm': consts['pm'],
             'cs': _c(crow.reshape(2, KC, 128).transpose(2, 1, 0)).reshape(128, KC * 2), 'wmod': wmod, 'modb': modb,
             'xT_in': _c(np.concatenate([ctx[b].T, x[b].T], axis=1)), 'final_g': fg}
        m.update(a_in(0, '_s0'))
        m.update(b_in(0, '_s1'))
        m.update(a_in(1, '_s1'))
        m.update(b_in(1, '_s2'))
        assert set(m) == set(fb.inputs), (set(m) ^ set(fb.inputs))
        maps.append(m)
    res = _launch(nc, maps)
    out = np.empty((Bsz, Lq, D), np.float32)
    for b in range(Bsz):
        out[b] = res[b]['yT'].T
    return out


def _c(a):
    return np.ascontiguousarray(a)


def _layer_params(inp, l):
    f = lambda k: np.asarray(inp[k][l], dtype=np.float32)
    W = {}
    W['wgu1'] = lay_wgu(f('ffn1_w_gu'))
    W['wd1'] = lay_wd(f('ffn1_w_down'))
    W['wgu2'] = lay_wgu(f('ffn2_w_gu'))
    W['wd2'] = lay_wd(f('ffn2_w_down'))
    win = f('w_in')
    W['wfm'] = lay_cols(win, _fm_cols())
    W['wtm'] = lay_cols(win, _tm_cols())
    W['wout'] = lay_wout(f('w_out'))
    W['normg'] = _c(f('norm_g').reshape(3, KC, 128).transpose(2, 0, 1)).reshape(128, 3 * KC)
    W['qkg'] = _c(f('qk_norm_g').T)
    W['gatew'] = _c(f('gla_gate_w').transpose(1, 0, 2)).reshape(16, 512)
    W['gateb'] = _c(f('gla_gate_b').reshape(1, 512))
    W['wsT_f'] = _c(f('gmlp_w_s').transpose(2, 0, 1)).reshape(128, 512)
    W['bsb'] = _c(np.broadcast_to(f('gmlp_b_s').reshape(1, 512), (128, 512)))
    W['gmg'] = _c(f('gmlp_norm_g').reshape(4, 128).T)
    W['glag'] = _c(f('gla_norm_g').reshape(4, 128).T)
    return W


def _launch(nc, in_maps):
    res = run_bass_kernel_spmd(nc, in_maps, core_ids=list(range(len(in_maps))))
    return res.results


def run_model(inp, nseg):
    x = np.asarray(inp['x'], dtype=np.float32)
    ctx = np.asarray(inp['ctx'], dtype=np.float32)
    c = np.asarray(inp['c'], dtype=np.float32)
    c_ctx = np.asarray(inp['c_ctx'], dtype=np.float32)
    Bsz, Lq, _ = x.shape
    nlat = Lq // nseg
    ncores = Bsz * nseg
    NT = CTX + nlat
    NCH = NT // 128
    nkeys = CTX + Lq
    R = Bsz + 1
    nchunks = 144 // ncores
    ncol = nchunks * 128
    ngrp = nchunks // 6
    crow = np.concatenate([c, c_ctx[None, :]], 0)
    cs = _c(crow.reshape(R, KC, 128).transpose(2, 1, 0)).reshape(128, KC * R)
    mod_w = inp['mod_w']
    mod_b = np.asarray(inp['mod_b'], dtype=np.float32)
    maps = []
    for k in range(ncores):
        wl = []
        for l in range(2):
            blk = np.asarray(mod_w[l][:, k * ncol:(k + 1) * ncol], dtype=np.float32)
            blk = blk.reshape(KC, 128, ngrp, 768).transpose(2, 1, 0, 3)
            wl.append(_c(blk).reshape(ngrp, 128, KC * 768))
        mb = np.stack([mod_b[l, k * ncol:(k + 1) * ncol].reshape(nchunks, 128).T for l in range(2)], 1)
        maps.append({'cs': cs, 'wmod': np.concatenate(wl, 0), 'modb': _c(mb).reshape(128, 2 * nchunks)})
    res = _launch(build_mod(nchunks, R), maps)
    mod_all = np.zeros((2, R, 9 * D), np.float32)
    for k in range(ncores):
        mo = res[k]['modo'].reshape(128, 2, nchunks, R)
        for l in range(2):
            mod_all[l, :, k * ncol:(k + 1) * ncol] = mo[:, l].transpose(2, 1, 0).reshape(R, ncol)

    def modT(l, b):
        m = mod_all[l][[b, R - 1]]
        return _c(m.reshape(2, 9, KC, 128).transpose(3, 1, 2, 0)).reshape(128, 9 * KC * 2)

    consts = [make_consts(nlat, s) for s in range(nseg)]
    LW = [_layer_params(inp, l) for l in range(2)]

    def common(k):
        s = k % nseg
        return {'c_ones': consts[s]['ones'], 'c_tri': consts[s]['tri'], 'c_pm': consts[s]['pm']}

    def a_inputs(k, l):
        b, s = divmod(k, nseg)
        W = LW[l]
        return {'modT_a': modT(l, b), 'normg_a': W['normg'], 'wgu_a': W['wgu1'], 'wd_a': W['wd1'], 'wfm_a': W['wfm'], 'wtm_a': W['wtm'],
                'p_qkg': W['qkg'], 'p_gatew': W['gatew'], 'p_gateb': W['gateb'], 'p_wsT_f': W['wsT_f'], 'p_bsb': W['bsb'], 'p_gmg': W['gmg'],
                'p_cos': consts[s]['cos'], 'p_sin': consts[s]['sin']}

    def b_inputs(k, l, prev):
        b, s = divmod(k, nseg)
        W = LW[l]
        o = prev[k]
        grp = [prev[b * nseg + j] for j in range(nseg)]
        kT = np.concatenate([grp[0]['o_kT'][:, :, :CTX]] + [g_['o_kT'][:, :, CTX:] for g_ in grp], axis=2)
        v = np.concatenate([grp[0]['o_v'][:CTX]] + [g_['o_v'][CTX:] for g_ in grp], axis=0)
        gs = [g_['o_glasum'].reshape(128, 2, 2, 2, 129) for g_ in grp]
        ctxS = _c(gs[s][:, 0, :, :, 0:128]).reshape(128, 2 * 2 * 128)
        predD = np.ones((128, 2, 3, 2), np.float32)
        predB = np.zeros((128, 2, 3, 2, 128), np.float32)
        fw = list(range(0, s))
        bw = list(range(nseg - 1, s, -1))
        for d, lst in ((0, fw), (1, bw)):
            for i, j in enumerate(lst):
                slot = 3 - len(lst) + i
                predD[:, d, slot, :] = gs[j][:, 1, d, :, 128]
                predB[:, d, slot, :, :] = gs[j][:, 1, d, :, 0:128]
        return {'modT_b': modT(l, b), 'normg_b': W['normg'], 'wout_b': W['wout'], 'wgu_b': W['wgu2'], 'wd_b': W['wd2'],
                'i_qT': o['o_qT'], 'i_kT': _c(kT), 'i_v': _c(v), 'i_gq': o['o_gq'], 'i_gk': o['o_gk'], 'i_gv': o['o_gv'], 'i_S0': o['o_S0'],
                'i_grs': o['o_grs'], 'i_gm': o['o_gm'], 'i_dcum': o['o_dcum'], 'i_ctxS': ctxS,
                'i_predD': predD.reshape(128, -1), 'i_predB': predB.reshape(128, -1), 'p_glag': W['glag']}

    maps = []
    for k in range(ncores):
        b, s = divmod(k, nseg)
        xT = np.concatenate([ctx[b].T, x[b, s * nlat:(s + 1) * nlat].T], axis=1)
        m = common(k)
        m.update(a_inputs(k, 0))
        m['xT_in'] = _c(xT)
        maps.append(m)
    r1 = _launch(Builder(nlat, nkeys, do_B=False, do_A=True, final=False).build(), maps)
    maps = []
    for k in range(ncores):
        m = common(k)
        m.update(b_inputs(k, 0, r1))
        m.update(a_inputs(k, 1))
        m['xT_in'] = r1[k]['xT_out']
        maps.append(m)
    r2 = _launch(Builder(nlat, nkeys, do_B=True, do_A=True, final=False).build(), maps)
    fg = _c(np.asarray(inp['final_norm_g'], dtype=np.float32).reshape(KC, 128).T)
    maps = []
    for k in range(ncores):
        m = common(k)
        m.update(b_inputs(k, 1, r2))
        m['xT_in'] = r2[k]['xT_out']
        m['final_g'] = fg
        maps.append(m)
    r3 = _launch(Builder(nlat, nkeys, do_B=True, do_A=False, final=True, ctx_B=False).build(), maps)
    out = np.empty((Bsz, Lq, D), np.float32)
    for k in range(ncores):
        b, s = divmod(k, nseg)
        out[b, s * nlat:(s + 1) * nlat, :] = r3[k]['yT'].T
    return out


def kernel(**inputs):
    return run_model(inputs, nseg=4)
```
